# Optimizing a Trainium2 kernel written in Bass

```python
import math
import jax, jax.numpy as jnp
from jax import lax
import numpy as np

D_MODEL = 1024
BATCH = 8
SEQ = 2048
DEPTH = 4
DEC_BATCH = 32
DEC_SEQ = 8
PAST_LEN = 8192
PAGE_SIZE = 128

H_A = 8
HD_A = 64
PATTERNS = ((128, 1), (512, 4), (2048, 16))
G_A = len(PATTERNS)
W_MAX = max(w for w, _ in PATTERNS)
N_BUCKETS = 32
REL_MAX_DIST = W_MAX
P_B = 16
D_B = 512
G_B = D_B // P_B
N_B = 64
DT_MIN = 1e-3
DT_MAX = 1e-1
H_C = 4
DK_C = 64
DV_C = 128
FG_RANK = 16
GLA_TAU = 16.0
GLA_CHUNK = 64
D_FF = 4 * D_MODEL
EPS = 1e-6
NEG = -1e30

COLS = (G_A * H_A * HD_A, H_A * HD_A, H_A * HD_A, D_B, H_C * DK_C, H_C * DK_C, H_C * DV_C, H_C * DV_C,
        FG_RANK, D_MODEL, D_MODEL, D_MODEL)
IN_COLS = sum(COLS)

kernel_name = "hybrid_dilated_s5_gla_decoder_step"

F32 = jnp.float32


def _rmsnorm(x, g):
    xf = x.astype(F32)
    y = xf * lax.rsqrt(jnp.mean(xf * xf, axis=-1, keepdims=True) + EPS)
    return (y * g.astype(F32)).astype(x.dtype)


def _t5_bucket(dist):
    exact = N_BUCKETS // 2
    d = np.maximum(dist, 1).astype(np.float32)
    large = exact + (np.log(d / exact) / np.log(REL_MAX_DIST / exact) * (N_BUCKETS - exact)).astype(np.int32)
    large = np.minimum(large, N_BUCKETS - 1)
    return np.where(dist < exact, dist, large).astype(np.int32)


def _group_bias(rel_bias, g, dil, n_back):
    bk = _t5_bucket(np.arange(n_back + 1) * dil)
    return rel_bias[bk][:, g * H_A:(g + 1) * H_A].T.astype(F32)


def _dil_prompt(q, k, v, bias, dil, n_back):
    B, S, H, hd = q.shape
    L = S // dil
    nblk = -(-L // n_back)
    Lp = nblk * n_back

    def res(t):
        return t.reshape(B, L, dil, H, hd).transpose(0, 2, 1, 3, 4)

    qr = jnp.pad(res(q), ((0, 0), (0, 0), (0, Lp - L), (0, 0), (0, 0))).reshape(B, dil, nblk, n_back, H, hd)

    def kblocks(t):
        tp = jnp.pad(res(t), ((0, 0), (0, 0), (n_back, Lp - L), (0, 0), (0, 0)))
        tp = tp.reshape(B, dil, nblk + 1, n_back, H, hd)
        return jnp.concatenate([tp[:, :, :-1], tp[:, :, 1:]], axis=3)

    kb, vb = kblocks(k), kblocks(v)
    i = np.arange(n_back)[:, None]
    kk = np.arange(2 * n_back)[None, :]
    step = n_back + i - kk
    m_key = np.arange(nblk)[:, None, None] * n_back + kk[None] - n_back
    valid = ((step >= 0) & (step <= n_back))[None] & (m_key >= 0) & (m_key < L)
    s = jnp.einsum('brnqhd,brnkhd->brnhqk', qr, kb).astype(F32) + bias[:, np.clip(step, 0, n_back)]
    s = jnp.where(valid[None, None, :, None], s, NEG)
    mx = jnp.max(s, axis=-1, keepdims=True)
    p = jnp.exp(s - mx)
    den = jnp.sum(p, axis=-1, keepdims=True)
    o = jnp.einsum('brnhqk,brnkhd->brnqhd', p / den, vb.astype(F32))
    lse = jnp.swapaxes((mx + jnp.log(den))[..., 0], 3, 4)
    o = o.reshape(B, dil, Lp, H, hd)[:, :, :L].transpose(0, 2, 1, 3, 4).reshape(B, S, H, hd)
    lse = lse.reshape(B, dil, Lp, H)[:, :, :L].transpose(0, 2, 1, 3).reshape(B, S, H)
    return o, lse


def _dil_sample(q, keys, vals, bias, dil, n_back, wbuf):
    T = q.shape[1]
    idx = wbuf + np.arange(T)[:, None] - np.arange(n_back + 1)[None, :] * dil
    valid = idx >= 0
    idx = np.maximum(idx, 0)
    kg = keys[:, idx]
    vg = vals[:, idx]
    s = jnp.einsum('bthd,btkhd->bhtk', q, kg).astype(F32) + bias[:, None, :]
    s = jnp.where(valid, s, NEG)
    mx = jnp.max(s, axis=-1, keepdims=True)
    p = jnp.exp(s - mx)
    den = jnp.sum(p, axis=-1, keepdims=True)
    o = jnp.einsum('bhtk,btkhd->bthd', p / den, vg.astype(F32))
    lse = jnp.swapaxes((mx + jnp.log(den))[..., 0], 1, 2)
    return o, lse


def _cplx_combine(e1, e2):
    a1r, a1i, b1r, b1i = e1
    a2r, a2i, b2r, b2i = e2
    return (a2r * a1r - a2i * a1i, a2r * a1i + a2i * a1r,
            a2r * b1r - a2i * b1i + b2r, a2r * b1i + a2i * b1r + b2i)


def _s5(u, x0_re, x0_im, log_dt, a_re, a_im, b_re, b_im, c_re, c_im, d_skip):
    a_re, a_im, b_re, b_im, c_re, c_im = [t.astype(F32) for t in (a_re, a_im, b_re, b_im, c_re, c_im)]
    dt = jnp.exp(log_dt.astype(F32))[:, None]
    mag = jnp.exp(a_re * dt)
    ab_re = mag * jnp.cos(a_im * dt)
    ab_im = mag * jnp.sin(a_im * dt)
    den = a_re * a_re + a_im * a_im
    nr = ab_re - 1.0
    co_re = (nr * a_re + ab_im * a_im) / den
    co_im = (ab_im * a_re - nr * a_im) / den
    bb_re = co_re[..., None] * b_re - co_im[..., None] * b_im
    bb_im = co_re[..., None] * b_im + co_im[..., None] * b_re
    bu_re = jnp.einsum('btgp,gnp->btgn', u, bb_re)
    bu_im = jnp.einsum('btgp,gnp->btgn', u, bb_im)
    x0_re, x0_im = x0_re.astype(F32), x0_im.astype(F32)
    bu_re = bu_re.at[:, 0].add(ab_re * x0_re - ab_im * x0_im)
    bu_im = bu_im.at[:, 0].add(ab_re * x0_im + ab_im * x0_re)
    shp = bu_re.shape
    _, _, xr, xi = lax.associative_scan(
        _cplx_combine, (jnp.broadcast_to(ab_re, shp), jnp.broadcast_to(ab_im, shp), bu_re, bu_im), axis=1)
    y = (jnp.einsum('btgn,gpn->btgp', xr, c_re) - jnp.einsum('btgn,gpn->btgp', xi, c_im)
         + d_skip.astype(F32).reshape(G_B, P_B) * u)
    return y, xr[:, -1], xi[:, -1]


def _gla(q, k, v, log_a, s0):
    B, T, H, dk = q.shape
    dv = v.shape[-1]
    C = math.gcd(T, GLA_CHUNK)
    n = T // C

    def r(t):
        return t.reshape(B, n, C, H, t.shape[-1])

    q, k, v, log_a = r(q), r(k), r(v), r(log_a)
    b = jnp.cumsum(log_a, axis=2)
    b_last = b[:, :, -1]
    qt = q * jnp.exp(b)
    kt = k * jnp.exp(-b)
    ds = jnp.einsum('bnchk,bnchv->bnhkv', k * jnp.exp(b_last[:, :, None] - b), v)

    def step(s, inp):
        dec, d = inp
        return dec[..., None] * s + d, s

    s_fin, s_prev = lax.scan(step, s0.astype(F32), (jnp.moveaxis(jnp.exp(b_last), 1, 0), jnp.moveaxis(ds, 1, 0)))
    s_prev = jnp.moveaxis(s_prev, 0, 1)
    causal = np.tril(np.ones((C, C), dtype=bool))
    att = jnp.where(causal, jnp.einsum('bnchk,bnshk->bnhcs', qt, kt), 0.0)
    o = jnp.einsum('bnhcs,bnshv->bnchv', att, v) + jnp.einsum('bnchk,bnhkv->bnchv', qt, s_prev)
    return o.reshape(B, T, H, dv), s_fin


def _mixer(h, lw, rel_bias, kbuf, vbuf, ssm_re0, ssm_im0, gla0):
    (w_in, w_o_a, log_dt, a_re, a_im, b_re, b_im, c_re, c_im, d_skip, w_glu,
     w_fg2, b_fg, gla_norm, w_o_c, w_out) = lw
    B, T, _ = h.shape
    z = h @ w_in
    qa, ka, va, ub, qc, kc, vc, rc, fc, ga, gb, gc = jnp.split(z, np.cumsum(COLS)[:-1].tolist(), axis=-1)

    qa = qa.reshape(B, T, G_A, H_A, HD_A) * (HD_A ** -0.5)
    ka = ka.reshape(B, T, H_A, HD_A)
    va = va.reshape(B, T, H_A, HD_A)
    if kbuf is None:
        keys, vals = ka, va
        new_k, new_v = ka[:, -min(W_MAX, T):], va[:, -min(W_MAX, T):]
        wbuf = 0
    else:
        wbuf = kbuf.shape[1]
        keys = jnp.concatenate([kbuf.astype(ka.dtype), ka], axis=1)
        vals = jnp.concatenate([vbuf.astype(va.dtype), va], axis=1)
        new_k, new_v = keys[:, -wbuf:], vals[:, -wbuf:]
    outs, lses = [], []
    for g, (win, dil) in enumerate(PATTERNS):
        n_back = win // dil
        bias = _group_bias(rel_bias, g, dil, n_back)
        if kbuf is None:
            o, lse = _dil_prompt(qa[:, :, g], keys, vals, bias, dil, n_back)
        else:
            o, lse = _dil_sample(qa[:, :, g], keys, vals, bias, dil, n_back, wbuf)
        outs.append(o)
        lses.append(lse)
    wts = jax.nn.softmax(jnp.stack(lses), axis=0)
    o_a = jnp.einsum('gbth,gbthd->bthd', wts, jnp.stack(outs))
    y_a = o_a.reshape(B, T, H_A * HD_A).astype(h.dtype) @ w_o_a

    u = ub.reshape(B, T, G_B, P_B).astype(F32)
    y, ssm_re, ssm_im = _s5(u, ssm_re0, ssm_im0, log_dt, a_re, a_im, b_re, b_im, c_re, c_im, d_skip)
    y = jax.nn.gelu(y.reshape(B, T, D_B))
    ab = y @ w_glu.astype(F32)
    y_b = (ab[..., :D_MODEL] * jax.nn.sigmoid(ab[..., D_MODEL:])).astype(h.dtype)

    qc = qc.reshape(B, T, H_C, DK_C).astype(F32) * (DK_C ** -0.5)
    kc = kc.reshape(B, T, H_C, DK_C).astype(F32)
    vc = vc.reshape(B, T, H_C, DV_C).astype(F32)
    log_a = (jax.nn.log_sigmoid((fc @ w_fg2 + b_fg).astype(F32)) / GLA_TAU).reshape(B, T, H_C, DK_C)
    o_c, gla_s = _gla(qc, kc, vc, log_a, gla0)
    o_c = _rmsnorm(o_c, gla_norm)
    y_c = (o_c.reshape(B, T, H_C * DV_C) * jax.nn.silu(rc.astype(F32))).astype(h.dtype) @ w_o_c

    m = jax.nn.sigmoid(ga) * y_a + jax.nn.sigmoid(gb) * y_b + jax.nn.sigmoid(gc) * y_c
    return (m @ w_out).astype(h.dtype), (new_k, new_v, ssm_re, ssm_im, gla_s)


def _trunk(x, layer_w, norm_mix, norm_mlp, w_up, w_down, norm_final, rel_bias, caches):
    B = x.shape[0]
    new = [[] for _ in range(5)]
    for l in range(DEPTH):
        lw = tuple(w[l] for w in layer_w)
        if caches is None:
            c = (None, None, jnp.zeros((B, G_B, N_B), F32), jnp.zeros((B, G_B, N_B), F32),
                 jnp.zeros((B, H_C, DK_C, DV_C), F32))
        else:
            c = tuple(t[l] for t in caches)
        mix, st = _mixer(_rmsnorm(x, norm_mix[l]), lw, rel_bias, *c)
        x = x + mix
        hm = _rmsnorm(x, norm_mlp[l])
        x = x + (jnp.square(jax.nn.relu(hm @ w_up[l])) @ w_down[l]).astype(x.dtype)
        for lst, s in zip(new, st):
            lst.append(s)
    return _rmsnorm(x, norm_final), [jnp.stack(s) for s in new]


def setup_inputs(seed: int = 0) -> dict:
    key = jax.random.key(seed)
    ks = list(jax.random.split(key, 32))
    cnt = [0]

    def nk():
        cnt[0] += 1
        return ks[cnt[0] - 1]

    def nrm(shape, scale):
        return jax.random.normal(nk(), shape, F32) * scale

    D = D_MODEL
    wbuf = min(W_MAX, PAST_LEN)
    n_idx = jnp.arange(N_B, dtype=F32)
    out = {}
    out['x_prompt'] = nrm((BATCH, SEQ, D), 1.0)
    out['x_sample'] = nrm((DEC_BATCH, DEC_SEQ, D), 1.0)
    out['cache_k_win'] = nrm((DEPTH, DEC_BATCH, wbuf, H_A, HD_A), 1.0)
    out['cache_v_win'] = nrm((DEPTH, DEC_BATCH, wbuf, H_A, HD_A), 1.0)
    out['state_ssm_re'] = nrm((DEPTH, DEC_BATCH, G_B, N_B), 0.1)
    out['state_ssm_im'] = nrm((DEPTH, DEC_BATCH, G_B, N_B), 0.1)
    out['state_gla'] = nrm((DEPTH, DEC_BATCH, H_C, DK_C, DV_C), 1.0)
    out['rel_bias'] = nrm((N_BUCKETS, G_A * H_A), 0.5)
    out['norm_mix'] = 1.0 + nrm((DEPTH, D), 0.02)
    out['w_in'] = nrm((DEPTH, D, IN_COLS), D ** -0.5)
    out['w_o_a'] = nrm((DEPTH, H_A * HD_A, D), (H_A * HD_A) ** -0.5)
    out['s5_log_dt'] = math.log(DT_MIN) + jax.random.uniform(nk(), (DEPTH, G_B), F32) * (math.log(DT_MAX) - math.log(DT_MIN))
    out['s5_a_re'] = -0.5 + nrm((DEPTH, G_B, N_B), 0.01)
    out['s5_a_im'] = math.pi * n_idx + nrm((DEPTH, G_B, N_B), 0.01)
    out['s5_b_re'] = nrm((DEPTH, G_B, N_B, P_B), (2 * P_B) ** -0.5)
    out['s5_b_im'] = nrm((DEPTH, G_B, N_B, P_B), (2 * P_B) ** -0.5)
    out['s5_c_re'] = nrm((DEPTH, G_B, P_B, N_B), N_B ** -0.5)
    out['s5_c_im'] = nrm((DEPTH, G_B, P_B, N_B), N_B ** -0.5)
    out['s5_d'] = nrm((DEPTH, D_B), 1.0)
    out['w_glu'] = nrm((DEPTH, D_B, 2 * D), D_B ** -0.5)
    out['w_fg2'] = nrm((DEPTH, FG_RANK, H_C * DK_C), FG_RANK ** -0.5)
    out['b_fg'] = nrm((DEPTH, H_C * DK_C), 0.1)
    out['gla_norm'] = 1.0 + nrm((DEPTH, DV_C), 0.02)
    out['w_o_c'] = nrm((DEPTH, H_C * DV_C, D), (H_C * DV_C) ** -0.5)
    out['w_out'] = nrm((DEPTH, D, D), D ** -0.5)
    out['norm_mlp'] = 1.0 + nrm((DEPTH, D), 0.02)
    out['w_up'] = nrm((DEPTH, D, D_FF), D ** -0.5)
    out['w_down'] = nrm((DEPTH, D_FF, D), D_FF ** -0.5)
    out['norm_final'] = 1.0 + nrm((D,), 0.02)
    return out


def reference(x_prompt, x_sample, cache_k_win, cache_v_win, state_ssm_re, state_ssm_im, state_gla,
              rel_bias, norm_mix, w_in, w_o_a, s5_log_dt, s5_a_re, s5_a_im, s5_b_re, s5_b_im,
              s5_c_re, s5_c_im, s5_d, w_glu, w_fg2, b_fg, gla_norm, w_o_c, w_out,
              norm_mlp, w_up, w_down, norm_final):
    layer_w = (w_in, w_o_a, s5_log_dt, s5_a_re, s5_a_im, s5_b_re, s5_b_im, s5_c_re, s5_c_im, s5_d,
               w_glu, w_fg2, b_fg, gla_norm, w_o_c, w_out)
    y_prompt, st_p = _trunk(x_prompt, layer_w, norm_mix, norm_mlp, w_up, w_down, norm_final, rel_bias, None)
    y_sample, st_s = _trunk(x_sample, layer_w, norm_mix, norm_mlp, w_up, w_down, norm_final, rel_bias,
                            (cache_k_win, cache_v_win, state_ssm_re, state_ssm_im, state_gla))
    k_win_p, v_win_p, ssm_re_p, ssm_im_p, gla_p = st_p
    k_win_s, v_win_s, ssm_re_s, ssm_im_s, gla_s = st_s
    return (y_prompt, y_sample, k_win_p, v_win_p, k_win_s, v_win_s,
            ssm_re_p, ssm_im_p, ssm_re_s, ssm_im_s, gla_p, gla_s)
```

```python
import math
import contextlib
import numpy as np
import concourse.bass as bass
import concourse.mybir as mybir
from concourse.bass_utils import run_bass_kernel_spmd

F32 = mybir.dt.float32
BF16 = mybir.dt.bfloat16
AF = mybir.ActivationFunctionType
ALU = mybir.AluOpType
AX = mybir.AxisListType

ENGS = ("pe", "act", "dve", "pool", "sp")
L_ = 4
D = 1024
T = 2048
TS = 32
NT = T + TS
INC = 7696
EPS = 1e-6
CH = [(0, 512), (512, 512), (1024, 512), (1536, 512), (2048, 32)]
PAT = ((128, 1), (512, 4), (2048, 16))


class Reg:
    __slots__ = ("name", "w", "rs")

    def __init__(self, name):
        self.name = name
        self.w = None
        self.rs = []


class Op:
    __slots__ = ("eng", "fn", "deps", "need_inc", "dma", "slot", "slot_val", "cnt")

    def __init__(self, eng, fn, dma):
        self.eng = eng
        self.fn = fn
        self.dma = dma
        self.deps = set()
        self.need_inc = False
        self.slot = None
        self.slot_val = 0
        self.cnt = 0


class Prog:
    def __init__(self, nc, n_dma_slots=10):
        self.nc = nc
        self.ops = {e: [] for e in ENGS}
        self.n_slots = n_dma_slots
        self.dma_count = {e: 0 for e in ENGS}
        self.fence_ops = []

    def _add(self, op, reads, writes, nofence=False):
        for r in reads:
            if r.w is not None:
                op.deps.add(r.w)
        for w in writes:
            if w.w is not None:
                op.deps.add(w.w)
            for o in w.rs:
                op.deps.add(o)
        if not nofence:
            for o in self.fence_ops:
                op.deps.add(o)
        op.deps.discard(op)
        for r in reads:
            r.rs.append(op)
        for w in writes:
            w.w = op
            w.rs = []
        self.ops[op.eng].append(op)
        return op

    def fence(self):
        self.fence_ops = [self.ops[e][-1] for e in ENGS if self.ops[e]]

    def op(self, eng, fn, reads=(), writes=()):
        return self._add(Op(eng, fn, False), reads, writes)

    def dma(self, eng, fn, reads=(), writes=(), nofence=False):
        op = Op(eng, fn, True)
        k = self.dma_count[eng]
        self.dma_count[eng] += 1
        op.slot = k % self.n_slots
        op.slot_val = 16 * (k // self.n_slots + 1)
        return self._add(op, reads, writes, nofence)

    def emit(self):
        nc = self.nc
        for e in ENGS:
            for op in self.ops[e]:
                for d in op.deps:
                    if not d.dma:
                        if d.eng == "pe" and op.eng == "pe":
                            continue
                        d.need_inc = True
        for e in ENGS:
            c = 0
            for op in self.ops[e]:
                if not op.dma and op.need_inc:
                    c += 1
                op.cnt = c
        with contextlib.ExitStack() as st:
            sems = {e: st.enter_context(nc.semaphore("s_" + e)) for e in ENGS}
            dsems = {e: [st.enter_context(nc.semaphore("d_%s_%d" % (e, i))) for i in range(self.n_slots)]
                     for e in ENGS if self.dma_count[e] > 0}
            block = st.enter_context(nc.Block())

            def run(ename, eng):
                waited = {}
                dwaited = {}
                for op in self.ops[ename]:
                    need = {}
                    dneed = {}
                    for d in op.deps:
                        if d.dma:
                            key = (d.eng, d.slot)
                            if dwaited.get(key, 0) < d.slot_val:
                                dneed[key] = max(dneed.get(key, 0), d.slot_val)
                        else:
                            if d.eng == "pe" and ename == "pe":
                                continue
                            if waited.get(d.eng, 0) < d.cnt:
                                need[d.eng] = max(need.get(d.eng, 0), d.cnt)
                    if op.dma and op.slot_val > 16:
                        key = (ename, op.slot)
                        v = op.slot_val - 16
                        if dwaited.get(key, 0) < v:
                            dneed[key] = max(dneed.get(key, 0), v)
                    for pe_, v in need.items():
                        eng.wait_ge(sems[pe_], v)
                        waited[pe_] = v
                    for key, v in dneed.items():
                        eng.wait_ge(dsems[key[0]][key[1]], v)
                        dwaited[key] = v
                    ins = op.fn(eng)
                    if op.dma:
                        ins.then_inc(dsems[ename][op.slot], 16)
                    elif op.need_inc:
                        ins.then_inc(sems[ename], 1)
                if self.dma_count[ename] > 0:
                    last = {}
                    for op in self.ops[ename]:
                        if op.dma:
                            last[op.slot] = op.slot_val
                    for s, v in last.items():
                        if dwaited.get((ename, s), 0) < v:
                            eng.wait_ge(dsems[ename][s], v)

            if self.ops["sp"]:
                @block.sync
                def _(eng):
                    run("sp", eng)
            if self.ops["pe"]:
                @block.tensor
                def _(eng):
                    run("pe", eng)
            if self.ops["act"]:
                @block.scalar
                def _(eng):
                    run("act", eng)
            if self.ops["dve"]:
                @block.vector
                def _(eng):
                    run("dve", eng)
            if self.ops["pool"]:
                @block.gpsimd
                def _(eng):
                    run("pool", eng)


def _t5_bucket(dist):
    exact = 16
    d = np.maximum(dist, 1).astype(np.float32)
    large = exact + (np.log(d / exact) / np.log(2048 / exact) * (32 - exact)).astype(np.int32)
    large = np.minimum(large, 31)
    return np.where(dist < exact, dist, large).astype(np.int32)


def build(n_layers=L_, dbg=False):
    nc = bass.Bass("TRN2", target_bir_lowering=False)

    def din(name, shape):
        return nc.dram_tensor(name, list(shape), F32, kind="ExternalInput").ap()

    def dout(name, shape):
        return nc.dram_tensor(name, list(shape), F32, kind="ExternalOutput").ap()

    xp = din("xp", [T, D]); xs = din("xs", [TS, D])
    ck = din("ck", [L_, 4, 2048, 512]); cv = din("cv", [L_, 4, 2048, 512])
    w_in = din("w_in", [L_, D, INC]); w_o_a = din("w_o_a", [L_, 512, D])
    w_glu = din("w_glu", [L_, 512, 2048]); w_o_c = din("w_o_c", [L_, 512, D])
    w_out = din("w_out", [L_, D, D]); w_up = din("w_up", [L_, D, 4096]); w_down = din("w_down", [L_, 4096, D])
    ident_d = din("ident", [128, 128])
    gains_d = din("gains", [128, 9, 8])
    biasT_d = din("biasT", [24, 128, 256])
    sbias_d = din("sbias", [8, 128, 240])
    sbiasn_d = din("sbiasn", [8, 32, 96])
    aB_d = din("aB", [128, L_, 3, 16])
    cB_d = din("cB", [128, L_, 2, 16, 16])
    aA_d = din("aA", [128, L_, 5, 4, 64])
    dA_d = din("dA", [128, L_, 4])
    mk_d = din("mk", [128, 4])
    x0r_d = din("x0r", [128, L_, 16, 4]); x0i_d = din("x0i", [128, L_, 16, 4])
    efr_d = dout("efr", [L_, 128, 16]); efi_d = dout("efi", [L_, 128, 16])
    esr_d = dout("esr", [L_, 128, 64]); esi_d = dout("esi", [L_, 128, 64])
    wfg2_d = din("wfg2", [L_, 16, 256]); nbfg_d = din("nbfg", [128, L_, 2]); gnorm_d = din("gnorm", [128, L_])
    sg_d = din("sg", [L_, 4, 4, 64, 128])
    cmask_d = din("cmask", [128, 128]); smask_d = din("smask", [32, 32]); selm_d = din("selm", [32, 4])
    gp = dout("gp", [L_, 4, 64, 128]); gs = dout("gs", [L_, 4, 4, 64, 128])
    yp = dout("yp", [T, D]); ys = dout("ys", [TS, D])
    kwp = dout("kwp", [L_, T, 512]); vwp = dout("vwp", [L_, T, 512])
    kws = dout("kws", [L_, 4, 2048, 512]); vws = dout("vws", [L_, 4, 2048, 512])

    st = contextlib.ExitStack()
    with st:
        def sb(name, shape, dt):
            return st.enter_context(nc.sbuf_tensor(name, list(shape), dt))

        P = Prog(nc)
        X = sb("X", [128, 8, NT], F32); rX = [[Reg("X%d_%d" % (k, c)) for c in range(5)] for k in range(8)]
        H = sb("H", [128, 8, NT], BF16); rH = [Reg("H%d" % c) for c in range(5)]
        NSLOT = 2
        WR = [sb("wr%d" % i, [128, 4096], BF16) for i in range(NSLOT)]
        rWR = [Reg("wr%d" % i) for i in range(NSLOT)]
        ident = sb("ident_sb", [128, 128], F32); rident = Reg("ident")
        ones = sb("ones_sb", [128, 128], BF16); rones = Reg("ones")
        gains = sb("gains_sb", [128, 9, 8], F32); rgains = Reg("gains")
        identb = sb("identb_sb", [128, 128], BF16); ridentb = Reg("identb")
        nbfg = sb("nbfg_sb", [128, L_, 2], F32); rnbfg = Reg("nbfg")
        gnorm = sb("gnorm_sb", [128, L_], F32); rgnorm = Reg("gnorm")
        cmask = sb("cmask_sb", [128, 128], F32); rcmask = Reg("cmask")
        smask = sb("smask_sb", [32, 32], F32); rsmask = Reg("smask")
        selm = sb("selm_sb", [32, 4], F32); rselm = Reg("selm")
        mk = sb("mk_sb", [128, 4], F32); rmk = Reg("mk")
        negpi = sb("negpi_sb", [128, 1], F32); rnegpi = Reg("negpi")
        dA = sb("dA_sb", [128, L_, 4], F32); rdA = Reg("dA")
        PS = st.enter_context(nc.psum_tensor("PS", [128, 8, 512], F32))
        rPS = [Reg("ps%d" % i) for i in range(8)]
        SCR = sb("SCR", [128, 94208], mybir.dt.uint8)

        class Carver:
            def __init__(self):
                self.off = 0

            def reset(self):
                self.off = 0

            def take(self, shape, dt):
                es = 2 if dt == BF16 else 4
                n = int(np.prod(shape[1:])) * es
                n = (n + 63) // 64 * 64
                v = SCR[0:shape[0], self.off:self.off + int(np.prod(shape[1:])) * es].bitcast(dt)
                self.off += n
                assert self.off <= 94208, self.off
                if len(shape) == 3:
                    v = v.rearrange("p (a b) -> p a b", b=shape[2])
                elif len(shape) == 4:
                    v = v.rearrange("p (a b c) -> p a b c", b=shape[2], c=shape[3])
                return v
        CV = Carver()

        psc = [0]

        def bank():
            b = psc[0] % 8
            psc[0] += 1
            return b
        wc = [0]

        def wslot():
            s = wc[0] % NSLOT
            wc[0] += 1
            return s

        def wload_pm(srcs, K, cw):
            s = wslot()
            n = len(srcs)
            view = WR[s][:, 0:n * K * cw].rearrange("p (a k c) -> p a k c", k=K, c=cw)
            for a, src in enumerate(srcs):
                P.dma("pool", lambda e, src=src, a=a, view=view: e.dma_start(out=view[:, a, :, :], in_=src),
                      writes=[rWR[s]], nofence=True)
            return view, rWR[s]

        def wload(pieces, K, C):
            s = wslot()
            view = WR[s][:, 0:K * C].rearrange("p (k c) -> p k c", c=C)
            for (src, c0, cw) in pieces:
                P.dma("pool", lambda e, src=src, c0=c0, cw=cw, view=view: e.dma_start(out=view[:, :, c0:c0 + cw], in_=src),
                      writes=[rWR[s]], nofence=True)
            return view, rWR[s]

        def wsrc(w, l, kk, c0, cw):
            return w[l].rearrange("(k p) c -> p k c", p=128)[:, 0:kk, c0:c0 + cw]

        P.dma("sp", lambda e: e.dma_start(out=ident[:], in_=ident_d), writes=[rident])
        P.dma("sp", lambda e: e.dma_start(out=gains[:], in_=gains_d), writes=[rgains])
        P.op("dve", lambda e: e.memset(ones[:], 1.0), writes=[rones])
        P.op("dve", lambda e: e.memset(negpi[:], -math.pi), writes=[rnegpi])
        P.op("dve", lambda e: e.tensor_copy(identb[:], ident[:]), reads=[rident], writes=[ridentb])
        P.dma("sp", lambda e: e.dma_start(out=nbfg[:], in_=nbfg_d), writes=[rnbfg])
        P.dma("sp", lambda e: e.dma_start(out=mk[:], in_=mk_d), writes=[rmk])
        P.dma("sp", lambda e: e.dma_start(out=dA[:], in_=dA_d), writes=[rdA])
        P.dma("sp", lambda e: e.dma_start(out=gnorm[:], in_=gnorm_d), writes=[rgnorm])
        P.dma("sp", lambda e: e.dma_start(out=cmask[:], in_=cmask_d), writes=[rcmask])
        P.dma("sp", lambda e: e.dma_start(out=smask[:], in_=smask_d), writes=[rsmask])
        P.dma("sp", lambda e: e.dma_start(out=selm[:], in_=selm_d), writes=[rselm])

        CV.reset()
        xtok = [CV.take([128, 1024], F32) for _ in range(2)]
        rxtok = [Reg("xtok0"), Reg("xtok1")]
        for tb in range(17):
            i = tb % 2
            ntok = 128 if tb < 16 else TS
            src = xp[tb * 128:(tb + 1) * 128, :] if tb < 16 else xs
            P.dma("sp", lambda e, i=i, ntok=ntok, src=src: e.dma_start(out=xtok[i][0:ntok, :], in_=src), writes=[rxtok[i]])
            for half in range(2):
                b = bank()
                for kk in range(4):
                    k = half * 4 + kk
                    P.op("pe", lambda e, b=b, kk=kk, k=k, i=i, ntok=ntok: e.transpose(
                        PS[:, b, kk * 128:kk * 128 + ntok], xtok[i][0:ntok, k * 128:(k + 1) * 128], ident[0:ntok, 0:ntok]),
                        reads=[rxtok[i], rident], writes=[rPS[b]])
                c = tb // 4 if tb < 16 else 4
                P.op("act", lambda e, b=b, half=half, tb=tb, ntok=ntok: e.copy(
                    X[:, half * 4:half * 4 + 4, tb * 128:tb * 128 + ntok],
                    PS[:, b, :].rearrange("p (k t) -> p k t", t=128)[:, :, 0:ntok]),
                    reads=[rPS[b]], writes=[rX[k][c] for k in range(half * 4, half * 4 + 4)])
        P.fence()

        def rmsnorm_to_H(gi, SQ, rSQ, RS, rRS):
            for ci, (c0, n) in enumerate(CH):
                P.op("act", lambda e, c0=c0, n=n: e.activation(SQ[:, :, 0:n], X[:, :, c0:c0 + n], AF.Square),
                     reads=[rX[k][ci] for k in range(8)], writes=[rSQ])
                b = bank()
                for k in range(8):
                    P.op("pe", lambda e, b=b, k=k, n=n: e.matmul(PS[:, b, 0:n], ones[:], SQ[:, k, 0:n], start=(k == 0), stop=(k == 7)),
                         reads=[rSQ, rones], writes=[rPS[b]])
                P.op("dve", lambda e, b=b, n=n: e.tensor_scalar(RS[:, 0:n], PS[:, b, 0:n], 1.0 / D, EPS, ALU.mult, ALU.add),
                     reads=[rPS[b]], writes=[rRS])
                P.op("act", lambda e, n=n: e.activation(RS[:, 0:n], RS[:, 0:n], AF.Sqrt), reads=[rRS], writes=[rRS])
                P.op("dve", lambda e, n=n: e.reciprocal(RS[:, 0:n], RS[:, 0:n]), reads=[rRS], writes=[rRS])
                for k in range(8):
                    P.op("dve", lambda e, k=k, c0=c0, n=n: e.scalar_tensor_tensor(
                        H[:, k, c0:c0 + n], X[:, k, c0:c0 + n], gains[:, gi, k:k + 1], RS[:, 0:n], ALU.mult, ALU.mult),
                        reads=[rX[k][ci], rRS, rgains], writes=[rH[ci]])

        def mlp(l):
            CV.reset()
            SQ = CV.take([128, 8, 512], BF16); rSQ = Reg("sq")
            RS = CV.take([128, 512], F32); rRS = Reg("rs")
            rmsnorm_to_H(2 * l + 1, SQ, rSQ, RS, rRS)
            A = CV.take([128, 32, 512], BF16); rA = Reg("A")
            for ci, (c0, n) in enumerate(CH):
                for u in range(8):
                    W, rW = wload([(wsrc(w_up, l, 8, u * 512, 512), 0, 512)], 8, 512)
                    for m in range(4):
                        b = bank()
                        for k in range(8):
                            P.op("pe", lambda e, b=b, k=k, m=m, W=W, c0=c0, n=n: e.matmul(
                                PS[:, b, 0:n], W[:, k, m * 128:(m + 1) * 128], H[:, k, c0:c0 + n], start=(k == 0), stop=(k == 7)),
                                reads=[rW, rH[ci]], writes=[rPS[b]])
                        P.op("act", lambda e, b=b, n=n, u=u, m=m: e.activation(A[:, u * 4 + m, 0:n], PS[:, b, 0:n], AF.Relu),
                             reads=[rPS[b]], writes=[rA])
                        P.op("dve", lambda e, n=n, u=u, m=m: e.tensor_tensor(A[:, u * 4 + m, 0:n], A[:, u * 4 + m, 0:n], A[:, u * 4 + m, 0:n], ALU.mult),
                             reads=[rA], writes=[rA])
                for mu in range(2):
                    bs = [bank() for _ in range(4)]
                    for kq in range(4):
                        src = w_down[l].rearrange("(k p) c -> p k c", p=128)[:, kq * 8:(kq + 1) * 8, mu * 512:(mu + 1) * 512]
                        W, rW = wload([(src, 0, 512)], 8, 512)
                        for m in range(4):
                            for k in range(8):
                                kk = kq * 8 + k
                                P.op("pe", lambda e, b=bs[m], k=k, kk=kk, m=m, W=W, n=n: e.matmul(
                                    PS[:, b, 0:n], W[:, k, m * 128:(m + 1) * 128], A[:, kk, 0:n], start=(kk == 0), stop=(kk == 31)),
                                    reads=[rW, rA], writes=[rPS[bs[m]]])
                    for m in range(4):
                        mt = mu * 4 + m
                        P.op("dve", lambda e, b=bs[m], mt=mt, c0=c0, n=n: e.tensor_tensor(
                            X[:, mt, c0:c0 + n], X[:, mt, c0:c0 + n], PS[:, b, 0:n], ALU.add),
                            reads=[rPS[bs[m]], rX[mt][ci]], writes=[rX[mt][ci]])
            P.fence()

        def branch_out(l, Y, rY, w_o, gcol, glu=False):
            G = CV.take([128, 8, NT], BF16); rG = [Reg("G%d" % c) for c in range(5)]
            SG = CV.take([128, 512], F32); rSG = Reg("SG")
            SG2 = CV.take([128, 512], F32); rSG2 = Reg("SG2")
            for half in range(2):
                Wg, rWg = wload([(wsrc(w_in, l, 8, gcol + half * 512, 512), 0, 512)], 8, 512)
                if not glu:
                    Wo, rWo = wload([(wsrc(w_o, l, 4, half * 512, 512), 0, 512)], 4, 512)
                else:
                    Wo, rWo = wload([(wsrc(w_o, l, 4, half * 512, 512), 0, 512), (wsrc(w_o, l, 4, 1024 + half * 512, 512), 512, 512)], 4, 1024)
                for m in range(4):
                    mt = half * 4 + m
                    for ci, (c0, n) in enumerate(CH):
                        bg = bank()
                        for k in range(8):
                            P.op("pe", lambda e, b=bg, k=k, m=m, Wg=Wg, c0=c0, n=n: e.matmul(
                                PS[:, b, 0:n], Wg[:, k, m * 128:(m + 1) * 128], H[:, k, c0:c0 + n], start=(k == 0), stop=(k == 7)),
                                reads=[rWg, rH[ci]], writes=[rPS[bg]])
                        P.op("act", lambda e, b=bg, n=n: e.activation(SG[:, 0:n], PS[:, b, 0:n], AF.Sigmoid), reads=[rPS[bg]], writes=[rSG])
                        by = bank()
                        for k in range(4):
                            P.op("pe", lambda e, b=by, k=k, m=m, Wo=Wo, c0=c0, n=n: e.matmul(
                                PS[:, b, 0:n], Wo[:, k, m * 128:(m + 1) * 128], Y[:, k, c0:c0 + n], start=(k == 0), stop=(k == 3)),
                                reads=[rWo, rY], writes=[rPS[by]])
                        if glu:
                            b2 = bank()
                            for k in range(4):
                                P.op("pe", lambda e, b=b2, k=k, m=m, Wo=Wo, c0=c0, n=n: e.matmul(
                                    PS[:, b, 0:n], Wo[:, k, 512 + m * 128:512 + (m + 1) * 128], Y[:, k, c0:c0 + n], start=(k == 0), stop=(k == 3)),
                                    reads=[rWo, rY], writes=[rPS[b2]])
                            P.op("act", lambda e, b=b2, n=n: e.activation(SG2[:, 0:n], PS[:, b, 0:n], AF.Sigmoid), reads=[rPS[b2]], writes=[rSG2])
                            P.op("dve", lambda e, n=n: e.tensor_tensor(SG[:, 0:n], SG[:, 0:n], SG2[:, 0:n], ALU.mult), reads=[rSG, rSG2], writes=[rSG])
                        P.op("dve", lambda e, b=by, mt=mt, c0=c0, n=n: e.tensor_tensor(G[:, mt, c0:c0 + n], SG[:, 0:n], PS[:, b, 0:n], ALU.mult),
                             reads=[rSG, rPS[by]], writes=[rG[ci]])
            for half in range(2):
                Wo, rWo = wload([(wsrc(w_out, l, 8, half * 512, 512), 0, 512)], 8, 512)
                for m in range(4):
                    mt = half * 4 + m
                    for ci, (c0, n) in enumerate(CH):
                        b = bank()
                        for k in range(8):
                            P.op("pe", lambda e, b=b, k=k, m=m, Wo=Wo, c0=c0, n=n: e.matmul(
                                PS[:, b, 0:n], Wo[:, k, m * 128:(m + 1) * 128], G[:, k, c0:c0 + n], start=(k == 0), stop=(k == 7)),
                                reads=[rWo, rG[ci]], writes=[rPS[b]])
                        P.op("dve", lambda e, b=b, mt=mt, c0=c0, n=n: e.tensor_tensor(
                            X[:, mt, c0:c0 + n], X[:, mt, c0:c0 + n], PS[:, b, 0:n], ALU.add),
                            reads=[rPS[b], rX[mt][ci]], writes=[rX[mt][ci]])

        def attention(l):
            CV.reset()
            OA = CV.take([128, 4, NT], BF16); rOA = Reg("OA")
            mark = CV.off
            QT = CV.take([128, 3, NT], BF16); rQT = Reg("QT")
            KT = CV.take([128, NT], BF16); rKT = Reg("KT")
            Vn = CV.take([128, 3, 16, 128], BF16); rVn = Reg("Vn")
            ACN = CV.take([64, NT], F32); rACN = Reg("ACN")
            ACD = CV.take([64, NT], F32); rACD = Reg("ACD")
            EXB = CV.take([128, 6, 256], BF16); rBT = Reg("EXB")
            BTs = [CV.take([128, 256], F32) for _ in range(2)]; rBTs = [Reg("bts0"), Reg("bts1")]
            KVst = [CV.take([128, 256], F32) for _ in range(2)]; rKVst = [Reg("kvst0"), Reg("kvst1")]
            T1 = [CV.take([128, 256], BF16) for _ in range(3)]; rT1 = [Reg("t1%d" % i) for i in range(3)]
            PT = [CV.take([128, 256], BF16) for _ in range(6)]; rPT = [Reg("pt%d" % i) for i in range(6)]
            KCf = CV.take([128, 10, 128], F32); rKCf = Reg("KCf")
            VCf = CV.take([128, 10, 128], F32); rVCf = Reg("VCf")
            VC = CV.take([128, 10, 128], BF16); rVC = Reg("VC")

            def load_cache(hp, bq):
                for (src_d, dstt, rdst) in ((ck, KCf, rKCf), (cv, VCf, rVCf)):
                    P.dma("sp", lambda e, src_d=src_d, dstt=dstt, bq=bq, hp=hp: e.dma_start(
                        out=dstt[:, 0:4, :], in_=src_d[l, bq, 1536:2048, hp * 128:(hp + 1) * 128].rearrange("(k p) d -> p k d", p=128)), writes=[rdst])
                    for u in range(6):
                        P.dma("sp", lambda e, src_d=src_d, dstt=dstt, bq=bq, hp=hp, u=u: e.dma_start(
                            out=dstt[:, 4 + u, :],
                            in_=src_d[l, bq, 256 * u:256 * u + 256, hp * 128:(hp + 1) * 128].rearrange("(a c) d -> a c d", c=16)[:, 0:8, :]), writes=[rdst])
            KCT = CV.take([128, 10, 128], BF16); rKCT = Reg("KCT")
            SBt = CV.take([128, 2, 240], F32); rSBt = Reg("SBt")
            SBn = CV.take([32, 2, 96], F32); rSBn = Reg("SBn")
            T1s = CV.take([128, 240], F32); rT1s = Reg("T1s")
            T1n = CV.take([32, 24], F32); rT1n = Reg("T1n")
            PTs = CV.take([128, 10, 8], BF16); rPTs = Reg("PTs")
            PTn = CV.take([32, 8], BF16); rPTn = Reg("PTn")
            Vsn = CV.take([32, 128], BF16); rVsn = Reg("Vsn")
            RCs = CV.take([64, 32], F32); rRCs = Reg("RCs")
            print("ATT scratch used", CV.off, "of 94208")
            for hp in range(4):
                W, rW = wload_pm([wsrc(w_in, l, 8, g * 512 + hp * 128, 128) for g in range(3)]
                                 + [wsrc(w_in, l, 8, 1536 + hp * 128, 128)], 8, 128)
                for m in range(4):
                    for ci, (c0, n) in enumerate(CH):
                        b = bank()
                        for k in range(8):
                            P.op("pe", lambda e, b=b, k=k, m=m, W=W, c0=c0, n=n: e.matmul(
                                PS[:, b, 0:n], W[:, m, k, :], H[:, k, c0:c0 + n], start=(k == 0), stop=(k == 7)),
                                reads=[rW, rH[ci]], writes=[rPS[b]])
                        if m < 3:
                            P.op("act", lambda e, b=b, m=m, c0=c0, n=n: e.mul(QT[:, m, c0:c0 + n], PS[:, b, 0:n], 0.125),
                                 reads=[rPS[b]], writes=[rQT])
                        else:
                            P.op("act", lambda e, b=b, c0=c0, n=n: e.copy(KT[:, c0:c0 + n], PS[:, b, 0:n]), reads=[rPS[b]], writes=[rKT])
                W2, rW2 = wload_pm([wsrc(w_in, l, 8, 1536 + hp * 128, 128), wsrc(w_in, l, 8, 2048 + hp * 128, 128)], 8, 128)
                load_cache(hp, 0)
                import os
                KSUB = int(os.environ.get("KSUB", "7"))
                for tb in range(17 if (KSUB & 1) else 0):
                    ntok = 128 if tb < 16 else TS
                    b = bank()
                    for a in range(2):
                        for k in range(8):
                            P.op("pe", lambda e, b=b, k=k, a=a, tb=tb, ntok=ntok, W2=W2: e.matmul(
                                PS[0:ntok, b, a * 128:(a + 1) * 128], H[:, k, tb * 128:tb * 128 + ntok], W2[:, a, k, :], start=(k == 0), stop=(k == 7)),
                                reads=[rW2, rH[tb // 4 if tb < 16 else 4]], writes=[rPS[b]])
                    i = tb % 2
                    P.op("act", lambda e, b=b, i=i, ntok=ntok: e.copy(KVst[i][0:ntok, :], PS[0:ntok, b, 0:256]), reads=[rPS[b]], writes=[rKVst[i]])
                    if tb < 16:
                        P.op("dve", lambda e, i=i, tb=tb: e.tensor_copy(Vn[:, 0, tb, :], KVst[i][:, 128:256]), reads=[rKVst[i]], writes=[rVn])
                        P.dma("sp", lambda e, i=i, tb=tb, hp=hp: e.dma_start(out=kwp[l, tb * 128:(tb + 1) * 128, hp * 128:(hp + 1) * 128], in_=KVst[i][:, 0:128]), reads=[rKVst[i]])
                        P.dma("sp", lambda e, i=i, tb=tb, hp=hp: e.dma_start(out=vwp[l, tb * 128:(tb + 1) * 128, hp * 128:(hp + 1) * 128], in_=KVst[i][:, 128:256]), reads=[rKVst[i]])
                    else:
                        P.op("dve", lambda e, i=i: e.tensor_copy(Vsn[:], KVst[i][0:32, 128:256]), reads=[rKVst[i]], writes=[rVsn])
                        for bq in range(4):
                            P.dma("sp", lambda e, i=i, hp=hp, bq=bq: e.dma_start(out=kws[l, bq, 2040:2048, hp * 128:(hp + 1) * 128], in_=KVst[i][bq * 8:(bq + 1) * 8, 0:128]), reads=[rKVst[i]])
                            P.dma("sp", lambda e, i=i, hp=hp, bq=bq: e.dma_start(out=vws[l, bq, 2040:2048, hp * 128:(hp + 1) * 128], in_=KVst[i][bq * 8:(bq + 1) * 8, 128:256]), reads=[rKVst[i]])
                for g in ((1, 2) if (KSUB & 2) else ()):
                    d = PAT[g][1]
                    nblk = T // d // 128
                    for r in range(d):
                        for nb in range(nblk):
                            t0 = r + d * 128 * nb
                            b = bank()
                            for k in range(8):
                                P.op("pe", lambda e, b=b, k=k, t0=t0, d=d, W2=W2: e.matmul(
                                    PS[:, b, 0:128], H[:, k, t0:t0 + 128 * d:d], W2[:, 1, k, :], start=(k == 0), stop=(k == 7)),
                                    reads=[rW2] + rH[0:4], writes=[rPS[b]])
                            P.op("act", lambda e, b=b, g=g, r=r, nb=nb, nblk=nblk: e.copy(Vn[:, g, r * nblk + nb, :], PS[:, b, 0:128]),
                                 reads=[rPS[b]], writes=[rVn])
                for gi in range(6):
                    g, hh_ = gi // 2, gi % 2
                    P.dma("sp", lambda e, hp=hp, g=g, hh_=hh_, gi=gi: e.dma_start(out=BTs[gi % 2][:], in_=biasT_d[g * 8 + 2 * hp + hh_]), writes=[rBTs[gi % 2]])
                    P.op("act", lambda e, gi=gi: e.activation(EXB[:, gi, :], BTs[gi % 2][:], AF.Exp), reads=[rBTs[gi % 2]], writes=[rBT])
                import os
                KATT = int(os.environ.get("KATT", "3"))
                for hh in range(2 if KATT >= 2 else 0):
                    hs = slice(hh * 64, hh * 64 + 64)
                    tiles = []
                    for g in range(3):
                        d = PAT[g][1]
                        nblk = T // d // 128
                        for r in range(d):
                            for kb in range(nblk):
                                tiles.append((g, d, nblk, r, kb))
                    st_ = {}

                    def s_stage(i):
                        g, d, nblk, r, kb = tiles[i]
                        nq = 256 if kb < nblk - 1 else 128
                        k0 = r + d * 128 * kb
                        bS = bank()
                        P.op("pe", lambda e, b=bS, k0=k0, d=d, nq=nq, g=g, hs=hs: e.matmul(
                            PS[:, b, 0:nq], KT[hs, k0:k0 + 128 * d:d], QT[hs, g, k0:k0 + nq * d:d], start=True, stop=True),
                            reads=[rKT, rQT], writes=[rPS[bS]])
                        ti = i % 3
                        pidx = i % 6
                        P.op("act", lambda e, b=bS, nq=nq, ti=ti: e.activation(T1[ti][:, 0:nq], PS[:, b, 0:nq], AF.Exp),
                             reads=[rPS[bS]], writes=[rT1[ti]])
                        P.op("dve", lambda e, nq=nq, ti=ti, pidx=pidx, g=g, hh=hh: e.tensor_tensor(
                            PT[pidx][:, 0:nq], T1[ti][:, 0:nq], EXB[:, g * 2 + hh, 0:nq], ALU.mult),
                            reads=[rT1[ti], rBT], writes=[rPT[pidx]])

                    def pv_stage(i):
                        g, d, nblk, r, kb = tiles[i]
                        pidx = i % 6
                        qb = kb
                        if qb % 4 == 0:
                            st_["bN"] = bank(); st_["bD"] = bank()
                        bN, bD = st_["bN"], st_["bD"]
                        col = (qb % 4) * 128
                        first = True
                        if qb > 0:
                            pp = (i - 1) % 6
                            vblk = r * nblk + qb - 1
                            P.op("pe", lambda e, b=bN, col=col, g=g, vblk=vblk, hs=hs, pp=pp: e.matmul(
                                PS[0:64, b, col:col + 128], Vn[:, g, vblk, hs], PT[pp][:, 128:256], start=True, stop=False),
                                reads=[rVn, rPT[pp]], writes=[rPS[bN]])
                            P.op("pe", lambda e, b=bD, col=col, pp=pp: e.matmul(
                                PS[0:64, b, col:col + 128], ones[:, 0:64], PT[pp][:, 128:256], start=True, stop=False),
                                reads=[rones, rPT[pp]], writes=[rPS[bD]])
                            first = False
                        vblk = r * nblk + qb
                        P.op("pe", lambda e, b=bN, col=col, g=g, vblk=vblk, hs=hs, pidx=pidx, first=first: e.matmul(
                            PS[0:64, b, col:col + 128], Vn[:, g, vblk, hs], PT[pidx][:, 0:128], start=first, stop=True),
                            reads=[rVn, rPT[pidx]], writes=[rPS[bN]])
                        P.op("pe", lambda e, b=bD, col=col, pidx=pidx, first=first: e.matmul(
                            PS[0:64, b, col:col + 128], ones[:, 0:64], PT[pidx][:, 0:128], start=first, stop=True),
                            reads=[rones, rPT[pidx]], writes=[rPS[bD]])
                        if qb % 4 == 3 or qb == nblk - 1:
                            q0b = (qb // 4) * 4
                            nqq = (qb - q0b + 1) * 128
                            tt0 = r + d * 128 * q0b
                            dst = slice(tt0, tt0 + nqq * d, d)
                            if g == 0:
                                P.op("act", lambda e, b=bN, nqq=nqq, dst=dst: e.copy(ACN[:, dst], PS[0:64, b, 0:nqq]), reads=[rPS[bN]], writes=[rACN])
                                P.op("act", lambda e, b=bD, nqq=nqq, dst=dst: e.copy(ACD[:, dst], PS[0:64, b, 0:nqq]), reads=[rPS[bD]], writes=[rACD])
                            else:
                                P.op("dve", lambda e, b=bN, nqq=nqq, dst=dst: e.tensor_tensor(ACN[:, dst], ACN[:, dst], PS[0:64, b, 0:nqq], ALU.add), reads=[rPS[bN], rACN], writes=[rACN])
                                P.op("dve", lambda e, b=bD, nqq=nqq, dst=dst: e.tensor_tensor(ACD[:, dst], ACD[:, dst], PS[0:64, b, 0:nqq], ALU.add), reads=[rPS[bD], rACD], writes=[rACD])

                    LA = 2
                    for i in range(len(tiles) + LA):
                        if i < len(tiles):
                            s_stage(i)
                        if i - LA >= 0:
                            pv_stage(i - LA)
                    P.op("dve", lambda e: e.reciprocal(ACD[:, 0:T], ACD[:, 0:T]), reads=[rACD], writes=[rACD])
                    P.op("dve", lambda e, hs=hs, hp=hp: e.tensor_tensor(OA[hs, hp, 0:T], ACN[:, 0:T], ACD[:, 0:T], ALU.mult), reads=[rACN, rACD], writes=[rOA])
                P.dma("sp", lambda e, hp=hp: e.dma_start(out=SBt[:], in_=sbias_d[2 * hp:2 * hp + 2].rearrange("h p c -> p h c")), writes=[rSBt])
                P.dma("sp", lambda e, hp=hp: e.dma_start(out=SBn[:], in_=sbiasn_d[2 * hp:2 * hp + 2].rearrange("h p c -> p h c")), writes=[rSBn])
                for bq in range(4):
                    for grp, nblkg in ((0, 4), (1, 4), (2, 2)):
                        b = bank()
                        for kk in range(nblkg):
                            blk = grp * 4 + kk
                            P.op("pe", lambda e, b=b, kk=kk, blk=blk: e.transpose(PS[:, b, kk * 128:(kk + 1) * 128], KCf[:, blk, :], ident[:, :]),
                                 reads=[rKCf, rident], writes=[rPS[b]])
                        P.op("act", lambda e, b=b, grp=grp, nblkg=nblkg: e.copy(
                            KCT[:, grp * 4:grp * 4 + nblkg, :], PS[:, b, 0:nblkg * 128].rearrange("p (k r) -> p k r", r=128)),
                            reads=[rPS[b]], writes=[rKCT])
                    P.op("dve", lambda e: e.tensor_copy(VC[:], VCf[:]), reads=[rVCf], writes=[rVC])
                    if bq < 3:
                        load_cache(hp, bq + 1)
                    for hh in range(2):
                        hs = slice(hh * 64, hh * 64 + 64)
                        qs = QT[hs, :, T + 8 * bq:T + 8 * bq + 8]
                        bS = bank()
                        for blk in range(10):
                            P.op("pe", lambda e, b=bS, blk=blk, hs=hs, qs=qs: e.matmul(PS[:, b, blk * 24:(blk + 1) * 24], KCT[hs, blk, :], qs, start=True, stop=True),
                                 reads=[rKCT, rQT], writes=[rPS[bS]])
                        P.op("pe", lambda e, b=bS, hs=hs, qs=qs: e.matmul(PS[0:32, b, 240:264], KT[hs, T:NT], qs, start=True, stop=True),
                             reads=[rKT, rQT], writes=[rPS[bS]])
                        P.op("dve", lambda e, b=bS, hh=hh: e.tensor_tensor(T1s[:], PS[:, b, 0:240], SBt[:, hh, :], ALU.add), reads=[rPS[bS], rSBt], writes=[rT1s])
                        P.op("dve", lambda e, b=bS, hh=hh, bq=bq: e.tensor_tensor(T1n[:], PS[0:32, b, 240:264], SBn[:, hh, bq * 24:(bq + 1) * 24], ALU.add),
                             reads=[rPS[bS], rSBn], writes=[rT1n, rPS[bS]])
                        P.op("act", lambda e: e.activation(T1s[:], T1s[:], AF.Exp), reads=[rT1s], writes=[rT1s])
                        P.op("act", lambda e: e.activation(T1n[:], T1n[:], AF.Exp), reads=[rT1n], writes=[rT1n])
                        T1v = T1s[:].rearrange("p (k g t) -> p k g t", g=3, t=8)
                        P.op("dve", lambda e, T1v=T1v: e.tensor_tensor(T1v[:, :, 0, :], T1v[:, :, 0, :], T1v[:, :, 1, :], ALU.add), reads=[rT1s], writes=[rT1s])
                        P.op("dve", lambda e, T1v=T1v: e.tensor_tensor(PTs[:], T1v[:, :, 0, :], T1v[:, :, 2, :], ALU.add), reads=[rT1s], writes=[rPTs])
                        P.op("dve", lambda e: e.tensor_tensor(T1n[:, 0:8], T1n[:, 0:8], T1n[:, 8:16], ALU.add), reads=[rT1n], writes=[rT1n])
                        P.op("dve", lambda e: e.tensor_tensor(PTn[:], T1n[:, 0:8], T1n[:, 16:24], ALU.add), reads=[rT1n], writes=[rPTn])
                        cols = slice(0, 8)
                        bN = bank(); bD = bank()
                        for blk in range(10):
                            P.op("pe", lambda e, b=bN, blk=blk, hs=hs, cols=cols, VC=VC: e.matmul(PS[0:64, b, cols], VC[:, blk, hs], PTs[:, blk, :], start=(blk == 0), stop=False),
                                 reads=[rVC, rPTs], writes=[rPS[bN]])
                        P.op("pe", lambda e, b=bN, hs=hs, cols=cols: e.matmul(PS[0:64, b, cols], Vsn[:, hs], PTn[:], start=False, stop=True),
                             reads=[rVsn, rPTn], writes=[rPS[bN]])
                        for blk in range(10):
                            P.op("pe", lambda e, b=bD, blk=blk, cols=cols: e.matmul(PS[0:64, b, cols], ones[:, 0:64], PTs[:, blk, :], start=(blk == 0), stop=False),
                                 reads=[rones, rPTs], writes=[rPS[bD]])
                        P.op("pe", lambda e, b=bD, cols=cols: e.matmul(PS[0:64, b, cols], ones[0:32, 0:64], PTn[:], start=False, stop=True),
                             reads=[rones, rPTn], writes=[rPS[bD]])
                        P.op("dve", lambda e, b=bD: e.reciprocal(RCs[:, 0:8], PS[0:64, b, 0:8]), reads=[rPS[bD]], writes=[rRCs])
                        P.op("dve", lambda e, b=bN, hs=hs, hp=hp, bq=bq: e.tensor_tensor(OA[hs, hp, T + 8 * bq:T + 8 * bq + 8], PS[0:64, b, 0:8], RCs[:, 0:8], ALU.mult),
                             reads=[rPS[bN], rRCs], writes=[rOA])
            P.fence()
            CV.off = mark
            if KATT < 3:
                return
            branch_out(l, OA, rOA, w_o_a, 4624)
            P.fence()

        def gla(l):
            CV.reset()
            OC = CV.take([128, 4, NT], BF16); rOC = Reg("OC")
            mark = CV.off
            QT = CV.take([128, 2, NT], BF16); rQT = Reg("gQT")
            KT = CV.take([128, 2, NT], BF16); rKT = Reg("gKT")
            EBL = CV.take([128, 2, 36], F32); rEBL = Reg("EBL")
            WF = CV.take([16, 256], BF16); rWF = Reg("WF")
            S0b = CV.take([128, 2, 4, 128], BF16); rS0b = Reg("S0b")
            S0f = CV.take([128, 2, 4, 128], F32); rS0f = Reg("S0f")
            r2 = CV.off
            KLT = CV.take([128, 2, NT], BF16); rKLT = Reg("KLT")
            _o = CV.off
            CV.off = r2
            SP = CV.take([128, 31, 128], BF16); rSP = rKLT
            CV.off = _o
            ph_b = CV.off
            L1 = CV.take([128, NT], F32); rL1 = Reg("L1")
            C_ = CV.take([128, NT], F32); rC = Reg("C")
            MASK = CV.take([128, NT], F32); rMASK = Reg("MASK")
            FC = CV.take([16, NT], BF16); rFC = Reg("FC")
            EBc = CV.take([128, 512], F32); rEBc = Reg("EBc")
            ENBc = CV.take([128, 512], F32); rENBc = Reg("ENBc")
            KTf = CV.take([128, 512], F32); rKTf = Reg("KTf")
            P.dma("pool", lambda e: e.dma_start(out=WF[:], in_=wfg2_d[l]), writes=[rWF])
            for hh in range(2):
                for m in range(2):
                    P.dma("sp", lambda e, hh=hh, m=m: e.dma_start(
                        out=S0f[hh * 64:(hh + 1) * 64, m, :, :],
                        in_=sg_d[l][:, 2 * m + hh, :, :].rearrange("b k v -> k b v")), writes=[rS0f])
            P.op("act", lambda e: e.copy(S0b[:], S0f[:]), reads=[rS0f], writes=[rS0b])
            P.op("dve", lambda e: e.memset(MASK[:], 1.0), writes=[rMASK])
            P.op("dve", lambda e: e.memset(MASK[:, 0:T].rearrange("p (c s) -> p c s", s=64)[:, :, 0:1], 0.0), writes=[rMASK])
            P.op("dve", lambda e: e.memset(MASK[:, T:NT].rearrange("p (c s) -> p c s", s=8)[:, :, 0:1], 0.0), writes=[rMASK])
            Wfc, rWfc = wload_pm([wsrc(w_in, l, 8, 4608, 16)], 8, 16)
            for ci, (c0, n) in enumerate(CH):
                b = bank()
                for k in range(8):
                    P.op("pe", lambda e, b=b, k=k, c0=c0, n=n: e.matmul(PS[0:16, b, 0:n], Wfc[:, 0, k, :], H[:, k, c0:c0 + n], start=(k == 0), stop=(k == 7)),
                         reads=[rWfc, rH[ci]], writes=[rPS[b]])
                P.op("act", lambda e, b=b, c0=c0, n=n: e.copy(FC[:, c0:c0 + n], PS[0:16, b, 0:n]), reads=[rPS[b]], writes=[rFC])
            Wqk, rWqk = wload([(wsrc(w_in, l, 8, 3072, 512), 0, 512)], 8, 512)
            for m in range(2):
                for ci, (c0, n) in enumerate(CH):
                    b = bank()
                    P.op("pe", lambda e, b=b, m=m, c0=c0, n=n: e.matmul(PS[:, b, 0:n], WF[:, m * 128:(m + 1) * 128], FC[:, c0:c0 + n], start=True, stop=True),
                         reads=[rWF, rFC], writes=[rPS[b]])
                    P.op("act", lambda e, b=b, m=m, c0=c0, n=n: e.activation(L1[:, c0:c0 + n], PS[:, b, 0:n], AF.Exp, bias=nbfg[:, l, m:m + 1], scale=-1.0),
                         reads=[rPS[b], rnbfg], writes=[rL1])
                P.op("act", lambda e: e.activation(L1[:], L1[:], AF.Ln, bias=1.0), reads=[rL1], writes=[rL1])
                P.op("dve", lambda e: e.tensor_tensor_scan(C_[:], MASK[:], L1[:], 0.0, ALU.mult, ALU.add), reads=[rMASK, rL1], writes=[rC])
                P.op("act", lambda e, m=m: e.activation(EBL[:, m, 0:32], C_[:, 63:T:64], AF.Exp, scale=-1.0 / 16), reads=[rC], writes=[rEBL])
                P.op("act", lambda e, m=m: e.activation(EBL[:, m, 32:36], C_[:, T + 7:NT:8], AF.Exp, scale=-1.0 / 16), reads=[rC], writes=[rEBL])
                for ci, (c0, n) in enumerate(CH):
                    P.op("act", lambda e, c0=c0, n=n: e.activation(EBc[:, 0:n], C_[:, c0:c0 + n], AF.Exp, scale=-1.0 / 16), reads=[rC], writes=[rEBc])
                    P.op("act", lambda e, c0=c0, n=n: e.activation(ENBc[:, 0:n], C_[:, c0:c0 + n], AF.Exp, scale=1.0 / 16), reads=[rC], writes=[rENBc])
                    bq = bank()
                    for k in range(8):
                        P.op("pe", lambda e, b=bq, k=k, m=m, c0=c0, n=n: e.matmul(PS[:, b, 0:n], Wqk[:, k, m * 128:(m + 1) * 128], H[:, k, c0:c0 + n], start=(k == 0), stop=(k == 7)),
                             reads=[rWqk, rH[ci]], writes=[rPS[bq]])
                    P.op("dve", lambda e, b=bq, m=m, c0=c0, n=n: e.scalar_tensor_tensor(QT[:, m, c0:c0 + n], PS[:, b, 0:n], 0.125, EBc[:, 0:n], ALU.mult, ALU.mult),
                         reads=[rPS[bq], rEBc], writes=[rQT])
                    bk = bank()
                    for k in range(8):
                        P.op("pe", lambda e, b=bk, k=k, m=m, c0=c0, n=n: e.matmul(PS[:, b, 0:n], Wqk[:, k, 256 + m * 128:256 + (m + 1) * 128], H[:, k, c0:c0 + n], start=(k == 0), stop=(k == 7)),
                             reads=[rWqk, rH[ci]], writes=[rPS[bk]])
                    P.op("dve", lambda e, b=bk, n=n: e.tensor_tensor(KTf[:, 0:n], PS[:, b, 0:n], ENBc[:, 0:n], ALU.mult), reads=[rPS[bk], rENBc], writes=[rKTf])
                    P.op("act", lambda e, m=m, c0=c0, n=n: e.copy(KT[:, m, c0:c0 + n], KTf[:, 0:n]), reads=[rKTf], writes=[rKT])
                    cs = 64 if c0 < T else 8
                    cb = c0 // 64 if c0 < T else 32
                    P.op("dve", lambda e, m=m, c0=c0, n=n, cs=cs, cb=cb: e.tensor_tensor(
                        KLT[:, m, c0:c0 + n].rearrange("p (c s) -> p c s", s=cs), KTf[:, 0:n].rearrange("p (c s) -> p c s", s=cs),
                        EBL[:, m, cb:cb + n // cs].unsqueeze(2).broadcast_to([128, n // cs, cs]), ALU.mult), reads=[rKTf, rEBL], writes=[rKLT])
            P.fence()
            import os
            KGLA = int(os.environ.get("KGLA", "9"))
            if KGLA < 2:
                return
            CV.off = ph_b
            KLtok = CV.take([128, 17, 256], BF16); rKLtok = Reg("KLtok")
            VT = CV.take([128, 17, 512], BF16); rVT = Reg("VT")
            Sf = [CV.take([128, 128], F32) for _ in range(2)]; rSf = [Reg("Sf0"), Reg("Sf1")]
            ATT = [CV.take([128, 512], BF16) for _ in range(2)]; rATT = [Reg("att0"), Reg("att1")]
            OF = CV.take([128, 512], F32); rOF = Reg("OF")
            ATS = CV.take([32, 32], BF16); rATS = Reg("ATS")
            KLm = CV.take([32, 4, 256], BF16); rKLm = Reg("KLm")
            SQg = CV.take([128, 512], BF16); rSQg = Reg("SQg")
            RSg = CV.take([128, 512], F32); rRSg = Reg("RSg")
            T2 = RSg; rT2 = rRSg
            SR = CV.take([128, 512], F32); rSR = Reg("SR")
            SFs = CV.take([128, 4, 128], F32); rSFs = Reg("SFs")
            Wv, rWv = wload([(wsrc(w_in, l, 8, 3584, 512), 0, 512)], 8, 512)
            for tb in range(17):
                ntok = 128 if tb < 16 else TS
                b = bank()
                for k in range(8):
                    P.op("pe", lambda e, b=b, k=k, tb=tb, ntok=ntok: e.matmul(PS[0:ntok, b, :], H[:, k, tb * 128:tb * 128 + ntok], Wv[:, k, :], start=(k == 0), stop=(k == 7)),
                         reads=[rWv, rH[tb // 4 if tb < 16 else 4]], writes=[rPS[b]])
                P.op("act", lambda e, b=b, tb=tb, ntok=ntok: e.copy(VT[0:ntok, tb, :], PS[0:ntok, b, :]), reads=[rPS[b]], writes=[rVT])
            for tb in range(17):
                ntok = 128 if tb < 16 else TS
                b = bank()
                PSB = PS[:, b, :].bitcast(BF16)
                for m in range(2):
                    P.op("pe", lambda e, PSB=PSB, m=m, tb=tb, ntok=ntok: e.transpose(PSB[0:ntok, m * 128:(m + 1) * 128], KLT[:, m, tb * 128:tb * 128 + ntok], identb[:, :]),
                         reads=[rKLT, ridentb], writes=[rPS[b]])
                P.op("act", lambda e, PSB=PSB, tb=tb, ntok=ntok: e.copy(KLtok[0:ntok, tb, :], PSB[0:ntok, 0:256]), reads=[rPS[b]], writes=[rKLtok])
            for bq in range(4):
                P.op("dve", lambda e, bq=bq: e.tensor_scalar(KLm[:, bq, :], KLtok[0:32, 16, :], selm[:, bq:bq + 1], None, ALU.mult), reads=[rKLtok, rselm], writes=[rKLm])
            if KGLA < 3:
                return
            Wr, rWr = wload([(wsrc(w_in, l, 8, 4096, 512), 0, 512)], 8, 512)
            for m in range(2):
                cur = 0
                P.op("dve", lambda e: e.memset(Sf[0][:], 0.0), writes=[rSf[0]])
                for c in range(32):
                    tb, par = c // 2, c % 2
                    tp = slice(par * 64, par * 64 + 64)
                    b = bank()
                    for hh in range(2):
                        hd = 2 * m + hh
                        P.op("pe", lambda e, b=b, hh=hh, hd=hd, tb=tb, tp=tp, par=par: e.matmul(
                            PS[hh * 64:(hh + 1) * 64, b, 0:128], KLtok[tp, tb, hd * 64:(hd + 1) * 64], VT[tp, tb, hd * 128:(hd + 1) * 128],
                            start=True, stop=True, tile_position=(par * 64, hh * 64)),
                            reads=[rKLtok, rVT], writes=[rPS[b]])
                    nxt = 1 - cur
                    P.op("dve", lambda e, b=b, m=m, c=c, cur=cur, nxt=nxt: e.scalar_tensor_tensor(
                        Sf[nxt][:], Sf[cur][:], EBL[:, m, c:c + 1], PS[:, b, 0:128], ALU.mult, ALU.add),
                        reads=[rSf[cur], rEBL, rPS[b]], writes=[rSf[nxt]])
                    if c < 31:
                        P.op("act", lambda e, c=c, nxt=nxt: e.copy(SP[:, c, :], Sf[nxt][:]), reads=[rSf[nxt]], writes=[rSP])
                    cur = nxt
                P.dma("sp", lambda e, m=m, cur=cur: e.dma_start(out=gp[l, 2 * m:2 * m + 2].rearrange("h k v -> (h k) v"), in_=Sf[cur][:]), reads=[rSf[cur]])
                b = bank()
                for bq in range(4):
                    for hh in range(2):
                        hd = 2 * m + hh
                        P.op("pe", lambda e, b=b, bq=bq, hh=hh, hd=hd: e.matmul(
                            PS[hh * 64:(hh + 1) * 64, b, bq * 128:(bq + 1) * 128], KLm[:, bq, hd * 64:(hd + 1) * 64], VT[0:32, 16, hd * 128:(hd + 1) * 128],
                            start=True, stop=True, tile_position=(0, hh * 64)),
                            reads=[rKLm, rVT], writes=[rPS[b]])
                for bq in range(4):
                    P.op("dve", lambda e, b=b, bq=bq, m=m: e.scalar_tensor_tensor(
                        SFs[:, bq, :], S0f[:, m, bq, :], EBL[:, m, 32 + bq:33 + bq], PS[:, b, bq * 128:(bq + 1) * 128], ALU.mult, ALU.add),
                        reads=[rS0f, rEBL, rPS[b]], writes=[rSFs])
                for bq in range(4):
                    P.dma("sp", lambda e, m=m, bq=bq: e.dma_start(out=gs[l, bq, 2 * m:2 * m + 2].rearrange("h k v -> (h k) v"), in_=SFs[:, bq, :]), reads=[rSFs])
                for hh in range(2 if KGLA >= 4 else 0):
                    hd = 2 * m + hh
                    hs = slice(hh * 64, hh * 64 + 64)
                    for ci, (c0, n) in enumerate(CH):
                        bo = bank()
                        if c0 < T:
                            ba = bank()
                            ai = ci % 2
                            for jb in range(4):
                                bc = slice(c0 + jb * 128, c0 + jb * 128 + 128)
                                P.op("pe", lambda e, b=ba, jb=jb, bc=bc, m=m, hs=hs, hh=hh: e.matmul(
                                    PS[:, b, jb * 128:(jb + 1) * 128], KT[hs, m, bc], QT[hs, m, bc], start=True, stop=True, tile_position=(hh * 64, 0)),
                                    reads=[rKT, rQT], writes=[rPS[ba]])
                            P.op("dve", lambda e, b=ba, ai=ai: e.tensor_tensor(
                                ATT[ai][:].rearrange("p (j c) -> p j c", c=128), PS[:, b, :].rearrange("p (j c) -> p j c", c=128),
                                cmask[:].unsqueeze(1).broadcast_to([128, 4, 128]), ALU.mult), reads=[rPS[ba], rcmask], writes=[rATT[ai]])
                            for jb in range(4):
                                tb = c0 // 128 + jb
                                P.op("pe", lambda e, b=bo, jb=jb, tb=tb, hd=hd, ai=ai: e.matmul(
                                    PS[:, b, jb * 128:(jb + 1) * 128], VT[:, tb, hd * 128:(hd + 1) * 128], ATT[ai][:, jb * 128:(jb + 1) * 128], start=True, stop=False),
                                    reads=[rVT, rATT[ai]], writes=[rPS[bo]])
                                for par in range(2):
                                    c = 2 * tb + par
                                    if c == 0:
                                        continue
                                    cc = slice(c * 64, c * 64 + 64)
                                    P.op("pe", lambda e, b=bo, jb=jb, par=par, hs=hs, c=c, m=m, cc=cc, hh=hh: e.matmul(
                                        PS[:, b, jb * 128 + par * 64:jb * 128 + par * 64 + 64], SP[hs, c - 1, :], QT[hs, m, cc], start=False, stop=True, tile_position=(hh * 64, 0)),
                                        reads=[rSP, rQT], writes=[rPS[bo]])
                            P.op("act", lambda e, b=bo, n=n: e.copy(OF[:, 0:n], PS[:, b, 0:n]), reads=[rPS[bo]], writes=[rOF])
                        else:
                            ba = bank()
                            P.op("pe", lambda e, b=ba, m=m, hs=hs, hh=hh: e.matmul(PS[0:32, b, 0:32], KT[hs, m, T:NT], QT[hs, m, T:NT], start=True, stop=True,
                                                                                 tile_position=(hh * 64, 0)),
                                 reads=[rKT, rQT], writes=[rPS[ba]])
                            P.op("dve", lambda e, b=ba: e.tensor_tensor(ATS[:], PS[0:32, b, 0:32], smask[:], ALU.mult), reads=[rPS[ba], rsmask], writes=[rATS])
                            P.op("pe", lambda e, b=bo, hd=hd: e.matmul(PS[:, b, 0:32], VT[0:32, 16, hd * 128:(hd + 1) * 128], ATS[:], start=True, stop=True),
                                 reads=[rVT, rATS], writes=[rPS[bo]])
                            b2 = bank()
                            for bq in range(4):
                                P.op("pe", lambda e, b=b2, bq=bq, hs=hs, m=m, hh=hh: e.matmul(
                                    PS[:, b, bq * 8:(bq + 1) * 8], S0b[hs, m, bq, :], QT[hs, m, T + bq * 8:T + bq * 8 + 8], start=True, stop=True,
                                    tile_position=(hh * 64, 0)),
                                    reads=[rS0b, rQT], writes=[rPS[b2]])
                            P.op("act", lambda e, b=b2: e.copy(OF[:, 0:32], PS[:, b, 0:32]), reads=[rPS[b2]], writes=[rOF])
                            P.op("dve", lambda e, b=bo: e.tensor_tensor(OF[:, 0:32], OF[:, 0:32], PS[:, b, 0:32], ALU.add), reads=[rPS[bo], rOF], writes=[rOF])
                        if int(os.environ.get("KG4", "9")) < 3:
                            continue
                        P.op("act", lambda e, n=n: e.activation(SQg[:, 0:n], OF[:, 0:n], AF.Square), reads=[rOF], writes=[rSQg])
                        bs_ = bank()
                        P.op("pe", lambda e, b=bs_, n=n: e.matmul(PS[:, b, 0:n], ones[:], SQg[:, 0:n], start=True, stop=True), reads=[rSQg, rones], writes=[rPS[bs_]])
                        P.op("dve", lambda e, b=bs_, n=n: e.tensor_scalar(RSg[:, 0:n], PS[:, b, 0:n], 1.0 / 128, EPS, ALU.mult, ALU.add), reads=[rPS[bs_]], writes=[rRSg])
                        P.op("act", lambda e, n=n: e.activation(RSg[:, 0:n], RSg[:, 0:n], AF.Sqrt), reads=[rRSg], writes=[rRSg])
                        P.op("dve", lambda e, n=n: e.reciprocal(RSg[:, 0:n], RSg[:, 0:n]), reads=[rRSg], writes=[rRSg])
                        P.op("dve", lambda e, n=n: e.scalar_tensor_tensor(T2[:, 0:n], OF[:, 0:n], gnorm[:, l:l + 1], RSg[:, 0:n], ALU.mult, ALU.mult),
                             reads=[rOF, rgnorm, rRSg], writes=[rRSg])
                        br = bank()
                        for k in range(8):
                            P.op("pe", lambda e, b=br, k=k, hd=hd, c0=c0, n=n: e.matmul(PS[:, b, 0:n], Wr[:, k, hd * 128:(hd + 1) * 128], H[:, k, c0:c0 + n], start=(k == 0), stop=(k == 7)),
                                 reads=[rWr, rH[ci]], writes=[rPS[br]])
                        P.op("act", lambda e, b=br, n=n: e.activation(SR[:, 0:n], PS[:, b, 0:n], AF.Silu), reads=[rPS[br]], writes=[rSR])
                        P.op("dve", lambda e, hd=hd, c0=c0, n=n: e.tensor_tensor(OC[:, hd, c0:c0 + n], T2[:, 0:n], SR[:, 0:n], ALU.mult), reads=[rT2, rSR], writes=[rOC])
            P.fence()
            CV.off = mark
            if KGLA < 5:
                return
            branch_out(l, OC, rOC, w_o_c, 6672)
            P.fence()

        PI = math.pi

        def lam_tables(eng_d, A_re, A_im, LDT, shp, tk, rT):
            dt = tk(shp); ar = tk(shp); ai = tk(shp); mag = tk(shp); sn = tk(shp); cs = tk(shp); lr = tk(shp); li = tk(shp)
            P.op("act", lambda e: e.activation(dt, LDT, AF.Exp), reads=[rT], writes=[rT])
            P.op("dve", lambda e: e.tensor_tensor(ar, A_re, dt, ALU.mult), reads=[rT], writes=[rT])
            P.op("dve", lambda e: e.tensor_tensor(ai, A_im, dt, ALU.mult), reads=[rT], writes=[rT])
            P.op("act", lambda e: e.activation(mag, ar, AF.Exp), reads=[rT], writes=[rT])
            tmp = tk(shp)
            for dstt, shift in ((sn, 0.0), (cs, 0.5 * PI)):
                P.op("dve", lambda e, dstt=dstt, shift=shift: e.tensor_scalar(dstt, ai, shift, None, ALU.add), reads=[rT], writes=[rT])
                for jth in range(5):
                    thr = shift - (2 * jth + 1) * PI
                    P.op("dve", lambda e, thr=thr: e.tensor_scalar(tmp, ai, thr, 1e9, ALU.add, ALU.mult), reads=[rT], writes=[rT])
                    P.op("dve", lambda e: e.tensor_scalar(tmp, tmp, 0.0, 1.0, ALU.max, ALU.min), reads=[rT], writes=[rT])
                    P.op("dve", lambda e, dstt=dstt: e.scalar_tensor_tensor(dstt, tmp, -2 * PI, dstt, ALU.mult, ALU.add), reads=[rT], writes=[rT])
                P.op("act", lambda e, dstt=dstt: e.activation(dstt, dstt, AF.Sin), reads=[rT], writes=[rT])
            P.op("dve", lambda e: e.tensor_tensor(lr, mag, cs, ALU.mult), reads=[rT], writes=[rT])
            P.op("dve", lambda e: e.tensor_tensor(li, mag, sn, ALU.mult), reads=[rT], writes=[rT])
            return lr, li, cs, sn, mag

        def s5(l):
            CV.reset()
            YB = CV.take([128, 4, NT], BF16); rYB = Reg("YB")
            mark = CV.off
            UB = CV.take([128, 4, NT], BF16); rUB = Reg("UB")
            YF = CV.take([128, NT], BF16); rYF = Reg("YF")
            Wt0 = CV.take([128, 4, 2, 128], BF16); rWt0 = Reg("Wt0")
            Ct0 = CV.take([128, 16, 2, 32], BF16); rCt0 = Reg("Ct0")
            LPr = CV.take([128, 16, 11], F32); LPi = CV.take([128, 16, 11], F32); nLPi = CV.take([128, 16, 11], F32); rLP = Reg("LP")
            LAM = CV.take([128, 3, 16], F32)
            CW = CV.take([128, 2], F32); rCW = Reg("CW")
            BSr = CV.take([128, 16, 32], F32); BSi = CV.take([128, 16, 32], F32); rBS = Reg("BS")
            XSr = CV.take([128, 16, 32], BF16); XSi = CV.take([128, 16, 32], BF16); rXS = Reg("XS")
            X0r = CV.take([128, 16, 4], F32); X0i = CV.take([128, 16, 4], F32); rX0 = Reg("X0")
            EFr = CV.take([128, 16], F32); EFi = CV.take([128, 16], F32); rEF = Reg("EF")
            TS_ = [CV.take([128, 16, 4], F32) for _ in range(2)]; rTS = Reg("TSs")
            XCb = [CV.take([128, 512], BF16) for _ in range(2)]; rXCb = [Reg("xcb0"), Reg("xcb1")]
            G1 = CV.take([128, 512], F32); rG1 = Reg("G1")
            G2 = CV.take([128, 512], F32); rG2 = Reg("G2")
            xoff = CV.off
            XR = [CV.take([128, T], F32) for _ in range(2)]; XI = [CV.take([128, T], F32) for _ in range(2)]
            rXR = [Reg("xr0"), Reg("xr1")]; rXI = [Reg("xi0"), Reg("xi1")]
            RHOT = CV.take([128, 1024], F32); rRHOT = Reg("RHOT")
            CV.off = xoff
            rT = Reg("s5tmp")
            pA = CV.take([128, 5, 4, 64], F32)
            pB = CV.take([128, 3, 16], F32)
            cB = CV.take([128, 2, 16, 16], F32)
            P.dma("sp", lambda e: e.dma_start(out=pA, in_=aA_d[:, l]), writes=[rT])
            P.dma("sp", lambda e: e.dma_start(out=pB, in_=aB_d[:, l]), writes=[rT])
            P.dma("sp", lambda e: e.dma_start(out=cB, in_=cB_d[:, l]), writes=[rT])
            P.dma("sp", lambda e: e.dma_start(out=X0r[:], in_=x0r_d[:, l]), writes=[rX0])
            P.dma("sp", lambda e: e.dma_start(out=X0i[:], in_=x0i_d[:, l]), writes=[rX0])
            tkB = lambda shp: CV.take(shp, F32)
            lrB, liB, csB, snB, magB = lam_tables("dve", pB[:, 0, :], pB[:, 1, :], pB[:, 2, :], [128, 16], tkB, rT)
            P.op("act", lambda e: e.copy(LPr[:, :, 0], csB), reads=[rT], writes=[rLP])
            P.op("act", lambda e: e.copy(LPi[:, :, 0], snB), reads=[rT], writes=[rLP])
            P.op("act", lambda e: e.copy(LAM[:, 0, :], lrB), reads=[rT], writes=[rLP])
            P.op("act", lambda e: e.copy(LAM[:, 1, :], liB), reads=[rT], writes=[rLP])
            P.op("act", lambda e: e.copy(LAM[:, 2, :], magB), reads=[rT], writes=[rLP])
            t1 = CV.take([128, 16], F32); t2 = CV.take([128, 16], F32)

            def unit_norm(k):
                P.op("dve", lambda e, k=k: e.tensor_tensor(t1, LPr[:, :, k], LPr[:, :, k], ALU.mult), reads=[rLP], writes=[rT])
                P.op("dve", lambda e, k=k: e.tensor_tensor(t2, LPi[:, :, k], LPi[:, :, k], ALU.mult), reads=[rLP], writes=[rT])
                P.op("dve", lambda e: e.tensor_tensor(t1, t1, t2, ALU.add), reads=[rT], writes=[rT])
                P.op("act", lambda e: e.activation(t1, t1, AF.Sqrt), reads=[rT], writes=[rT])
                P.op("dve", lambda e: e.reciprocal(t1, t1), reads=[rT], writes=[rT])
                P.op("dve", lambda e, k=k: e.tensor_tensor(LPr[:, :, k], LPr[:, :, k], t1, ALU.mult), reads=[rT, rLP], writes=[rLP])
                P.op("dve", lambda e, k=k: e.tensor_tensor(LPi[:, :, k], LPi[:, :, k], t1, ALU.mult), reads=[rT, rLP], writes=[rLP])
            unit_norm(0)
            for k in range(10):
                P.op("dve", lambda e, k=k: e.tensor_tensor(t1, LPr[:, :, k], LPr[:, :, k], ALU.mult), reads=[rLP], writes=[rT])
                P.op("dve", lambda e, k=k: e.tensor_tensor(t2, LPi[:, :, k], LPi[:, :, k], ALU.mult), reads=[rLP], writes=[rT])
                P.op("dve", lambda e, k=k: e.tensor_tensor(LPr[:, :, k + 1], t1, t2, ALU.subtract), reads=[rT], writes=[rLP])
                P.op("dve", lambda e, k=k: e.tensor_tensor(t1, LPr[:, :, k], LPi[:, :, k], ALU.mult), reads=[rLP], writes=[rT])
                P.op("dve", lambda e, k=k: e.tensor_scalar(LPi[:, :, k + 1], t1, 2.0, None, ALU.mult), reads=[rT], writes=[rLP])
                unit_norm(k + 1)
            P.op("dve", lambda e: e.tensor_scalar(nLPi[:], LPi[:], -1.0, None, ALU.mult), reads=[rLP], writes=[rLP])
            for c_ in range(2):
                sgn = 1.0 if c_ == 0 else -1.0
                for g2 in range(2):
                    P.op("dve", lambda e, c_=c_, g2=g2, sgn=sgn: e.tensor_scalar(
                        Ct0[:, :, c_, g2 * 16:(g2 + 1) * 16], cB[:, c_, :, :], mk[:, 2 + g2:3 + g2], sgn, ALU.mult, ALU.mult),
                        reads=[rT, rmk], writes=[rCt0])
            shpA = [128, 4, 64]
            tkA = lambda shp: CV.take(shp, F32)
            lrA, liA, _c, _s, _m = lam_tables("dve", pA[:, 0], pA[:, 1], pA[:, 2], shpA, tkA, rT)
            den = tkA(shpA); nr = tkA(shpA); cor = tkA(shpA); coi = tkA(shpA); u1 = tkA(shpA); u2 = tkA(shpA)
            are, aim, bre, bim = pA[:, 0], pA[:, 1], pA[:, 3], pA[:, 4]
            def dv(fn):
                P.op("dve", fn, reads=[rT], writes=[rT])
            dv(lambda e: e.tensor_tensor(den, are, are, ALU.mult))
            dv(lambda e: e.tensor_tensor(u1, aim, aim, ALU.mult))
            dv(lambda e: e.tensor_tensor(den, den, u1, ALU.add))
            dv(lambda e: e.reciprocal(den, den))
            dv(lambda e: e.tensor_scalar(nr, lrA, -1.0, None, ALU.add))
            dv(lambda e: e.tensor_tensor(u1, nr, are, ALU.mult))
            dv(lambda e: e.tensor_tensor(u2, liA, aim, ALU.mult))
            dv(lambda e: e.tensor_tensor(u1, u1, u2, ALU.add))
            dv(lambda e: e.tensor_tensor(cor, u1, den, ALU.mult))
            dv(lambda e: e.tensor_tensor(u1, liA, are, ALU.mult))
            dv(lambda e: e.tensor_tensor(u2, nr, aim, ALU.mult))
            dv(lambda e: e.tensor_tensor(u1, u1, u2, ALU.subtract))
            dv(lambda e: e.tensor_tensor(coi, u1, den, ALU.mult))
            bbr = tkA(shpA); bbi = tkA(shpA)
            dv(lambda e: e.tensor_tensor(u1, cor, bre, ALU.mult))
            dv(lambda e: e.tensor_tensor(u2, coi, bim, ALU.mult))
            dv(lambda e: e.tensor_tensor(bbr, u1, u2, ALU.subtract))
            dv(lambda e: e.tensor_tensor(u1, cor, bim, ALU.mult))
            dv(lambda e: e.tensor_tensor(u2, coi, bre, ALU.mult))
            dv(lambda e: e.tensor_tensor(bbi, u1, u2, ALU.add))
            for c_, bb in enumerate((bbr, bbi)):
                for g2 in range(2):
                    P.op("dve", lambda e, c_=c_, g2=g2, bb=bb: e.tensor_scalar(
                        Wt0[:, :, c_, g2 * 64:(g2 + 1) * 64], bb, mk[:, g2:g2 + 1], None, ALU.mult), reads=[rT, rmk], writes=[rWt0])
            P.fence()
            import os
            KS5 = int(os.environ.get("KS5", "9"))
            if KS5 < 2:
                return
            Wu, rWu = wload([(wsrc(w_in, l, 8, 2560, 512), 0, 512)], 8, 512)
            for j in range(4):
                for ci, (c0, n) in enumerate(CH):
                    b = bank()
                    for k in range(8):
                        P.op("pe", lambda e, b=b, k=k, j=j, c0=c0, n=n: e.matmul(PS[:, b, 0:n], Wu[:, k, j * 128:(j + 1) * 128], H[:, k, c0:c0 + n], start=(k == 0), stop=(k == 7)),
                             reads=[rWu, rH[ci]], writes=[rPS[b]])
                    P.op("act", lambda e, b=b, j=j, c0=c0, n=n: e.copy(UB[:, j, c0:c0 + n], PS[:, b, 0:n]), reads=[rPS[b]], writes=[rUB])
            for q in range(16):
                j, qq = q // 4, q % 4
                rows = slice(32 * qq, 32 * qq + 32)
                b = bank()
                for c_ in range(2):
                    P.op("pe", lambda e, b=b, c_=c_, j=j, rows=rows, qq=qq: e.matmul(
                        PS[:, b, c_ * 32:(c_ + 1) * 32], Wt0[rows, j, c_, :], UB[rows, j, T:NT], start=True, stop=True, tile_position=(32 * qq, 0)),
                        reads=[rWt0, rUB], writes=[rPS[b]])
                P.op("act", lambda e, b=b, q=q: e.copy(BSr[:, q, :], PS[:, b, 0:32]), reads=[rPS[b]], writes=[rBS])
                P.op("act", lambda e, b=b, q=q: e.copy(BSi[:, q, :], PS[:, b, 32:64]), reads=[rPS[b]], writes=[rBS])
            BSr4 = BSr[:].rearrange("p q (b t) -> p q b t", t=8)
            BSi4 = BSi[:].rearrange("p q (b t) -> p q b t", t=8)
            lamr = LAM[:, 0, :].unsqueeze(2).broadcast_to([128, 16, 4]); lami = LAM[:, 1, :].unsqueeze(2).broadcast_to([128, 16, 4])
            for t in range(8):
                pr = X0r[:] if t == 0 else BSr4[:, :, :, t - 1]
                pi_ = X0i[:] if t == 0 else BSi4[:, :, :, t - 1]
                rd = [rBS, rX0, rLP]
                P.op("dve", lambda e, pr=pr: e.tensor_tensor(TS_[0][:], pr, lamr, ALU.mult), reads=rd, writes=[rTS])
                P.op("dve", lambda e, t=t: e.tensor_tensor(BSr4[:, :, :, t], BSr4[:, :, :, t], TS_[0][:], ALU.add), reads=[rTS, rBS], writes=[rBS])
                P.op("dve", lambda e, pi_=pi_: e.tensor_tensor(TS_[1][:], pi_, lami, ALU.mult), reads=rd, writes=[rTS])
                P.op("dve", lambda e, t=t: e.tensor_tensor(BSr4[:, :, :, t], BSr4[:, :, :, t], TS_[1][:], ALU.subtract), reads=[rTS, rBS], writes=[rBS])
                P.op("dve", lambda e, pi_=pi_: e.tensor_tensor(TS_[0][:], pi_, lamr, ALU.mult), reads=rd, writes=[rTS])
                P.op("dve", lambda e, t=t: e.tensor_tensor(BSi4[:, :, :, t], BSi4[:, :, :, t], TS_[0][:], ALU.add), reads=[rTS, rBS], writes=[rBS])
                P.op("dve", lambda e, pr=pr: e.tensor_tensor(TS_[1][:], pr, lami, ALU.mult), reads=rd, writes=[rTS])
                P.op("dve", lambda e, t=t: e.tensor_tensor(BSi4[:, :, :, t], BSi4[:, :, :, t], TS_[1][:], ALU.add), reads=[rTS, rBS], writes=[rBS])
            P.op("act", lambda e: e.copy(XSr[:], BSr[:]), reads=[rBS], writes=[rXS])
            P.op("act", lambda e: e.copy(XSi[:], BSi[:]), reads=[rBS], writes=[rXS])
            P.op("act", lambda e: e.copy(TS_[0][:], BSr4[:, :, :, 7]), reads=[rBS], writes=[rTS])
            P.dma("sp", lambda e: e.dma_start(out=esr_d[l], in_=TS_[0][:].rearrange("p q b -> p (q b)")), reads=[rTS])
            P.op("act", lambda e: e.copy(TS_[1][:], BSi4[:, :, :, 7]), reads=[rBS], writes=[rTS])
            P.dma("sp", lambda e: e.dma_start(out=esi_d[l], in_=TS_[1][:].rearrange("p q b -> p (q b)")), reads=[rTS])
            if KS5 < 3:
                return
            for q in range(16):
                j, qq = q // 4, q % 4
                rows = slice(32 * qq, 32 * qq + 32)
                for ci in range(4):
                    c0 = ci * 512
                    for c_, dst, rdst in ((0, XR[0], rXR[0]), (1, XI[0], rXI[0])):
                        b = bank()
                        P.op("pe", lambda e, b=b, c_=c_, j=j, rows=rows, qq=qq, c0=c0: e.matmul(
                            PS[:, b, :], Wt0[rows, j, c_, :], UB[rows, j, c0:c0 + 512], start=True, stop=True, tile_position=(32 * qq, 0)),
                            reads=[rWt0, rUB], writes=[rPS[b]])
                        P.op("act", lambda e, b=b, dst=dst, c0=c0: e.copy(dst[:, c0:c0 + 512], PS[:, b, :]), reads=[rPS[b]], writes=[rdst])
                HH = 1024
                Ar, Ai, rAr, rAi = XR[0], XI[0], rXR[0], rXI[0]
                Rr, Ri = XR[1][:, 0:HH], XR[1][:, HH:T]
                T1_, T2_ = XI[1][:, 0:HH], XI[1][:, HH:T]
                rR, rTT = rXR[1], rXI[1]
                P.op("dve", lambda e: e.memset(Rr[:, 0:1], 1.0), writes=[rR])
                P.op("dve", lambda e: e.memset(Ri[:, 0:1], 0.0), writes=[rR])
                for k in range(10):
                    m_ = 1 << k
                    P.op("dve", lambda e, m_=m_, q=q, k=k: e.tensor_scalar(Rr[:, m_:2 * m_], Rr[:, 0:m_], LPr[:, q, k:k + 1], None, ALU.mult), reads=[rR, rLP], writes=[rR])
                    P.op("dve", lambda e, m_=m_, q=q, k=k: e.scalar_tensor_tensor(Rr[:, m_:2 * m_], Ri[:, 0:m_], nLPi[:, q, k:k + 1], Rr[:, m_:2 * m_], ALU.mult, ALU.add), reads=[rR, rLP], writes=[rR])
                    P.op("dve", lambda e, m_=m_, q=q, k=k: e.tensor_scalar(Ri[:, m_:2 * m_], Ri[:, 0:m_], LPr[:, q, k:k + 1], None, ALU.mult), reads=[rR, rLP], writes=[rR])
                    P.op("dve", lambda e, m_=m_, q=q, k=k: e.scalar_tensor_tensor(Ri[:, m_:2 * m_], Rr[:, 0:m_], LPi[:, q, k:k + 1], Ri[:, m_:2 * m_], ALU.mult, ALU.add), reads=[rR, rLP], writes=[rR])
                P.op("dve", lambda e, q=q: e.tensor_copy(RHOT[:], LAM[:, 2, q:q + 1].broadcast_to([128, HH])), reads=[rLP], writes=[rRHOT])
                rho = RHOT[:]
                for hf in range(2):
                    cs_ = slice(hf * HH, (hf + 1) * HH)
                    br, bi = Ar[:, cs_], Ai[:, cs_]
                    TT = ALU
                    def tt(out, a, b_, op, rd, wr):
                        P.op("dve", lambda e, out=out, a=a, b_=b_, op=op: e.tensor_tensor(out, a, b_, op), reads=rd, writes=wr)
                    tt(T1_, Rr, br, ALU.mult, [rR, rAr], [rTT])
                    tt(T2_, Ri, bi, ALU.mult, [rR, rAi], [rTT])
                    tt(T1_, T1_, T2_, ALU.add, [rTT], [rTT])
                    tt(T2_, Ri, br, ALU.mult, [rR, rAr], [rTT])
                    tt(br, Rr, bi, ALU.mult, [rR, rAi, rAr], [rAr])
                    tt(br, br, T2_, ALU.subtract, [rAr, rTT], [rAr])
                    if hf == 0:
                        ini_r, ini_i, rdi = 0.0, 0.0, []
                    else:
                        xe_r, xe_i = Ar[:, HH - 1:HH], Ai[:, HH - 1:HH]
                        P.op("dve", lambda e, q=q: e.tensor_scalar(CW[:, 0:1], xe_r, LPr[:, q, 0:1], None, ALU.mult), reads=[rAr, rLP], writes=[rCW])
                        P.op("dve", lambda e, q=q: e.scalar_tensor_tensor(CW[:, 0:1], xe_i, nLPi[:, q, 0:1], CW[:, 0:1], ALU.mult, ALU.add), reads=[rAi, rLP, rCW], writes=[rCW])
                        P.op("dve", lambda e, q=q: e.tensor_scalar(CW[:, 1:2], xe_i, LPr[:, q, 0:1], None, ALU.mult), reads=[rAi, rLP], writes=[rCW])
                        P.op("dve", lambda e, q=q: e.scalar_tensor_tensor(CW[:, 1:2], xe_r, LPi[:, q, 0:1], CW[:, 1:2], ALU.mult, ALU.add), reads=[rAr, rLP, rCW], writes=[rCW])
                        ini_r, ini_i, rdi = CW[:, 0:1], CW[:, 1:2], [rCW]
                    P.op("dve", lambda e, bi=bi, ini_r=ini_r: e.tensor_tensor_scan(bi, rho, T1_, ini_r, ALU.mult, ALU.add), reads=[rTT, rRHOT] + rdi, writes=[rAi])
                    P.op("dve", lambda e, br=br, ini_i=ini_i: e.tensor_tensor_scan(T2_, rho, br, ini_i, ALU.mult, ALU.add), reads=[rAr, rRHOT] + rdi, writes=[rTT])
                    tt(T1_, Rr, bi, ALU.mult, [rR, rAi], [rTT])
                    tt(br, Ri, T2_, ALU.mult, [rR, rTT], [rAr])
                    tt(br, T1_, br, ALU.subtract, [rTT, rAr], [rAr])
                    tt(T1_, Rr, T2_, ALU.mult, [rR, rTT], [rTT])
                    tt(bi, bi, Ri, ALU.mult, [rAi, rR], [rAi])
                    tt(bi, bi, T1_, ALU.add, [rAi, rTT], [rAi])
                cur = 0
                fr, fi, rfr, rfi = XR[cur], XI[cur], rXR[cur], rXI[cur]
                P.op("act", lambda e, fr=fr, q=q: e.copy(EFr[:, q:q + 1], fr[:, T - 1:T]), reads=[rfr], writes=[rEF])
                P.op("act", lambda e, fi=fi, q=q: e.copy(EFi[:, q:q + 1], fi[:, T - 1:T]), reads=[rfi], writes=[rEF])
                if KS5 < 4:
                    continue
                for ci, (c0, n) in enumerate(CH):
                    if c0 < T:
                        P.op("act", lambda e, fr=fr, c0=c0: e.copy(XCb[0][:], fr[:, c0:c0 + 512]), reads=[rfr], writes=[rXCb[0]])
                        P.op("act", lambda e, fi=fi, c0=c0: e.copy(XCb[1][:], fi[:, c0:c0 + 512]), reads=[rfi], writes=[rXCb[1]])
                        r0, r1 = XCb[0][:, 0:n], XCb[1][:, 0:n]
                        rr = [rXCb[0], rXCb[1]]
                    else:
                        r0, r1 = XSr[:, q, :], XSi[:, q, :]
                        rr = [rXS]
                    b = bank()
                    P.op("pe", lambda e, b=b, q=q, qq=qq, n=n, r0=r0, rows=rows: e.matmul(PS[rows, b, 0:n], Ct0[:, q, 0, :], r0, start=True, stop=False, tile_position=(0, 32 * qq)),
                         reads=[rCt0] + rr, writes=[rPS[b]])
                    P.op("pe", lambda e, b=b, q=q, qq=qq, n=n, r1=r1, rows=rows: e.matmul(PS[rows, b, 0:n], Ct0[:, q, 1, :], r1, start=False, stop=True, tile_position=(0, 32 * qq)),
                         reads=[rCt0] + rr, writes=[rPS[b]])
                    P.op("act", lambda e, b=b, rows=rows, c0=c0, n=n: e.copy(YF[rows, c0:c0 + n], PS[rows, b, 0:n]), reads=[rPS[b]], writes=[rYF])
                if qq == 3 and int(os.environ.get("KS5G", "1")):
                    for ci, (c0, n) in enumerate(CH):
                        P.op("act", lambda e, j=j, c0=c0, n=n: e.activation(G1[:, 0:n], UB[:, j, c0:c0 + n], AF.Copy, scale=dA[:, l, j:j + 1]),
                             reads=[rUB, rdA], writes=[rG1])
                        P.op("dve", lambda e, c0=c0, n=n: e.tensor_tensor(G1[:, 0:n], G1[:, 0:n], YF[:, c0:c0 + n], ALU.add),
                             reads=[rG1, rYF], writes=[rG1])
                        P.op("act", lambda e, n=n: e.activation(G2[:, 0:n], G1[:, 0:n], AF.Square), reads=[rG1], writes=[rG2])
                        P.op("dve", lambda e, n=n: e.tensor_scalar(G2[:, 0:n], G2[:, 0:n], 0.044715, 1.0, ALU.mult, ALU.add), reads=[rG2], writes=[rG2])
                        P.op("dve", lambda e, n=n: e.tensor_tensor(G2[:, 0:n], G2[:, 0:n], G1[:, 0:n], ALU.mult), reads=[rG2, rG1], writes=[rG2])
                        P.op("act", lambda e, n=n: e.activation(G2[:, 0:n], G2[:, 0:n], AF.Sigmoid, scale=1.5957691216057308), reads=[rG2], writes=[rG2])
                        P.op("dve", lambda e, j=j, c0=c0, n=n: e.tensor_tensor(YB[:, j, c0:c0 + n], G2[:, 0:n], G1[:, 0:n], ALU.mult), reads=[rG2, rG1], writes=[rYB])
            P.dma("sp", lambda e: e.dma_start(out=efr_d[l], in_=EFr[:]), reads=[rEF])
            P.dma("sp", lambda e: e.dma_start(out=efi_d[l], in_=EFi[:]), reads=[rEF])
            P.fence()
            CV.off = mark
            if KS5 < 5:
                return
            branch_out(l, YB, rYB, w_glu, 5648, glu=True)
            P.fence()

        def cache_copy(l, part):
            for bq in range(4):
                for (src_d, dst_d) in ((ck, kws), (cv, vws)):
                    r0 = 8 + part * 510
                    P.dma("sp", lambda e, src_d=src_d, dst_d=dst_d, bq=bq, r0=r0, l=l: e.dma_start(
                        out=dst_d[l, bq, r0 - 8:r0 - 8 + 510, :], in_=src_d[l, bq, r0:r0 + 510, :]), nofence=True)

        for l in range(n_layers):
            CV.reset()
            SQ = CV.take([128, 8, 512], BF16); rSQ = Reg("sq")
            RS = CV.take([128, 512], F32); rRS = Reg("rs")
            rmsnorm_to_H(2 * l, SQ, rSQ, RS, rRS)
            P.fence()
            import os
            cache_copy(l, 0)
            if os.environ.get("KSKIP_ATT") != "1":
                attention(l)
            cache_copy(l, 1)
            if os.environ.get("KSKIP_S5") != "1":
                s5(l)
            cache_copy(l, 2)
            if os.environ.get("KSKIP_GLA") != "1":
                gla(l)
            cache_copy(l, 3)
            if os.environ.get("KSKIP_MLP") != "1":
                mlp(l)

        import os
        if os.environ.get("KSTAGE") == "1":
            P.dma("sp", lambda e: e.dma_start(out=yp.rearrange("(p a) d -> p (a d)", p=128).rearrange("p (k t) -> p k t", t=2048), in_=X[:, :, 0:2048]),
                  reads=[rX[k][c] for k in range(8) for c in range(5)])
            P.emit()
            return nc
        CV.reset()
        SQ = CV.take([128, 8, 512], BF16); rSQ = Reg("sq")
        RS = CV.take([128, 512], F32); rRS = Reg("rs")
        HF = CV.take([128, 8, 512], F32); rHF = Reg("HF")
        YT = [CV.take([128, 1024], F32) for _ in range(2)]; rYT = [Reg("yt0"), Reg("yt1")]
        for ci, (c0, n) in enumerate(CH):
            P.op("act", lambda e, c0=c0, n=n: e.activation(SQ[:, :, 0:n], X[:, :, c0:c0 + n], AF.Square),
                 reads=[rX[k][ci] for k in range(8)], writes=[rSQ])
            b = bank()
            for k in range(8):
                P.op("pe", lambda e, b=b, k=k, n=n: e.matmul(PS[:, b, 0:n], ones[:], SQ[:, k, 0:n], start=(k == 0), stop=(k == 7)),
                     reads=[rSQ, rones], writes=[rPS[b]])
            P.op("dve", lambda e, b=b, n=n: e.tensor_scalar(RS[:, 0:n], PS[:, b, 0:n], 1.0 / D, EPS, ALU.mult, ALU.add), reads=[rPS[b]], writes=[rRS])
            P.op("act", lambda e, n=n: e.activation(RS[:, 0:n], RS[:, 0:n], AF.Sqrt), reads=[rRS], writes=[rRS])
            P.op("dve", lambda e, n=n: e.reciprocal(RS[:, 0:n], RS[:, 0:n]), reads=[rRS], writes=[rRS])
            for k in range(8):
                P.op("dve", lambda e, k=k, c0=c0, n=n: e.scalar_tensor_tensor(
                    HF[:, k, 0:n], X[:, k, c0:c0 + n], gains[:, 8, k:k + 1], RS[:, 0:n], ALU.mult, ALU.mult),
                    reads=[rX[k][ci], rRS, rgains], writes=[rHF])
            for tq in range((n + 127) // 128):
                ntok = min(128, n - tq * 128)
                i = tq % 2
                for half in range(2):
                    b = bank()
                    for kk in range(4):
                        k = half * 4 + kk
                        P.op("pe", lambda e, b=b, kk=kk, k=k, tq=tq, ntok=ntok: e.transpose(
                            PS[0:ntok, b, kk * 128:(kk + 1) * 128], HF[:, k, tq * 128:tq * 128 + ntok], ident[:, :]),
                            reads=[rHF, rident], writes=[rPS[b]])
                    P.op("act", lambda e, b=b, half=half, i=i, ntok=ntok: e.copy(YT[i][0:ntok, half * 512:(half + 1) * 512], PS[0:ntok, b, :]),
                         reads=[rPS[b]], writes=[rYT[i]])
                t0 = c0 + tq * 128
                dst = yp[t0:t0 + ntok, :] if c0 < T else ys
                P.dma("sp", lambda e, i=i, ntok=ntok, dst=dst: e.dma_start(out=dst, in_=YT[i][0:ntok, :]), reads=[rYT[i]])
        P.emit()
    return nc


_NC = {}


def kernel(**inp):
    f = lambda a: np.ascontiguousarray(np.asarray(a, dtype=np.float32))
    if "nc" not in _NC:
        _NC["nc"] = build()
    nc = _NC["nc"]
    gains = np.zeros((128, 9, 8), np.float32)
    for l in range(L_):
        gains[:, 2 * l, :] = f(inp["norm_mix"])[l].reshape(8, 128).T
        gains[:, 2 * l + 1, :] = f(inp["norm_mlp"])[l].reshape(8, 128).T
    gains[:, 8, :] = f(inp["norm_final"]).reshape(8, 128).T
    rel_bias = f(inp["rel_bias"])
    biasT = np.full((24, 128, 256), -1e30, np.float32)
    kk = np.arange(128)[:, None]
    qq = np.arange(256)[None, :]
    step = qq - kk
    valid = (step >= 0) & (step <= 128)
    for g in range(3):
        bk = _t5_bucket(np.arange(129) * PAT[g][1])
        for h in range(8):
            vals = rel_bias[bk, g * 8 + h]
            biasT[g * 8 + h] = np.where(valid, vals[np.clip(step, 0, 128)], np.float32(-1e30))
    a_re, a_im, ldt = f(inp["s5_a_re"]), f(inp["s5_a_im"]), f(inp["s5_log_dt"])
    b_re, b_im, c_re, c_im = f(inp["s5_b_re"]), f(inp["s5_b_im"]), f(inp["s5_c_re"]), f(inp["s5_c_im"])
    aB = np.zeros((128, L_, 3, 16), np.float32)
    cB = np.zeros((128, L_, 2, 16, 16), np.float32)
    aA = np.zeros((128, L_, 5, 4, 64), np.float32)
    for l in range(L_):
        for g2 in range(2):
            grp = np.arange(16) * 2 + g2
            aB[g2 * 64:(g2 + 1) * 64, l, 0, :] = a_re[l][grp].T
            aB[g2 * 64:(g2 + 1) * 64, l, 1, :] = a_im[l][grp].T
            aB[g2 * 64:(g2 + 1) * 64, l, 2, :] = ldt[l][grp][None, :]
            cB[g2 * 64:(g2 + 1) * 64, l, 0] = c_re[l][grp].transpose(2, 0, 1)
            cB[g2 * 64:(g2 + 1) * 64, l, 1] = c_im[l][grp].transpose(2, 0, 1)
        ar4 = a_re[l].reshape(4, 8, 64); ai4 = a_im[l].reshape(4, 8, 64); ld4 = ldt[l].reshape(4, 8)
        br4 = b_re[l].reshape(4, 8, 64, 16); bi4 = b_im[l].reshape(4, 8, 64, 16)
        aA[:, l, 0] = np.repeat(ar4.transpose(1, 0, 2), 16, axis=0)
        aA[:, l, 1] = np.repeat(ai4.transpose(1, 0, 2), 16, axis=0)
        aA[:, l, 2] = np.repeat(np.broadcast_to(ld4.T[:, :, None], (8, 4, 64)), 16, axis=0)
        aA[:, l, 3] = br4.transpose(1, 3, 0, 2).reshape(128, 4, 64)
        aA[:, l, 4] = bi4.transpose(1, 3, 0, 2).reshape(128, 4, 64)
    dA = np.ascontiguousarray(f(inp["s5_d"]).reshape(L_, 4, 128).transpose(2, 0, 1))
    pidx = np.arange(128)
    mk = np.stack([((pidx // 16) % 2 == 0), ((pidx // 16) % 2 == 1), pidx < 64, pidx >= 64], axis=1).astype(np.float32)
    sre = f(inp["state_ssm_re"]); sim = f(inp["state_ssm_im"])
    nbfg = np.ascontiguousarray(-f(inp["b_fg"]).reshape(L_, 2, 128).transpose(2, 0, 1))
    gnorm = np.ascontiguousarray(f(inp["gla_norm"]).T)
    pp = np.arange(128)[:, None]
    cq = np.arange(128)[None, :]
    cmask = ((cq >= pp) & (cq // 64 == pp // 64)).astype(np.float32)
    si = np.arange(32)
    smask = ((si[:, None] // 8 == si[None, :] // 8) & (si[None, :] >= si[:, None])).astype(np.float32)
    selm = (si[:, None] // 8 == np.arange(4)[None, :]).astype(np.float32)
    sg = f(inp["state_gla"])
    sbias = np.full((8, 128, 10, 3, 8), -1e30, np.float32)
    sbiasn = np.full((8, 32, 4, 3, 8), -1e30, np.float32)
    pp_ = np.arange(128)
    rows = np.zeros((10, 128), np.int64)
    for blk in range(4):
        rows[blk] = 1536 + 128 * blk + pp_
    for u in range(6):
        rows[4 + u] = 256 * u + 16 * (pp_ // 8) + (pp_ % 8)
    tt = np.arange(8)
    for g in range(3):
        dd = PAT[g][1]
        dist = 2048 + tt[None, None, :] - rows[:, :, None]
        ok = (dist % dd == 0) & (dist // dd <= 128) & (dist >= 0)
        bkt = _t5_bucket(np.where(ok, dist, 0))
        for h in range(8):
            vals = rel_bias[bkt, g * 8 + h]
            sbias[h, :, :, g, :] = np.where(ok, vals, np.float32(-1e30)).transpose(1, 0, 2)
        kb_, kt_ = np.arange(32) // 8, np.arange(32) % 8
        for bq in range(4):
            dn = tt[None, :] - kt_[:, None]
            okn = (kb_[:, None] == bq) & (dn >= 0) & (dn % dd == 0)
            bktn = _t5_bucket(np.where(okn, dn, 0))
            for h in range(8):
                sbiasn[h, :, bq, g, :] = np.where(okn, rel_bias[bktn, g * 8 + h], np.float32(-1e30))
    sbias = np.ascontiguousarray(sbias.reshape(8, 128, 240))
    sbiasn = np.ascontiguousarray(sbiasn.reshape(8, 32, 96))
    shared = dict(sbias=sbias, sbiasn=sbiasn, aB=aB, cB=cB, aA=aA, dA=dA, mk=mk, wfg2=f(inp["w_fg2"]), nbfg=nbfg, gnorm=gnorm, cmask=cmask, smask=smask, selm=selm, w_in=f(inp["w_in"]), w_o_a=f(inp["w_o_a"]), w_glu=f(inp["w_glu"]), w_o_c=f(inp["w_o_c"]),
                  w_out=f(inp["w_out"]), w_up=f(inp["w_up"]), w_down=f(inp["w_down"]),
                  ident=np.eye(128, dtype=np.float32), gains=gains, biasT=biasT)
    xp = f(inp["x_prompt"]); xs = f(inp["x_sample"])
    ck = f(inp["cache_k_win"]); cv = f(inp["cache_v_win"])
    in_maps = []
    for c in range(8):
        m = dict(shared)
        m["xp"] = xp[c]
        m["xs"] = xs[4 * c:4 * c + 4].reshape(TS, D)
        m["ck"] = ck[:, 4 * c:4 * c + 4].reshape(L_, 4, 2048, 512)
        m["cv"] = cv[:, 4 * c:4 * c + 4].reshape(L_, 4, 2048, 512)
        m["sg"] = np.ascontiguousarray(sg[:, 4 * c:4 * c + 4])
        m["x0r"] = np.ascontiguousarray(sre[:, 4 * c:4 * c + 4].reshape(L_, 4, 16, 2, 64).transpose(3, 4, 0, 2, 1).reshape(128, L_, 16, 4))
        m["x0i"] = np.ascontiguousarray(sim[:, 4 * c:4 * c + 4].reshape(L_, 4, 16, 2, 64).transpose(3, 4, 0, 2, 1).reshape(128, L_, 16, 4))
        in_maps.append(m)
    res = run_bass_kernel_spmd(nc, in_maps, core_ids=list(range(8)))
    R = res.results
    y_prompt = np.stack([R[c]["yp"] for c in range(8)])
    y_sample = np.concatenate([R[c]["ys"].reshape(4, 8, D) for c in range(8)])
    kwp = np.stack([R[c]["kwp"].reshape(L_, T, 8, 64) for c in range(8)], axis=1)
    vwp = np.stack([R[c]["vwp"].reshape(L_, T, 8, 64) for c in range(8)], axis=1)
    kws = np.concatenate([R[c]["kws"].reshape(L_, 4, 2048, 8, 64) for c in range(8)], axis=1)
    vws = np.concatenate([R[c]["vws"].reshape(L_, 4, 2048, 8, 64) for c in range(8)], axis=1)
    z = lambda *s: np.zeros(s, np.float32)
    gla_p = np.stack([R[c]["gp"] for c in range(8)], axis=1)
    gla_s = np.concatenate([R[c]["gs"] for c in range(8)], axis=1)
    def unp(a):
        return np.ascontiguousarray(a.reshape(L_, 2, 64, 16).transpose(0, 3, 1, 2).reshape(L_, 32, 64))

    def uns(a):
        return np.ascontiguousarray(a.reshape(L_, 2, 64, 16, 4).transpose(0, 4, 3, 1, 2).reshape(L_, 4, 32, 64))
    srp_ = np.stack([unp(R[c]["efr"]) for c in range(8)], axis=1)
    sip_ = np.stack([unp(R[c]["efi"]) for c in range(8)], axis=1)
    srs_ = np.concatenate([uns(R[c]["esr"]) for c in range(8)], axis=1)
    sis_ = np.concatenate([uns(R[c]["esi"]) for c in range(8)], axis=1)
    return (y_prompt, y_sample, kwp, vwp, kws, vws, srp_, sip_, srs_, sis_, gla_p, gla_s)
```

```python
import math
import contextlib
import numpy as np
import concourse.bass as bass
import concourse.mybir as mybir
from concourse.bass_utils import run_bass_kernel_spmd

F32 = mybir.dt.float32
BF16 = mybir.dt.bfloat16
AF = mybir.ActivationFunctionType
ALU = mybir.AluOpType
AX = mybir.AxisListType

ENGS = ("pe", "act", "dve", "pool", "sp")
L_ = 4
D = 1024
T = 2048
TS = 32
NT = T + TS
INC = 7696
EPS = 1e-6
CH = [(0, 512), (512, 512), (1024, 512), (1536, 512), (2048, 32)]
PAT = ((128, 1), (512, 4), (2048, 16))


class Reg:
    __slots__ = ("name", "w", "rs")

    def __init__(self, name):
        self.name = name
        self.w = None
        self.rs = []


class Op:
    __slots__ = ("eng", "fn", "deps", "need_inc", "dma", "slot", "slot_val", "cnt")

    def __init__(self, eng, fn, dma):
        self.eng = eng
        self.fn = fn
        self.dma = dma
        self.deps = set()
        self.need_inc = False
        self.slot = None
        self.slot_val = 0
        self.cnt = 0


class Prog:
    def __init__(self, nc, n_dma_slots=16):
        self.nc = nc
        self.ops = {e: [] for e in ENGS}
        self.n_slots = n_dma_slots
        self.dma_count = {e: 0 for e in ENGS}
        self.fence_ops = []

    def _add(self, op, reads, writes, nofence=False):
        for r in reads:
            if r.w is not None:
                op.deps.add(r.w)
        for w in writes:
            if w.w is not None:
                op.deps.add(w.w)
            for o in w.rs:
                op.deps.add(o)
        if not nofence:
            for o in self.fence_ops:
                op.deps.add(o)
        op.deps.discard(op)
        for r in reads:
            r.rs.append(op)
        for w in writes:
            w.w = op
            w.rs = []
        self.ops[op.eng].append(op)
        return op

    def fence(self):
        self.fence_ops = [self.ops[e][-1] for e in ENGS if self.ops[e]]

    def op(self, eng, fn, reads=(), writes=()):
        return self._add(Op(eng, fn, False), reads, writes)

    def dma(self, eng, fn, reads=(), writes=(), nofence=False):
        op = Op(eng, fn, True)
        k = self.dma_count[eng]
        self.dma_count[eng] += 1
        op.slot = k % self.n_slots
        op.slot_val = 16 * (k // self.n_slots + 1)
        return self._add(op, reads, writes, nofence)

    def emit(self):
        nc = self.nc
        for e in ENGS:
            for op in self.ops[e]:
                for d in op.deps:
                    if not d.dma:
                        if d.eng == "pe" and op.eng == "pe":
                            continue
                        d.need_inc = True
        for e in ENGS:
            c = 0
            for op in self.ops[e]:
                if not op.dma and op.need_inc:
                    c += 1
                op.cnt = c
        with contextlib.ExitStack() as st:
            sems = {e: st.enter_context(nc.semaphore("s_" + e)) for e in ENGS}
            dsems = {e: [st.enter_context(nc.semaphore("d_%s_%d" % (e, i))) for i in range(self.n_slots)]
                     for e in ENGS if self.dma_count[e] > 0}
            block = st.enter_context(nc.Block())

            def run(ename, eng):
                waited = {}
                dwaited = {}
                for op in self.ops[ename]:
                    need = {}
                    dneed = {}
                    for d in op.deps:
                        if d.dma:
                            key = (d.eng, d.slot)
                            if dwaited.get(key, 0) < d.slot_val:
                                dneed[key] = max(dneed.get(key, 0), d.slot_val)
                        else:
                            if d.eng == "pe" and ename == "pe":
                                continue
                            if waited.get(d.eng, 0) < d.cnt:
                                need[d.eng] = max(need.get(d.eng, 0), d.cnt)
                    if op.dma and op.slot_val > 16:
                        key = (ename, op.slot)
                        v = op.slot_val - 16
                        if dwaited.get(key, 0) < v:
                            dneed[key] = max(dneed.get(key, 0), v)
                    for pe_, v in need.items():
                        eng.wait_ge(sems[pe_], v)
                        waited[pe_] = v
                    for key, v in dneed.items():
                        eng.wait_ge(dsems[key[0]][key[1]], v)
                        dwaited[key] = v
                    ins = op.fn(eng)
                    if op.dma:
                        ins.then_inc(dsems[ename][op.slot], 16)
                    elif op.need_inc:
                        ins.then_inc(sems[ename], 1)
                if self.dma_count[ename] > 0:
                    last = {}
                    for op in self.ops[ename]:
                        if op.dma:
                            last[op.slot] = op.slot_val
                    for s, v in last.items():
                        if dwaited.get((ename, s), 0) < v:
                            eng.wait_ge(dsems[ename][s], v)

            if self.ops["sp"]:
                @block.sync
                def _(eng):
                    run("sp", eng)
            if self.ops["pe"]:
                @block.tensor
                def _(eng):
                    run("pe", eng)
            if self.ops["act"]:
                @block.scalar
                def _(eng):
                    run("act", eng)
            if self.ops["dve"]:
                @block.vector
                def _(eng):
                    run("dve", eng)
            if self.ops["pool"]:
                @block.gpsimd
                def _(eng):
                    run("pool", eng)


def _t5_bucket(dist):
    exact = 16
    d = np.maximum(dist, 1).astype(np.float32)
    large = exact + (np.log(d / exact) / np.log(2048 / exact) * (32 - exact)).astype(np.int32)
    large = np.minimum(large, 31)
    return np.where(dist < exact, dist, large).astype(np.int32)


def build(n_layers=L_, dbg=False):
    nc = bass.Bass("TRN2", target_bir_lowering=False)

    def din(name, shape):
        return nc.dram_tensor(name, list(shape), F32, kind="ExternalInput").ap()

    def dout(name, shape):
        return nc.dram_tensor(name, list(shape), F32, kind="ExternalOutput").ap()

    xp = din("xp", [T, D]); xs = din("xs", [TS, D])
    ck = din("ck", [L_, 4, 2048, 512]); cv = din("cv", [L_, 4, 2048, 512])
    w_in = din("w_in", [L_, D, INC]); w_o_a = din("w_o_a", [L_, 512, D])
    w_glu = din("w_glu", [L_, 512, 2048]); w_o_c = din("w_o_c", [L_, 512, D])
    w_out = din("w_out", [L_, D, D]); w_up = din("w_up", [L_, D, 4096]); w_down = din("w_down", [L_, 4096, D])
    ident_d = din("ident", [128, 128])
    gains_d = din("gains", [128, 9, 8])
    biasT_d = din("biasT", [24, 128, 256])
    sbias_d = din("sbias", [8, 128, 240])
    sbiasn_d = din("sbiasn", [8, 32, 96])
    aB_d = din("aB", [128, L_, 3, 16])
    cB_d = din("cB", [128, L_, 2, 16, 16])
    aA_d = din("aA", [128, L_, 5, 4, 64])
    dA_d = din("dA", [128, L_, 4])
    mk_d = din("mk", [128, 4])
    x0r_d = din("x0r", [128, L_, 16, 4]); x0i_d = din("x0i", [128, L_, 16, 4])
    efr_d = dout("efr", [L_, 128, 16]); efi_d = dout("efi", [L_, 128, 16])
    esr_d = dout("esr", [L_, 128, 64]); esi_d = dout("esi", [L_, 128, 64])
    wfg2_d = din("wfg2", [L_, 16, 256]); nbfg_d = din("nbfg", [128, L_, 2]); gnorm_d = din("gnorm", [128, L_])
    sg_d = din("sg", [L_, 4, 4, 64, 128])
    cmask_d = din("cmask", [128, 128]); smask_d = din("smask", [32, 32]); selm_d = din("selm", [32, 4])
    gp = dout("gp", [L_, 4, 64, 128]); gs = dout("gs", [L_, 4, 4, 64, 128])
    yp = dout("yp", [T, D]); ys = dout("ys", [TS, D])
    kwp = dout("kwp", [L_, T, 512]); vwp = dout("vwp", [L_, T, 512])
    kws = dout("kws", [L_, 4, 2048, 512]); vws = dout("vws", [L_, 4, 2048, 512])

    st = contextlib.ExitStack()
    with st:
        def sb(name, shape, dt):
            return st.enter_context(nc.sbuf_tensor(name, list(shape), dt))

        P = Prog(nc)
        X = sb("X", [128, 8, NT], F32); rX = [[Reg("X%d_%d" % (k, c)) for c in range(5)] for k in range(8)]
        H = sb("H", [128, 8, NT], BF16); rH = [Reg("H%d" % c) for c in range(5)]
        NSLOT = 2
        WR = [sb("wr%d" % i, [128, 4096], BF16) for i in range(NSLOT)]
        rWR = [Reg("wr%d" % i) for i in range(NSLOT)]
        ident = sb("ident_sb", [128, 128], F32); rident = Reg("ident")
        ones = sb("ones_sb", [128, 128], BF16); rones = Reg("ones")
        gains = sb("gains_sb", [128, 9, 8], F32); rgains = Reg("gains")
        identb = sb("identb_sb", [128, 128], BF16); ridentb = Reg("identb")
        nbfg = sb("nbfg_sb", [128, L_, 2], F32); rnbfg = Reg("nbfg")
        gnorm = sb("gnorm_sb", [128, L_], F32); rgnorm = Reg("gnorm")
        cmask = sb("cmask_sb", [128, 128], F32); rcmask = Reg("cmask")
        smask = sb("smask_sb", [32, 32], F32); rsmask = Reg("smask")
        selm = sb("selm_sb", [32, 4], F32); rselm = Reg("selm")
        mk = sb("mk_sb", [128, 4], F32); rmk = Reg("mk")
        negpi = sb("negpi_sb", [128, 1], F32); rnegpi = Reg("negpi")
        dA = sb("dA_sb", [128, L_, 4], F32); rdA = Reg("dA")
        PS = st.enter_context(nc.psum_tensor("PS", [128, 8, 512], F32))
        rPS = [Reg("ps%d" % i) for i in range(8)]
        SCR = sb("SCR", [128, 94208], mybir.dt.uint8)

        class Carver:
            def __init__(self):
                self.off = 0

            def reset(self):
                self.off = 0

            def take(self, shape, dt):
                es = 2 if dt == BF16 else 4
                n = int(np.prod(shape[1:])) * es
                n = (n + 63) // 64 * 64
                v = SCR[0:shape[0], self.off:self.off + int(np.prod(shape[1:])) * es].bitcast(dt)
                self.off += n
                assert self.off <= 94208, self.off
                if len(shape) == 3:
                    v = v.rearrange("p (a b) -> p a b", b=shape[2])
                elif len(shape) == 4:
                    v = v.rearrange("p (a b c) -> p a b c", b=shape[2], c=shape[3])
                return v
        CV = Carver()

        psc = [0]

        def bank():
            b = psc[0] % 8
            psc[0] += 1
            return b
        wc = [0]

        def wslot():
            s = wc[0] % NSLOT
            wc[0] += 1
            return s

        def wload_pm(srcs, K, cw):
            s = wslot()
            n = len(srcs)
            view = WR[s][:, 0:n * K * cw].rearrange("p (a k c) -> p a k c", k=K, c=cw)
            for a, src in enumerate(srcs):
                P.dma("pool", lambda e, src=src, a=a, view=view: e.dma_start(out=view[:, a, :, :], in_=src),
                      writes=[rWR[s]], nofence=True)
            return view, rWR[s]

        def wload(pieces, K, C):
            s = wslot()
            view = WR[s][:, 0:K * C].rearrange("p (k c) -> p k c", c=C)
            for (src, c0, cw) in pieces:
                P.dma("pool", lambda e, src=src, c0=c0, cw=cw, view=view: e.dma_start(out=view[:, :, c0:c0 + cw], in_=src),
                      writes=[rWR[s]], nofence=True)
            return view, rWR[s]

        def wsrc(w, l, kk, c0, cw):
            return w[l].rearrange("(k p) c -> p k c", p=128)[:, 0:kk, c0:c0 + cw]

        P.dma("sp", lambda e: e.dma_start(out=ident[:], in_=ident_d), writes=[rident])
        P.dma("sp", lambda e: e.dma_start(out=gains[:], in_=gains_d), writes=[rgains])
        P.op("dve", lambda e: e.memset(ones[:], 1.0), writes=[rones])
        P.op("dve", lambda e: e.memset(negpi[:], -math.pi), writes=[rnegpi])
        P.op("dve", lambda e: e.tensor_copy(identb[:], ident[:]), reads=[rident], writes=[ridentb])
        P.dma("sp", lambda e: e.dma_start(out=nbfg[:], in_=nbfg_d), writes=[rnbfg])
        P.dma("sp", lambda e: e.dma_start(out=mk[:], in_=mk_d), writes=[rmk])
        P.dma("sp", lambda e: e.dma_start(out=dA[:], in_=dA_d), writes=[rdA])
        P.dma("sp", lambda e: e.dma_start(out=gnorm[:], in_=gnorm_d), writes=[rgnorm])
        P.dma("sp", lambda e: e.dma_start(out=cmask[:], in_=cmask_d), writes=[rcmask])
        P.dma("sp", lambda e: e.dma_start(out=smask[:], in_=smask_d), writes=[rsmask])
        P.dma("sp", lambda e: e.dma_start(out=selm[:], in_=selm_d), writes=[rselm])

        CV.reset()
        xtok = [CV.take([128, 1024], F32) for _ in range(2)]
        rxtok = [Reg("xtok0"), Reg("xtok1")]
        for tb in range(17):
            i = tb % 2
            ntok = 128 if tb < 16 else TS
            src = xp[tb * 128:(tb + 1) * 128, :] if tb < 16 else xs
            P.dma("sp", lambda e, i=i, ntok=ntok, src=src: e.dma_start(out=xtok[i][0:ntok, :], in_=src), writes=[rxtok[i]])
            for half in range(2):
                b = bank()
                for kk in range(4):
                    k = half * 4 + kk
                    P.op("pe", lambda e, b=b, kk=kk, k=k, i=i, ntok=ntok: e.transpose(
                        PS[:, b, kk * 128:kk * 128 + ntok], xtok[i][0:ntok, k * 128:(k + 1) * 128], ident[0:ntok, 0:ntok]),
                        reads=[rxtok[i], rident], writes=[rPS[b]])
                c = tb // 4 if tb < 16 else 4
                P.op("act", lambda e, b=b, half=half, tb=tb, ntok=ntok: e.copy(
                    X[:, half * 4:half * 4 + 4, tb * 128:tb * 128 + ntok],
                    PS[:, b, :].rearrange("p (k t) -> p k t", t=128)[:, :, 0:ntok]),
                    reads=[rPS[b]], writes=[rX[k][c] for k in range(half * 4, half * 4 + 4)])
        P.fence()

        def rmsnorm_to_H(gi, SQ, rSQ, RS, rRS):
            for ci, (c0, n) in enumerate(CH):
                P.op("act", lambda e, c0=c0, n=n: e.activation(SQ[:, :, 0:n], X[:, :, c0:c0 + n], AF.Square),
                     reads=[rX[k][ci] for k in range(8)], writes=[rSQ])
                b = bank()
                for k in range(8):
                    P.op("pe", lambda e, b=b, k=k, n=n: e.matmul(PS[:, b, 0:n], ones[:], SQ[:, k, 0:n], start=(k == 0), stop=(k == 7)),
                         reads=[rSQ, rones], writes=[rPS[b]])
                P.op("dve", lambda e, b=b, n=n: e.tensor_scalar(RS[:, 0:n], PS[:, b, 0:n], 1.0 / D, EPS, ALU.mult, ALU.add),
                     reads=[rPS[b]], writes=[rRS])
                P.op("act", lambda e, n=n: e.activation(RS[:, 0:n], RS[:, 0:n], AF.Sqrt), reads=[rRS], writes=[rRS])
                P.op("dve", lambda e, n=n: e.reciprocal(RS[:, 0:n], RS[:, 0:n]), reads=[rRS], writes=[rRS])
                for k in range(8):
                    P.op("dve", lambda e, k=k, c0=c0, n=n: e.scalar_tensor_tensor(
                        H[:, k, c0:c0 + n], X[:, k, c0:c0 + n], gains[:, gi, k:k + 1], RS[:, 0:n], ALU.mult, ALU.mult),
                        reads=[rX[k][ci], rRS, rgains], writes=[rH[ci]])

        def mlp(l):
            CV.reset()
            SQ = CV.take([128, 8, 512], BF16); rSQ = Reg("sq")
            RS = CV.take([128, 512], F32); rRS = Reg("rs")
            rmsnorm_to_H(2 * l + 1, SQ, rSQ, RS, rRS)
            A = CV.take([128, 32, 512], BF16); rA = Reg("A")
            for ci, (c0, n) in enumerate(CH):
                for u in range(8):
                    W, rW = wload([(wsrc(w_up, l, 8, u * 512, 512), 0, 512)], 8, 512)
                    for m in range(4):
                        b = bank()
                        for k in range(8):
                            P.op("pe", lambda e, b=b, k=k, m=m, W=W, c0=c0, n=n: e.matmul(
                                PS[:, b, 0:n], W[:, k, m * 128:(m + 1) * 128], H[:, k, c0:c0 + n], start=(k == 0), stop=(k == 7)),
                                reads=[rW, rH[ci]], writes=[rPS[b]])
                        P.op("act", lambda e, b=b, n=n, u=u, m=m: e.activation(A[:, u * 4 + m, 0:n], PS[:, b, 0:n], AF.Relu),
                             reads=[rPS[b]], writes=[rA])
                        P.op("dve", lambda e, n=n, u=u, m=m: e.tensor_tensor(A[:, u * 4 + m, 0:n], A[:, u * 4 + m, 0:n], A[:, u * 4 + m, 0:n], ALU.mult),
                             reads=[rA], writes=[rA])
                for mu in range(2):
                    bs = [bank() for _ in range(4)]
                    for kq in range(4):
                        src = w_down[l].rearrange("(k p) c -> p k c", p=128)[:, kq * 8:(kq + 1) * 8, mu * 512:(mu + 1) * 512]
                        W, rW = wload([(src, 0, 512)], 8, 512)
                        for m in range(4):
                            for k in range(8):
                                kk = kq * 8 + k
                                P.op("pe", lambda e, b=bs[m], k=k, kk=kk, m=m, W=W, n=n: e.matmul(
                                    PS[:, b, 0:n], W[:, k, m * 128:(m + 1) * 128], A[:, kk, 0:n], start=(kk == 0), stop=(kk == 31)),
                                    reads=[rW, rA], writes=[rPS[bs[m]]])
                    for m in range(4):
                        mt = mu * 4 + m
                        P.op("dve", lambda e, b=bs[m], mt=mt, c0=c0, n=n: e.tensor_tensor(
                            X[:, mt, c0:c0 + n], X[:, mt, c0:c0 + n], PS[:, b, 0:n], ALU.add),
                            reads=[rPS[bs[m]], rX[mt][ci]], writes=[rX[mt][ci]])
            P.fence()

        def branch_out(l, Y, rY, w_o, gcol, glu=False):
            G = CV.take([128, 8, NT], BF16); rG = [Reg("G%d" % c) for c in range(5)]
            SG = CV.take([128, 512], F32); rSG = Reg("SG")
            SG2 = CV.take([128, 512], F32); rSG2 = Reg("SG2")
            for half in range(2):
                Wg, rWg = wload([(wsrc(w_in, l, 8, gcol + half * 512, 512), 0, 512)], 8, 512)
                if not glu:
                    Wo, rWo = wload([(wsrc(w_o, l, 4, half * 512, 512), 0, 512)], 4, 512)
                else:
                    Wo, rWo = wload([(wsrc(w_o, l, 4, half * 512, 512), 0, 512), (wsrc(w_o, l, 4, 1024 + half * 512, 512), 512, 512)], 4, 1024)
                for m in range(4):
                    mt = half * 4 + m
                    for ci, (c0, n) in enumerate(CH):
                        bg = bank()
                        for k in range(8):
                            P.op("pe", lambda e, b=bg, k=k, m=m, Wg=Wg, c0=c0, n=n: e.matmul(
                                PS[:, b, 0:n], Wg[:, k, m * 128:(m + 1) * 128], H[:, k, c0:c0 + n], start=(k == 0), stop=(k == 7)),
                                reads=[rWg, rH[ci]], writes=[rPS[bg]])
                        P.op("act", lambda e, b=bg, n=n: e.activation(SG[:, 0:n], PS[:, b, 0:n], AF.Sigmoid), reads=[rPS[bg]], writes=[rSG])
                        by = bank()
                        for k in range(4):
                            P.op("pe", lambda e, b=by, k=k, m=m, Wo=Wo, c0=c0, n=n: e.matmul(
                                PS[:, b, 0:n], Wo[:, k, m * 128:(m + 1) * 128], Y[:, k, c0:c0 + n], start=(k == 0), stop=(k == 3)),
                                reads=[rWo, rY], writes=[rPS[by]])
                        if glu:
                            b2 = bank()
                            for k in range(4):
                                P.op("pe", lambda e, b=b2, k=k, m=m, Wo=Wo, c0=c0, n=n: e.matmul(
                                    PS[:, b, 0:n], Wo[:, k, 512 + m * 128:512 + (m + 1) * 128], Y[:, k, c0:c0 + n], start=(k == 0), stop=(k == 3)),
                                    reads=[rWo, rY], writes=[rPS[b2]])
                            P.op("act", lambda e, b=b2, n=n: e.activation(SG2[:, 0:n], PS[:, b, 0:n], AF.Sigmoid), reads=[rPS[b2]], writes=[rSG2])
                            P.op("dve", lambda e, n=n: e.tensor_tensor(SG[:, 0:n], SG[:, 0:n], SG2[:, 0:n], ALU.mult), reads=[rSG, rSG2], writes=[rSG])
                        P.op("dve", lambda e, b=by, mt=mt, c0=c0, n=n: e.tensor_tensor(G[:, mt, c0:c0 + n], SG[:, 0:n], PS[:, b, 0:n], ALU.mult),
                             reads=[rSG, rPS[by]], writes=[rG[ci]])
            for half in range(2):
                Wo, rWo = wload([(wsrc(w_out, l, 8, half * 512, 512), 0, 512)], 8, 512)
                for m in range(4):
                    mt = half * 4 + m
                    for ci, (c0, n) in enumerate(CH):
                        b = bank()
                        for k in range(8):
                            P.op("pe", lambda e, b=b, k=k, m=m, Wo=Wo, c0=c0, n=n: e.matmul(
                                PS[:, b, 0:n], Wo[:, k, m * 128:(m + 1) * 128], G[:, k, c0:c0 + n], start=(k == 0), stop=(k == 7)),
                                reads=[rWo, rG[ci]], writes=[rPS[b]])
                        P.op("dve", lambda e, b=b, mt=mt, c0=c0, n=n: e.tensor_tensor(
                            X[:, mt, c0:c0 + n], X[:, mt, c0:c0 + n], PS[:, b, 0:n], ALU.add),
                            reads=[rPS[b], rX[mt][ci]], writes=[rX[mt][ci]])

        def attention(l):
            CV.reset()
            OA = CV.take([128, 4, NT], BF16); rOA = Reg("OA")
            mark = CV.off
            QT = CV.take([128, 3, NT], BF16); rQT = Reg("QT")
            KT = CV.take([128, NT], BF16); rKT = Reg("KT")
            Vn = CV.take([128, 3, 16, 128], BF16); rVn = Reg("Vn")
            ACN = CV.take([64, NT], F32); rACN = Reg("ACN")
            ACD = CV.take([64, NT], F32); rACD = Reg("ACD")
            EXB = CV.take([128, 6, 256], BF16); rBT = Reg("EXB")
            BTs = [CV.take([128, 256], F32) for _ in range(2)]; rBTs = [Reg("bts0"), Reg("bts1")]
            KVst = [CV.take([128, 256], F32) for _ in range(2)]; rKVst = [Reg("kvst0"), Reg("kvst1")]
            T1 = [CV.take([128, 256], BF16) for _ in range(3)]; rT1 = [Reg("t1%d" % i) for i in range(3)]
            PT = [CV.take([128, 256], BF16) for _ in range(6)]; rPT = [Reg("pt%d" % i) for i in range(6)]
            KC2 = [CV.take([128, 10, 128], BF16) for _ in range(2)]; rKC2 = [Reg("KC0"), Reg("KC1")]
            VC2 = [CV.take([128, 10, 128], BF16) for _ in range(2)]; rVC2 = [Reg("VC0"), Reg("VC1")]

            def load_cache(hp, bq):
                for (src_d, dstt, rdst) in ((ck, KC2[bq % 2], rKC2[bq % 2]), (cv, VC2[bq % 2], rVC2[bq % 2])):
                    P.dma("pool", lambda e, src_d=src_d, dstt=dstt, bq=bq, hp=hp: e.dma_start(
                        out=dstt[:, 0:4, :], in_=src_d[l, bq, 1536:2048, hp * 128:(hp + 1) * 128].rearrange("(k p) d -> p k d", p=128)), writes=[rdst])
                    for u in range(6):
                        P.dma("pool", lambda e, src_d=src_d, dstt=dstt, bq=bq, hp=hp, u=u: e.dma_start(
                            out=dstt[:, 4 + u, :],
                            in_=src_d[l, bq, 256 * u:256 * u + 256, hp * 128:(hp + 1) * 128].rearrange("(a c) d -> a c d", c=16)[:, 0:8, :]), writes=[rdst])
            KCT = CV.take([128, 10, 128], BF16); rKCT = Reg("KCT")
            SBt = CV.take([128, 2, 240], F32); rSBt = Reg("SBt")
            SBn = CV.take([32, 2, 96], F32); rSBn = Reg("SBn")
            T1s = CV.take([128, 240], F32); rT1s = Reg("T1s")
            T1n = CV.take([32, 24], F32); rT1n = Reg("T1n")
            PTs = CV.take([128, 10, 8], BF16); rPTs = Reg("PTs")
            PTn = CV.take([32, 8], BF16); rPTn = Reg("PTn")
            Vsn = CV.take([32, 128], BF16); rVsn = Reg("Vsn")
            RCs = CV.take([64, 32], F32); rRCs = Reg("RCs")
            print("ATT scratch used", CV.off, "of 94208")
            for hp in range(4):
                W, rW = wload_pm([wsrc(w_in, l, 8, g * 512 + hp * 128, 128) for g in range(3)]
                                 + [wsrc(w_in, l, 8, 1536 + hp * 128, 128)], 8, 128)
                for m in range(4):
                    for ci, (c0, n) in enumerate(CH):
                        b = bank()
                        for k in range(8):
                            P.op("pe", lambda e, b=b, k=k, m=m, W=W, c0=c0, n=n: e.matmul(
                                PS[:, b, 0:n], W[:, m, k, :], H[:, k, c0:c0 + n], start=(k == 0), stop=(k == 7)),
                                reads=[rW, rH[ci]], writes=[rPS[b]])
                        if m < 3:
                            P.op("act", lambda e, b=b, m=m, c0=c0, n=n: e.mul(QT[:, m, c0:c0 + n], PS[:, b, 0:n], 0.125),
                                 reads=[rPS[b]], writes=[rQT])
                        else:
                            P.op("act", lambda e, b=b, c0=c0, n=n: e.copy(KT[:, c0:c0 + n], PS[:, b, 0:n]), reads=[rPS[b]], writes=[rKT])
                W2, rW2 = wload_pm([wsrc(w_in, l, 8, 1536 + hp * 128, 128), wsrc(w_in, l, 8, 2048 + hp * 128, 128)], 8, 128)
                load_cache(hp, 0)
                load_cache(hp, 1)
                import os
                KSUB = int(os.environ.get("KSUB", "7"))
                for tb in range(17 if (KSUB & 1) else 0):
                    ntok = 128 if tb < 16 else TS
                    b = bank()
                    for a in range(2):
                        for k in range(8):
                            P.op("pe", lambda e, b=b, k=k, a=a, tb=tb, ntok=ntok, W2=W2: e.matmul(
                                PS[0:ntok, b, a * 128:(a + 1) * 128], H[:, k, tb * 128:tb * 128 + ntok], W2[:, a, k, :], start=(k == 0), stop=(k == 7)),
                                reads=[rW2, rH[tb // 4 if tb < 16 else 4]], writes=[rPS[b]])
                    i = tb % 2
                    P.op("act", lambda e, b=b, i=i, ntok=ntok: e.copy(KVst[i][0:ntok, :], PS[0:ntok, b, 0:256]), reads=[rPS[b]], writes=[rKVst[i]])
                    if tb < 16:
                        P.op("dve", lambda e, i=i, tb=tb: e.tensor_copy(Vn[:, 0, tb, :], KVst[i][:, 128:256]), reads=[rKVst[i]], writes=[rVn])
                        P.dma("sp", lambda e, i=i, tb=tb, hp=hp: e.dma_start(out=kwp[l, tb * 128:(tb + 1) * 128, hp * 128:(hp + 1) * 128], in_=KVst[i][:, 0:128]), reads=[rKVst[i]])
                        P.dma("sp", lambda e, i=i, tb=tb, hp=hp: e.dma_start(out=vwp[l, tb * 128:(tb + 1) * 128, hp * 128:(hp + 1) * 128], in_=KVst[i][:, 128:256]), reads=[rKVst[i]])
                    else:
                        P.op("dve", lambda e, i=i: e.tensor_copy(Vsn[:], KVst[i][0:32, 128:256]), reads=[rKVst[i]], writes=[rVsn])
                        for bq in range(4):
                            P.dma("sp", lambda e, i=i, hp=hp, bq=bq: e.dma_start(out=kws[l, bq, 2040:2048, hp * 128:(hp + 1) * 128], in_=KVst[i][bq * 8:(bq + 1) * 8, 0:128]), reads=[rKVst[i]])
                            P.dma("sp", lambda e, i=i, hp=hp, bq=bq: e.dma_start(out=vws[l, bq, 2040:2048, hp * 128:(hp + 1) * 128], in_=KVst[i][bq * 8:(bq + 1) * 8, 128:256]), reads=[rKVst[i]])
                for g in ((1, 2) if (KSUB & 2) else ()):
                    d = PAT[g][1]
                    nblk = T // d // 128
                    for r in range(d):
                        for nb in range(nblk):
                            t0 = r + d * 128 * nb
                            b = bank()
                            for k in range(8):
                                P.op("pe", lambda e, b=b, k=k, t0=t0, d=d, W2=W2: e.matmul(
                                    PS[:, b, 0:128], H[:, k, t0:t0 + 128 * d:d], W2[:, 1, k, :], start=(k == 0), stop=(k == 7)),
                                    reads=[rW2] + rH[0:4], writes=[rPS[b]])
                            P.op("act", lambda e, b=b, g=g, r=r, nb=nb, nblk=nblk: e.copy(Vn[:, g, r * nblk + nb, :], PS[:, b, 0:128]),
                                 reads=[rPS[b]], writes=[rVn])
                for gi in range(6):
                    g, hh_ = gi // 2, gi % 2
                    P.dma("sp", lambda e, hp=hp, g=g, hh_=hh_, gi=gi: e.dma_start(out=BTs[gi % 2][:], in_=biasT_d[g * 8 + 2 * hp + hh_]), writes=[rBTs[gi % 2]])
                    P.op("act", lambda e, gi=gi: e.activation(EXB[:, gi, :], BTs[gi % 2][:], AF.Exp), reads=[rBTs[gi % 2]], writes=[rBT])
                import os
                KATT = int(os.environ.get("KATT", "3"))
                for hh in range(2 if KATT >= 2 else 0):
                    hs = slice(hh * 64, hh * 64 + 64)
                    tiles = []
                    for g in range(3):
                        d = PAT[g][1]
                        nblk = T // d // 128
                        for r in range(d):
                            for kb in range(nblk):
                                tiles.append((g, d, nblk, r, kb))
                    st_ = {}

                    def s_stage(i):
                        g, d, nblk, r, kb = tiles[i]
                        nq = 256 if kb < nblk - 1 else 128
                        k0 = r + d * 128 * kb
                        bS = bank()
                        P.op("pe", lambda e, b=bS, k0=k0, d=d, nq=nq, g=g, hs=hs: e.matmul(
                            PS[:, b, 0:nq], KT[hs, k0:k0 + 128 * d:d], QT[hs, g, k0:k0 + nq * d:d], start=True, stop=True),
                            reads=[rKT, rQT], writes=[rPS[bS]])
                        ti = i % 3
                        pidx = i % 6
                        P.op("act", lambda e, b=bS, nq=nq, ti=ti: e.activation(T1[ti][:, 0:nq], PS[:, b, 0:nq], AF.Exp),
                             reads=[rPS[bS]], writes=[rT1[ti]])
                        P.op("dve", lambda e, nq=nq, ti=ti, pidx=pidx, g=g, hh=hh: e.tensor_tensor(
                            PT[pidx][:, 0:nq], T1[ti][:, 0:nq], EXB[:, g * 2 + hh, 0:nq], ALU.mult),
                            reads=[rT1[ti], rBT], writes=[rPT[pidx]])

                    def pv_stage(i):
                        g, d, nblk, r, kb = tiles[i]
                        pidx = i % 6
                        qb = kb
                        if qb % 4 == 0:
                            st_["bN"] = bank(); st_["bD"] = bank()
                        bN, bD = st_["bN"], st_["bD"]
                        col = (qb % 4) * 128
                        first = True
                        if qb > 0:
                            pp = (i - 1) % 6
                            vblk = r * nblk + qb - 1
                            P.op("pe", lambda e, b=bN, col=col, g=g, vblk=vblk, hs=hs, pp=pp: e.matmul(
                                PS[0:64, b, col:col + 128], Vn[:, g, vblk, hs], PT[pp][:, 128:256], start=True, stop=False),
                                reads=[rVn, rPT[pp]], writes=[rPS[bN]])
                            P.op("pe", lambda e, b=bD, col=col, pp=pp: e.matmul(
                                PS[0:64, b, col:col + 128], ones[:, 0:64], PT[pp][:, 128:256], start=True, stop=False),
                                reads=[rones, rPT[pp]], writes=[rPS[bD]])
                            first = False
                        vblk = r * nblk + qb
                        P.op("pe", lambda e, b=bN, col=col, g=g, vblk=vblk, hs=hs, pidx=pidx, first=first: e.matmul(
                            PS[0:64, b, col:col + 128], Vn[:, g, vblk, hs], PT[pidx][:, 0:128], start=first, stop=True),
                            reads=[rVn, rPT[pidx]], writes=[rPS[bN]])
                        P.op("pe", lambda e, b=bD, col=col, pidx=pidx, first=first: e.matmul(
                            PS[0:64, b, col:col + 128], ones[:, 0:64], PT[pidx][:, 0:128], start=first, stop=True),
                            reads=[rones, rPT[pidx]], writes=[rPS[bD]])
                        if qb % 4 == 3 or qb == nblk - 1:
                            q0b = (qb // 4) * 4
                            nqq = (qb - q0b + 1) * 128
                            tt0 = r + d * 128 * q0b
                            dst = slice(tt0, tt0 + nqq * d, d)
                            if g == 0:
                                P.op("act", lambda e, b=bN, nqq=nqq, dst=dst: e.copy(ACN[:, dst], PS[0:64, b, 0:nqq]), reads=[rPS[bN]], writes=[rACN])
                                P.op("act", lambda e, b=bD, nqq=nqq, dst=dst: e.copy(ACD[:, dst], PS[0:64, b, 0:nqq]), reads=[rPS[bD]], writes=[rACD])
                            else:
                                P.op("dve", lambda e, b=bN, nqq=nqq, dst=dst: e.tensor_tensor(ACN[:, dst], ACN[:, dst], PS[0:64, b, 0:nqq], ALU.add), reads=[rPS[bN], rACN], writes=[rACN])
                                P.op("dve", lambda e, b=bD, nqq=nqq, dst=dst: e.tensor_tensor(ACD[:, dst], ACD[:, dst], PS[0:64, b, 0:nqq], ALU.add), reads=[rPS[bD], rACD], writes=[rACD])

                    LA = 2
                    for i in range(len(tiles) + LA):
                        if i < len(tiles):
                            s_stage(i)
                        if i - LA >= 0:
                            pv_stage(i - LA)
                    P.op("dve", lambda e: e.reciprocal(ACD[:, 0:T], ACD[:, 0:T]), reads=[rACD], writes=[rACD])
                    P.op("dve", lambda e, hs=hs, hp=hp: e.tensor_tensor(OA[hs, hp, 0:T], ACN[:, 0:T], ACD[:, 0:T], ALU.mult), reads=[rACN, rACD], writes=[rOA])
                P.dma("sp", lambda e, hp=hp: e.dma_start(out=SBt[:], in_=sbias_d[2 * hp:2 * hp + 2].rearrange("h p c -> p h c")), writes=[rSBt])
                P.dma("sp", lambda e, hp=hp: e.dma_start(out=SBn[:], in_=sbiasn_d[2 * hp:2 * hp + 2].rearrange("h p c -> p h c")), writes=[rSBn])
                for bq in range(4):
                    KC, rKC, VC, rVC = KC2[bq % 2], rKC2[bq % 2], VC2[bq % 2], rVC2[bq % 2]
                    if bq >= 2:
                        load_cache(hp, bq)
                    for half in range(2):
                        b = bank()
                        PSB = PS[:, b, :].bitcast(BF16)
                        for kk in range(5):
                            blk = half * 5 + kk
                            P.op("pe", lambda e, PSB=PSB, kk=kk, blk=blk, KC=KC: e.transpose(PSB[:, kk * 128:(kk + 1) * 128], KC[:, blk, :], identb[:, :]),
                                 reads=[rKC, ridentb], writes=[rPS[b]])
                        P.op("act", lambda e, PSB=PSB, half=half: e.copy(KCT[:, half * 5:half * 5 + 5, :], PSB[:, 0:640].rearrange("p (k r) -> p k r", r=128)),
                             reads=[rPS[b]], writes=[rKCT])
                    for hh in range(2):
                        hs = slice(hh * 64, hh * 64 + 64)
                        qs = QT[hs, :, T + 8 * bq:T + 8 * bq + 8]
                        bS = bank()
                        for blk in range(10):
                            P.op("pe", lambda e, b=bS, blk=blk, hs=hs, qs=qs: e.matmul(PS[:, b, blk * 24:(blk + 1) * 24], KCT[hs, blk, :], qs, start=True, stop=True),
                                 reads=[rKCT, rQT], writes=[rPS[bS]])
                        P.op("pe", lambda e, b=bS, hs=hs, qs=qs: e.matmul(PS[0:32, b, 240:264], KT[hs, T:NT], qs, start=True, stop=True),
                             reads=[rKT, rQT], writes=[rPS[bS]])
                        P.op("dve", lambda e, b=bS, hh=hh: e.tensor_tensor(T1s[:], PS[:, b, 0:240], SBt[:, hh, :], ALU.add), reads=[rPS[bS], rSBt], writes=[rT1s])
                        P.op("dve", lambda e, b=bS, hh=hh, bq=bq: e.tensor_tensor(T1n[:], PS[0:32, b, 240:264], SBn[:, hh, bq * 24:(bq + 1) * 24], ALU.add),
                             reads=[rPS[bS], rSBn], writes=[rT1n, rPS[bS]])
                        P.op("act", lambda e: e.activation(T1s[:], T1s[:], AF.Exp), reads=[rT1s], writes=[rT1s])
                        P.op("act", lambda e: e.activation(T1n[:], T1n[:], AF.Exp), reads=[rT1n], writes=[rT1n])
                        T1v = T1s[:].rearrange("p (k g t) -> p k g t", g=3, t=8)
                        P.op("dve", lambda e, T1v=T1v: e.tensor_tensor(T1v[:, :, 0, :], T1v[:, :, 0, :], T1v[:, :, 1, :], ALU.add), reads=[rT1s], writes=[rT1s])
                        P.op("dve", lambda e, T1v=T1v: e.tensor_tensor(PTs[:], T1v[:, :, 0, :], T1v[:, :, 2, :], ALU.add), reads=[rT1s], writes=[rPTs])
                        P.op("dve", lambda e: e.tensor_tensor(T1n[:, 0:8], T1n[:, 0:8], T1n[:, 8:16], ALU.add), reads=[rT1n], writes=[rT1n])
                        P.op("dve", lambda e: e.tensor_tensor(PTn[:], T1n[:, 0:8], T1n[:, 16:24], ALU.add), reads=[rT1n], writes=[rPTn])
                        cols = slice(0, 8)
                        bN = bank(); bD = bank()
                        for blk in range(10):
                            P.op("pe", lambda e, b=bN, blk=blk, hs=hs, cols=cols, VC=VC: e.matmul(PS[0:64, b, cols], VC[:, blk, hs], PTs[:, blk, :], start=(blk == 0), stop=False),
                                 reads=[rVC, rPTs], writes=[rPS[bN]])
                        P.op("pe", lambda e, b=bN, hs=hs, cols=cols: e.matmul(PS[0:64, b, cols], Vsn[:, hs], PTn[:], start=False, stop=True),
                             reads=[rVsn, rPTn], writes=[rPS[bN]])
                        for blk in range(10):
                            P.op("pe", lambda e, b=bD, blk=blk, cols=cols: e.matmul(PS[0:64, b, cols], ones[:, 0:64], PTs[:, blk, :], start=(blk == 0), stop=False),
                                 reads=[rones, rPTs], writes=[rPS[bD]])
                        P.op("pe", lambda e, b=bD, cols=cols: e.matmul(PS[0:64, b, cols], ones[0:32, 0:64], PTn[:], start=False, stop=True),
                             reads=[rones, rPTn], writes=[rPS[bD]])
                        P.op("dve", lambda e, b=bD: e.reciprocal(RCs[:, 0:8], PS[0:64, b, 0:8]), reads=[rPS[bD]], writes=[rRCs])
                        P.op("dve", lambda e, b=bN, hs=hs, hp=hp, bq=bq: e.tensor_tensor(OA[hs, hp, T + 8 * bq:T + 8 * bq + 8], PS[0:64, b, 0:8], RCs[:, 0:8], ALU.mult),
                             reads=[rPS[bN], rRCs], writes=[rOA])
            P.fence()
            CV.off = mark
            if KATT < 3:
                return
            branch_out(l, OA, rOA, w_o_a, 4624)
            P.fence()

        def gla(l):
            CV.reset()
            OC = CV.take([128, 4, NT], BF16); rOC = Reg("OC")
            mark = CV.off
            QT = CV.take([128, 2, NT], BF16); rQT = Reg("gQT")
            KT = CV.take([128, 2, NT], BF16); rKT = Reg("gKT")
            EBL = CV.take([128, 2, 36], F32); rEBL = Reg("EBL")
            WF = CV.take([16, 256], BF16); rWF = Reg("WF")
            S0b = CV.take([128, 2, 4, 128], BF16); rS0b = Reg("S0b")
            S0f = CV.take([128, 2, 4, 128], F32); rS0f = Reg("S0f")
            r2 = CV.off
            KLT = CV.take([128, 2, NT], BF16); rKLT = Reg("KLT")
            _o = CV.off
            CV.off = r2
            SP = CV.take([128, 31, 128], BF16); rSP = rKLT
            CV.off = _o
            ph_b = CV.off
            L1 = CV.take([128, NT], F32); rL1 = Reg("L1")
            C_ = CV.take([128, NT], F32); rC = Reg("C")
            MASK = CV.take([128, NT], F32); rMASK = Reg("MASK")
            FC = CV.take([16, NT], BF16); rFC = Reg("FC")
            EBc = CV.take([128, 512], F32); rEBc = Reg("EBc")
            ENBc = CV.take([128, 512], F32); rENBc = Reg("ENBc")
            KTf = CV.take([128, 512], F32); rKTf = Reg("KTf")
            P.dma("pool", lambda e: e.dma_start(out=WF[:], in_=wfg2_d[l]), writes=[rWF])
            for hh in range(2):
                for m in range(2):
                    P.dma("sp", lambda e, hh=hh, m=m: e.dma_start(
                        out=S0f[hh * 64:(hh + 1) * 64, m, :, :],
                        in_=sg_d[l][:, 2 * m + hh, :, :].rearrange("b k v -> k b v")), writes=[rS0f])
            P.op("act", lambda e: e.copy(S0b[:], S0f[:]), reads=[rS0f], writes=[rS0b])
            P.op("dve", lambda e: e.memset(MASK[:], 1.0), writes=[rMASK])
            P.op("dve", lambda e: e.memset(MASK[:, 0:T].rearrange("p (c s) -> p c s", s=64)[:, :, 0:1], 0.0), writes=[rMASK])
            P.op("dve", lambda e: e.memset(MASK[:, T:NT].rearrange("p (c s) -> p c s", s=8)[:, :, 0:1], 0.0), writes=[rMASK])
            Wfc, rWfc = wload_pm([wsrc(w_in, l, 8, 4608, 16)], 8, 16)
            for ci, (c0, n) in enumerate(CH):
                b = bank()
                for k in range(8):
                    P.op("pe", lambda e, b=b, k=k, c0=c0, n=n: e.matmul(PS[0:16, b, 0:n], Wfc[:, 0, k, :], H[:, k, c0:c0 + n], start=(k == 0), stop=(k == 7)),
                         reads=[rWfc, rH[ci]], writes=[rPS[b]])
                P.op("act", lambda e, b=b, c0=c0, n=n: e.copy(FC[:, c0:c0 + n], PS[0:16, b, 0:n]), reads=[rPS[b]], writes=[rFC])
            Wqk, rWqk = wload([(wsrc(w_in, l, 8, 3072, 512), 0, 512)], 8, 512)
            for m in range(2):
                for ci, (c0, n) in enumerate(CH):
                    b = bank()
                    P.op("pe", lambda e, b=b, m=m, c0=c0, n=n: e.matmul(PS[:, b, 0:n], WF[:, m * 128:(m + 1) * 128], FC[:, c0:c0 + n], start=True, stop=True),
                         reads=[rWF, rFC], writes=[rPS[b]])
                    P.op("act", lambda e, b=b, m=m, c0=c0, n=n: e.activation(L1[:, c0:c0 + n], PS[:, b, 0:n], AF.Exp, bias=nbfg[:, l, m:m + 1], scale=-1.0),
                         reads=[rPS[b], rnbfg], writes=[rL1])
                P.op("act", lambda e: e.activation(L1[:], L1[:], AF.Ln, bias=1.0), reads=[rL1], writes=[rL1])
                P.op("dve", lambda e: e.tensor_tensor_scan(C_[:], MASK[:], L1[:], 0.0, ALU.mult, ALU.add), reads=[rMASK, rL1], writes=[rC])
                P.op("act", lambda e, m=m: e.activation(EBL[:, m, 0:32], C_[:, 63:T:64], AF.Exp, scale=-1.0 / 16), reads=[rC], writes=[rEBL])
                P.op("act", lambda e, m=m: e.activation(EBL[:, m, 32:36], C_[:, T + 7:NT:8], AF.Exp, scale=-1.0 / 16), reads=[rC], writes=[rEBL])
                for ci, (c0, n) in enumerate(CH):
                    P.op("act", lambda e, c0=c0, n=n: e.activation(EBc[:, 0:n], C_[:, c0:c0 + n], AF.Exp, scale=-1.0 / 16), reads=[rC], writes=[rEBc])
                    P.op("act", lambda e, c0=c0, n=n: e.activation(ENBc[:, 0:n], C_[:, c0:c0 + n], AF.Exp, scale=1.0 / 16), reads=[rC], writes=[rENBc])
                    bq = bank()
                    for k in range(8):
                        P.op("pe", lambda e, b=bq, k=k, m=m, c0=c0, n=n: e.matmul(PS[:, b, 0:n], Wqk[:, k, m * 128:(m + 1) * 128], H[:, k, c0:c0 + n], start=(k == 0), stop=(k == 7)),
                             reads=[rWqk, rH[ci]], writes=[rPS[bq]])
                    P.op("dve", lambda e, b=bq, m=m, c0=c0, n=n: e.scalar_tensor_tensor(QT[:, m, c0:c0 + n], PS[:, b, 0:n], 0.125, EBc[:, 0:n], ALU.mult, ALU.mult),
                         reads=[rPS[bq], rEBc], writes=[rQT])
                    bk = bank()
                    for k in range(8):
                        P.op("pe", lambda e, b=bk, k=k, m=m, c0=c0, n=n: e.matmul(PS[:, b, 0:n], Wqk[:, k, 256 + m * 128:256 + (m + 1) * 128], H[:, k, c0:c0 + n], start=(k == 0), stop=(k == 7)),
                             reads=[rWqk, rH[ci]], writes=[rPS[bk]])
                    P.op("dve", lambda e, b=bk, n=n: e.tensor_tensor(KTf[:, 0:n], PS[:, b, 0:n], ENBc[:, 0:n], ALU.mult), reads=[rPS[bk], rENBc], writes=[rKTf])
                    P.op("act", lambda e, m=m, c0=c0, n=n: e.copy(KT[:, m, c0:c0 + n], KTf[:, 0:n]), reads=[rKTf], writes=[rKT])
                    cs = 64 if c0 < T else 8
                    cb = c0 // 64 if c0 < T else 32
                    P.op("dve", lambda e, m=m, c0=c0, n=n, cs=cs, cb=cb: e.tensor_tensor(
                        KLT[:, m, c0:c0 + n].rearrange("p (c s) -> p c s", s=cs), KTf[:, 0:n].rearrange("p (c s) -> p c s", s=cs),
                        EBL[:, m, cb:cb + n // cs].unsqueeze(2).broadcast_to([128, n // cs, cs]), ALU.mult), reads=[rKTf, rEBL], writes=[rKLT])
            P.fence()
            import os
            KGLA = int(os.environ.get("KGLA", "9"))
            if KGLA < 2:
                return
            CV.off = ph_b
            KLtok = CV.take([128, 17, 256], BF16); rKLtok = Reg("KLtok")
            VT = CV.take([128, 17, 512], BF16); rVT = Reg("VT")
            Sf = [CV.take([128, 128], F32) for _ in range(2)]; rSf = [Reg("Sf0"), Reg("Sf1")]
            ATT = [CV.take([128, 512], BF16) for _ in range(2)]; rATT = [Reg("att0"), Reg("att1")]
            OF = CV.take([128, 512], F32); rOF = Reg("OF")
            ATS = CV.take([32, 32], BF16); rATS = Reg("ATS")
            KLm = CV.take([32, 4, 256], BF16); rKLm = Reg("KLm")
            SQg = CV.take([128, 512], BF16); rSQg = Reg("SQg")
            RSg = CV.take([128, 512], F32); rRSg = Reg("RSg")
            T2 = RSg; rT2 = rRSg
            SR = CV.take([128, 512], F32); rSR = Reg("SR")
            SFs = CV.take([128, 4, 128], F32); rSFs = Reg("SFs")
            Wv, rWv = wload([(wsrc(w_in, l, 8, 3584, 512), 0, 512)], 8, 512)
            for tb in range(17):
                ntok = 128 if tb < 16 else TS
                b = bank()
                for k in range(8):
                    P.op("pe", lambda e, b=b, k=k, tb=tb, ntok=ntok: e.matmul(PS[0:ntok, b, :], H[:, k, tb * 128:tb * 128 + ntok], Wv[:, k, :], start=(k == 0), stop=(k == 7)),
                         reads=[rWv, rH[tb // 4 if tb < 16 else 4]], writes=[rPS[b]])
                P.op("act", lambda e, b=b, tb=tb, ntok=ntok: e.copy(VT[0:ntok, tb, :], PS[0:ntok, b, :]), reads=[rPS[b]], writes=[rVT])
            for tb in range(17):
                ntok = 128 if tb < 16 else TS
                b = bank()
                PSB = PS[:, b, :].bitcast(BF16)
                for m in range(2):
                    P.op("pe", lambda e, PSB=PSB, m=m, tb=tb, ntok=ntok: e.transpose(PSB[0:ntok, m * 128:(m + 1) * 128], KLT[:, m, tb * 128:tb * 128 + ntok], identb[:, :]),
                         reads=[rKLT, ridentb], writes=[rPS[b]])
                P.op("act", lambda e, PSB=PSB, tb=tb, ntok=ntok: e.copy(KLtok[0:ntok, tb, :], PSB[0:ntok, 0:256]), reads=[rPS[b]], writes=[rKLtok])
            for bq in range(4):
                P.op("dve", lambda e, bq=bq: e.tensor_scalar(KLm[:, bq, :], KLtok[0:32, 16, :], selm[:, bq:bq + 1], None, ALU.mult), reads=[rKLtok, rselm], writes=[rKLm])
            if KGLA < 3:
                return
            Wr, rWr = wload([(wsrc(w_in, l, 8, 4096, 512), 0, 512)], 8, 512)
            for m in range(2):
                cur = 0
                P.op("dve", lambda e: e.memset(Sf[0][:], 0.0), writes=[rSf[0]])
                for c in range(32):
                    tb, par = c // 2, c % 2
                    tp = slice(par * 64, par * 64 + 64)
                    b = bank()
                    for hh in range(2):
                        hd = 2 * m + hh
                        P.op("pe", lambda e, b=b, hh=hh, hd=hd, tb=tb, tp=tp, par=par: e.matmul(
                            PS[hh * 64:(hh + 1) * 64, b, 0:128], KLtok[tp, tb, hd * 64:(hd + 1) * 64], VT[tp, tb, hd * 128:(hd + 1) * 128],
                            start=True, stop=True, tile_position=(par * 64, hh * 64)),
                            reads=[rKLtok, rVT], writes=[rPS[b]])
                    nxt = 1 - cur
                    P.op("dve", lambda e, b=b, m=m, c=c, cur=cur, nxt=nxt: e.scalar_tensor_tensor(
                        Sf[nxt][:], Sf[cur][:], EBL[:, m, c:c + 1], PS[:, b, 0:128], ALU.mult, ALU.add),
                        reads=[rSf[cur], rEBL, rPS[b]], writes=[rSf[nxt]])
                    if c < 31:
                        P.op("act", lambda e, c=c, nxt=nxt: e.copy(SP[:, c, :], Sf[nxt][:]), reads=[rSf[nxt]], writes=[rSP])
                    cur = nxt
                P.dma("sp", lambda e, m=m, cur=cur: e.dma_start(out=gp[l, 2 * m:2 * m + 2].rearrange("h k v -> (h k) v"), in_=Sf[cur][:]), reads=[rSf[cur]])
                b = bank()
                for bq in range(4):
                    for hh in range(2):
                        hd = 2 * m + hh
                        P.op("pe", lambda e, b=b, bq=bq, hh=hh, hd=hd: e.matmul(
                            PS[hh * 64:(hh + 1) * 64, b, bq * 128:(bq + 1) * 128], KLm[:, bq, hd * 64:(hd + 1) * 64], VT[0:32, 16, hd * 128:(hd + 1) * 128],
                            start=True, stop=True, tile_position=(0, hh * 64)),
                            reads=[rKLm, rVT], writes=[rPS[b]])
                for bq in range(4):
                    P.op("dve", lambda e, b=b, bq=bq, m=m: e.scalar_tensor_tensor(
                        SFs[:, bq, :], S0f[:, m, bq, :], EBL[:, m, 32 + bq:33 + bq], PS[:, b, bq * 128:(bq + 1) * 128], ALU.mult, ALU.add),
                        reads=[rS0f, rEBL, rPS[b]], writes=[rSFs])
                for bq in range(4):
                    P.dma("sp", lambda e, m=m, bq=bq: e.dma_start(out=gs[l, bq, 2 * m:2 * m + 2].rearrange("h k v -> (h k) v"), in_=SFs[:, bq, :]), reads=[rSFs])
                for hh in range(2 if KGLA >= 4 else 0):
                    hd = 2 * m + hh
                    hs = slice(hh * 64, hh * 64 + 64)
                    for ci, (c0, n) in enumerate(CH):
                        bo = bank()
                        if c0 < T:
                            ba = bank()
                            ai = ci % 2
                            for jb in range(4):
                                bc = slice(c0 + jb * 128, c0 + jb * 128 + 128)
                                P.op("pe", lambda e, b=ba, jb=jb, bc=bc, m=m, hs=hs, hh=hh: e.matmul(
                                    PS[:, b, jb * 128:(jb + 1) * 128], KT[hs, m, bc], QT[hs, m, bc], start=True, stop=True, tile_position=(hh * 64, 0)),
                                    reads=[rKT, rQT], writes=[rPS[ba]])
                            P.op("dve", lambda e, b=ba, ai=ai: e.tensor_tensor(
                                ATT[ai][:].rearrange("p (j c) -> p j c", c=128), PS[:, b, :].rearrange("p (j c) -> p j c", c=128),
                                cmask[:].unsqueeze(1).broadcast_to([128, 4, 128]), ALU.mult), reads=[rPS[ba], rcmask], writes=[rATT[ai]])
                            for jb in range(4):
                                tb = c0 // 128 + jb
                                P.op("pe", lambda e, b=bo, jb=jb, tb=tb, hd=hd, ai=ai: e.matmul(
                                    PS[:, b, jb * 128:(jb + 1) * 128], VT[:, tb, hd * 128:(hd + 1) * 128], ATT[ai][:, jb * 128:(jb + 1) * 128], start=True, stop=False),
                                    reads=[rVT, rATT[ai]], writes=[rPS[bo]])
                                for par in range(2):
                                    c = 2 * tb + par
                                    if c == 0:
                                        continue
                                    cc = slice(c * 64, c * 64 + 64)
                                    P.op("pe", lambda e, b=bo, jb=jb, par=par, hs=hs, c=c, m=m, cc=cc, hh=hh: e.matmul(
                                        PS[:, b, jb * 128 + par * 64:jb * 128 + par * 64 + 64], SP[hs, c - 1, :], QT[hs, m, cc], start=False, stop=True, tile_position=(hh * 64, 0)),
                                        reads=[rSP, rQT], writes=[rPS[bo]])
                            P.op("act", lambda e, b=bo, n=n: e.copy(OF[:, 0:n], PS[:, b, 0:n]), reads=[rPS[bo]], writes=[rOF])
                        else:
                            ba = bank()
                            P.op("pe", lambda e, b=ba, m=m, hs=hs, hh=hh: e.matmul(PS[0:32, b, 0:32], KT[hs, m, T:NT], QT[hs, m, T:NT], start=True, stop=True,
                                                                                 tile_position=(hh * 64, 0)),
                                 reads=[rKT, rQT], writes=[rPS[ba]])
                            P.op("dve", lambda e, b=ba: e.tensor_tensor(ATS[:], PS[0:32, b, 0:32], smask[:], ALU.mult), reads=[rPS[ba], rsmask], writes=[rATS])
                            P.op("pe", lambda e, b=bo, hd=hd: e.matmul(PS[:, b, 0:32], VT[0:32, 16, hd * 128:(hd + 1) * 128], ATS[:], start=True, stop=True),
                                 reads=[rVT, rATS], writes=[rPS[bo]])
                            b2 = bank()
                            for bq in range(4):
                                P.op("pe", lambda e, b=b2, bq=bq, hs=hs, m=m, hh=hh: e.matmul(
                                    PS[:, b, bq * 8:(bq + 1) * 8], S0b[hs, m, bq, :], QT[hs, m, T + bq * 8:T + bq * 8 + 8], start=True, stop=True,
                                    tile_position=(hh * 64, 0)),
                                    reads=[rS0b, rQT], writes=[rPS[b2]])
                            P.op("act", lambda e, b=b2: e.copy(OF[:, 0:32], PS[:, b, 0:32]), reads=[rPS[b2]], writes=[rOF])
                            P.op("dve", lambda e, b=bo: e.tensor_tensor(OF[:, 0:32], OF[:, 0:32], PS[:, b, 0:32], ALU.add), reads=[rPS[bo], rOF], writes=[rOF])
                        if int(os.environ.get("KG4", "9")) < 3:
                            continue
                        P.op("act", lambda e, n=n: e.activation(SQg[:, 0:n], OF[:, 0:n], AF.Square), reads=[rOF], writes=[rSQg])
                        bs_ = bank()
                        P.op("pe", lambda e, b=bs_, n=n: e.matmul(PS[:, b, 0:n], ones[:], SQg[:, 0:n], start=True, stop=True), reads=[rSQg, rones], writes=[rPS[bs_]])
                        P.op("dve", lambda e, b=bs_, n=n: e.tensor_scalar(RSg[:, 0:n], PS[:, b, 0:n], 1.0 / 128, EPS, ALU.mult, ALU.add), reads=[rPS[bs_]], writes=[rRSg])
                        P.op("act", lambda e, n=n: e.activation(RSg[:, 0:n], RSg[:, 0:n], AF.Sqrt), reads=[rRSg], writes=[rRSg])
                        P.op("dve", lambda e, n=n: e.reciprocal(RSg[:, 0:n], RSg[:, 0:n]), reads=[rRSg], writes=[rRSg])
                        P.op("dve", lambda e, n=n: e.scalar_tensor_tensor(T2[:, 0:n], OF[:, 0:n], gnorm[:, l:l + 1], RSg[:, 0:n], ALU.mult, ALU.mult),
                             reads=[rOF, rgnorm, rRSg], writes=[rRSg])
                        br = bank()
                        for k in range(8):
                            P.op("pe", lambda e, b=br, k=k, hd=hd, c0=c0, n=n: e.matmul(PS[:, b, 0:n], Wr[:, k, hd * 128:(hd + 1) * 128], H[:, k, c0:c0 + n], start=(k == 0), stop=(k == 7)),
                                 reads=[rWr, rH[ci]], writes=[rPS[br]])
                        P.op("act", lambda e, b=br, n=n: e.activation(SR[:, 0:n], PS[:, b, 0:n], AF.Silu), reads=[rPS[br]], writes=[rSR])
                        P.op("dve", lambda e, hd=hd, c0=c0, n=n: e.tensor_tensor(OC[:, hd, c0:c0 + n], T2[:, 0:n], SR[:, 0:n], ALU.mult), reads=[rT2, rSR], writes=[rOC])
            P.fence()
            CV.off = mark
            if KGLA < 5:
                return
            branch_out(l, OC, rOC, w_o_c, 6672)
            P.fence()

        PI = math.pi

        def lam_tables(eng_d, A_re, A_im, LDT, shp, tk, rT):
            dt = tk(shp); ar = tk(shp); ai = tk(shp); mag = tk(shp); sn = tk(shp); cs = tk(shp); lr = tk(shp); li = tk(shp)
            P.op("act", lambda e: e.activation(dt, LDT, AF.Exp), reads=[rT], writes=[rT])
            P.op("dve", lambda e: e.tensor_tensor(ar, A_re, dt, ALU.mult), reads=[rT], writes=[rT])
            P.op("dve", lambda e: e.tensor_tensor(ai, A_im, dt, ALU.mult), reads=[rT], writes=[rT])
            P.op("act", lambda e: e.activation(mag, ar, AF.Exp), reads=[rT], writes=[rT])
            tmp = tk(shp)
            for dstt, shift in ((sn, 0.0), (cs, 0.5 * PI)):
                P.op("dve", lambda e, dstt=dstt, shift=shift: e.tensor_scalar(dstt, ai, shift, None, ALU.add), reads=[rT], writes=[rT])
                for jth in range(5):
                    thr = shift - (2 * jth + 1) * PI
                    P.op("dve", lambda e, thr=thr: e.tensor_scalar(tmp, ai, thr, 1e9, ALU.add, ALU.mult), reads=[rT], writes=[rT])
                    P.op("dve", lambda e: e.tensor_scalar(tmp, tmp, 0.0, 1.0, ALU.max, ALU.min), reads=[rT], writes=[rT])
                    P.op("dve", lambda e, dstt=dstt: e.scalar_tensor_tensor(dstt, tmp, -2 * PI, dstt, ALU.mult, ALU.add), reads=[rT], writes=[rT])
                P.op("act", lambda e, dstt=dstt: e.activation(dstt, dstt, AF.Sin), reads=[rT], writes=[rT])
            P.op("dve", lambda e: e.tensor_tensor(lr, mag, cs, ALU.mult), reads=[rT], writes=[rT])
            P.op("dve", lambda e: e.tensor_tensor(li, mag, sn, ALU.mult), reads=[rT], writes=[rT])
            return lr, li, cs, sn, mag

        def s5(l):
            CV.reset()
            YB = CV.take([128, 4, NT], BF16); rYB = Reg("YB")
            mark = CV.off
            UB = CV.take([128, 4, NT], BF16); rUB = Reg("UB")
            YF = CV.take([128, NT], BF16); rYF = Reg("YF")
            Wt0 = CV.take([128, 4, 2, 128], BF16); rWt0 = Reg("Wt0")
            Ct0 = CV.take([128, 16, 2, 32], BF16); rCt0 = Reg("Ct0")
            LPr = CV.take([128, 16, 11], F32); LPi = CV.take([128, 16, 11], F32); nLPi = CV.take([128, 16, 11], F32); rLP = Reg("LP")
            LAM = CV.take([128, 3, 16], F32)
            CW = CV.take([128, 2], F32); rCW = Reg("CW")
            BSr = CV.take([128, 16, 32], F32); BSi = CV.take([128, 16, 32], F32); rBS = Reg("BS")
            XSr = CV.take([128, 16, 32], BF16); XSi = CV.take([128, 16, 32], BF16); rXS = Reg("XS")
            X0r = CV.take([128, 16, 4], F32); X0i = CV.take([128, 16, 4], F32); rX0 = Reg("X0")
            EFr = CV.take([128, 16], F32); EFi = CV.take([128, 16], F32); rEF = Reg("EF")
            TS_ = [CV.take([128, 16, 4], F32) for _ in range(2)]; rTS = Reg("TSs")
            XCb = [CV.take([128, 512], BF16) for _ in range(2)]; rXCb = [Reg("xcb0"), Reg("xcb1")]
            G1 = CV.take([128, 512], F32); rG1 = Reg("G1")
            G2 = CV.take([128, 512], F32); rG2 = Reg("G2")
            xoff = CV.off
            XR = [CV.take([128, T], F32) for _ in range(2)]; XI = [CV.take([128, T], F32) for _ in range(2)]
            rXR = [Reg("xr0"), Reg("xr1")]; rXI = [Reg("xi0"), Reg("xi1")]
            RHOT = CV.take([128, 1024], F32); rRHOT = Reg("RHOT")
            CV.off = xoff
            rT = Reg("s5tmp")
            pA = CV.take([128, 5, 4, 64], F32)
            pB = CV.take([128, 3, 16], F32)
            cB = CV.take([128, 2, 16, 16], F32)
            P.dma("sp", lambda e: e.dma_start(out=pA, in_=aA_d[:, l]), writes=[rT])
            P.dma("sp", lambda e: e.dma_start(out=pB, in_=aB_d[:, l]), writes=[rT])
            P.dma("sp", lambda e: e.dma_start(out=cB, in_=cB_d[:, l]), writes=[rT])
            P.dma("sp", lambda e: e.dma_start(out=X0r[:], in_=x0r_d[:, l]), writes=[rX0])
            P.dma("sp", lambda e: e.dma_start(out=X0i[:], in_=x0i_d[:, l]), writes=[rX0])
            tkB = lambda shp: CV.take(shp, F32)
            lrB, liB, csB, snB, magB = lam_tables("dve", pB[:, 0, :], pB[:, 1, :], pB[:, 2, :], [128, 16], tkB, rT)
            P.op("act", lambda e: e.copy(LPr[:, :, 0], csB), reads=[rT], writes=[rLP])
            P.op("act", lambda e: e.copy(LPi[:, :, 0], snB), reads=[rT], writes=[rLP])
            P.op("act", lambda e: e.copy(LAM[:, 0, :], lrB), reads=[rT], writes=[rLP])
            P.op("act", lambda e: e.copy(LAM[:, 1, :], liB), reads=[rT], writes=[rLP])
            P.op("act", lambda e: e.copy(LAM[:, 2, :], magB), reads=[rT], writes=[rLP])
            t1 = CV.take([128, 16], F32); t2 = CV.take([128, 16], F32)

            def unit_norm(k):
                P.op("dve", lambda e, k=k: e.tensor_tensor(t1, LPr[:, :, k], LPr[:, :, k], ALU.mult), reads=[rLP], writes=[rT])
                P.op("dve", lambda e, k=k: e.tensor_tensor(t2, LPi[:, :, k], LPi[:, :, k], ALU.mult), reads=[rLP], writes=[rT])
                P.op("dve", lambda e: e.tensor_tensor(t1, t1, t2, ALU.add), reads=[rT], writes=[rT])
                P.op("act", lambda e: e.activation(t1, t1, AF.Sqrt), reads=[rT], writes=[rT])
                P.op("dve", lambda e: e.reciprocal(t1, t1), reads=[rT], writes=[rT])
                P.op("dve", lambda e, k=k: e.tensor_tensor(LPr[:, :, k], LPr[:, :, k], t1, ALU.mult), reads=[rT, rLP], writes=[rLP])
                P.op("dve", lambda e, k=k: e.tensor_tensor(LPi[:, :, k], LPi[:, :, k], t1, ALU.mult), reads=[rT, rLP], writes=[rLP])
            unit_norm(0)
            for k in range(10):
                P.op("dve", lambda e, k=k: e.tensor_tensor(t1, LPr[:, :, k], LPr[:, :, k], ALU.mult), reads=[rLP], writes=[rT])
                P.op("dve", lambda e, k=k: e.tensor_tensor(t2, LPi[:, :, k], LPi[:, :, k], ALU.mult), reads=[rLP], writes=[rT])
                P.op("dve", lambda e, k=k: e.tensor_tensor(LPr[:, :, k + 1], t1, t2, ALU.subtract), reads=[rT], writes=[rLP])
                P.op("dve", lambda e, k=k: e.tensor_tensor(t1, LPr[:, :, k], LPi[:, :, k], ALU.mult), reads=[rLP], writes=[rT])
                P.op("dve", lambda e, k=k: e.tensor_scalar(LPi[:, :, k + 1], t1, 2.0, None, ALU.mult), reads=[rT], writes=[rLP])
                unit_norm(k + 1)
            P.op("dve", lambda e: e.tensor_scalar(nLPi[:], LPi[:], -1.0, None, ALU.mult), reads=[rLP], writes=[rLP])
            for c_ in range(2):
                sgn = 1.0 if c_ == 0 else -1.0
                for g2 in range(2):
                    P.op("dve", lambda e, c_=c_, g2=g2, sgn=sgn: e.tensor_scalar(
                        Ct0[:, :, c_, g2 * 16:(g2 + 1) * 16], cB[:, c_, :, :], mk[:, 2 + g2:3 + g2], sgn, ALU.mult, ALU.mult),
                        reads=[rT, rmk], writes=[rCt0])
            shpA = [128, 4, 64]
            tkA = lambda shp: CV.take(shp, F32)
            lrA, liA, _c, _s, _m = lam_tables("dve", pA[:, 0], pA[:, 1], pA[:, 2], shpA, tkA, rT)
            den = tkA(shpA); nr = tkA(shpA); cor = tkA(shpA); coi = tkA(shpA); u1 = tkA(shpA); u2 = tkA(shpA)
            are, aim, bre, bim = pA[:, 0], pA[:, 1], pA[:, 3], pA[:, 4]
            def dv(fn):
                P.op("dve", fn, reads=[rT], writes=[rT])
            dv(lambda e: e.tensor_tensor(den, are, are, ALU.mult))
            dv(lambda e: e.tensor_tensor(u1, aim, aim, ALU.mult))
            dv(lambda e: e.tensor_tensor(den, den, u1, ALU.add))
            dv(lambda e: e.reciprocal(den, den))
            dv(lambda e: e.tensor_scalar(nr, lrA, -1.0, None, ALU.add))
            dv(lambda e: e.tensor_tensor(u1, nr, are, ALU.mult))
            dv(lambda e: e.tensor_tensor(u2, liA, aim, ALU.mult))
            dv(lambda e: e.tensor_tensor(u1, u1, u2, ALU.add))
            dv(lambda e: e.tensor_tensor(cor, u1, den, ALU.mult))
            dv(lambda e: e.tensor_tensor(u1, liA, are, ALU.mult))
            dv(lambda e: e.tensor_tensor(u2, nr, aim, ALU.mult))
            dv(lambda e: e.tensor_tensor(u1, u1, u2, ALU.subtract))
            dv(lambda e: e.tensor_tensor(coi, u1, den, ALU.mult))
            bbr = tkA(shpA); bbi = tkA(shpA)
            dv(lambda e: e.tensor_tensor(u1, cor, bre, ALU.mult))
            dv(lambda e: e.tensor_tensor(u2, coi, bim, ALU.mult))
            dv(lambda e: e.tensor_tensor(bbr, u1, u2, ALU.subtract))
            dv(lambda e: e.tensor_tensor(u1, cor, bim, ALU.mult))
            dv(lambda e: e.tensor_tensor(u2, coi, bre, ALU.mult))
            dv(lambda e: e.tensor_tensor(bbi, u1, u2, ALU.add))
            for c_, bb in enumerate((bbr, bbi)):
                for g2 in range(2):
                    P.op("dve", lambda e, c_=c_, g2=g2, bb=bb: e.tensor_scalar(
                        Wt0[:, :, c_, g2 * 64:(g2 + 1) * 64], bb, mk[:, g2:g2 + 1], None, ALU.mult), reads=[rT, rmk], writes=[rWt0])
            P.fence()
            import os
            KS5 = int(os.environ.get("KS5", "9"))
            if KS5 < 2:
                return
            Wu, rWu = wload([(wsrc(w_in, l, 8, 2560, 512), 0, 512)], 8, 512)
            for j in range(4):
                for ci, (c0, n) in enumerate(CH):
                    b = bank()
                    for k in range(8):
                        P.op("pe", lambda e, b=b, k=k, j=j, c0=c0, n=n: e.matmul(PS[:, b, 0:n], Wu[:, k, j * 128:(j + 1) * 128], H[:, k, c0:c0 + n], start=(k == 0), stop=(k == 7)),
                             reads=[rWu, rH[ci]], writes=[rPS[b]])
                    P.op("act", lambda e, b=b, j=j, c0=c0, n=n: e.copy(UB[:, j, c0:c0 + n], PS[:, b, 0:n]), reads=[rPS[b]], writes=[rUB])
            for q in range(16):
                j, qq = q // 4, q % 4
                rows = slice(32 * qq, 32 * qq + 32)
                b = bank()
                for c_ in range(2):
                    P.op("pe", lambda e, b=b, c_=c_, j=j, rows=rows, qq=qq: e.matmul(
                        PS[:, b, c_ * 32:(c_ + 1) * 32], Wt0[rows, j, c_, :], UB[rows, j, T:NT], start=True, stop=True, tile_position=(32 * qq, 0)),
                        reads=[rWt0, rUB], writes=[rPS[b]])
                P.op("act", lambda e, b=b, q=q: e.copy(BSr[:, q, :], PS[:, b, 0:32]), reads=[rPS[b]], writes=[rBS])
                P.op("act", lambda e, b=b, q=q: e.copy(BSi[:, q, :], PS[:, b, 32:64]), reads=[rPS[b]], writes=[rBS])
            BSr4 = BSr[:].rearrange("p q (b t) -> p q b t", t=8)
            BSi4 = BSi[:].rearrange("p q (b t) -> p q b t", t=8)
            lamr = LAM[:, 0, :].unsqueeze(2).broadcast_to([128, 16, 4]); lami = LAM[:, 1, :].unsqueeze(2).broadcast_to([128, 16, 4])
            for t in range(8):
                pr = X0r[:] if t == 0 else BSr4[:, :, :, t - 1]
                pi_ = X0i[:] if t == 0 else BSi4[:, :, :, t - 1]
                rd = [rBS, rX0, rLP]
                P.op("dve", lambda e, pr=pr: e.tensor_tensor(TS_[0][:], pr, lamr, ALU.mult), reads=rd, writes=[rTS])
                P.op("dve", lambda e, t=t: e.tensor_tensor(BSr4[:, :, :, t], BSr4[:, :, :, t], TS_[0][:], ALU.add), reads=[rTS, rBS], writes=[rBS])
                P.op("dve", lambda e, pi_=pi_: e.tensor_tensor(TS_[1][:], pi_, lami, ALU.mult), reads=rd, writes=[rTS])
                P.op("dve", lambda e, t=t: e.tensor_tensor(BSr4[:, :, :, t], BSr4[:, :, :, t], TS_[1][:], ALU.subtract), reads=[rTS, rBS], writes=[rBS])
                P.op("dve", lambda e, pi_=pi_: e.tensor_tensor(TS_[0][:], pi_, lamr, ALU.mult), reads=rd, writes=[rTS])
                P.op("dve", lambda e, t=t: e.tensor_tensor(BSi4[:, :, :, t], BSi4[:, :, :, t], TS_[0][:], ALU.add), reads=[rTS, rBS], writes=[rBS])
                P.op("dve", lambda e, pr=pr: e.tensor_tensor(TS_[1][:], pr, lami, ALU.mult), reads=rd, writes=[rTS])
                P.op("dve", lambda e, t=t: e.tensor_tensor(BSi4[:, :, :, t], BSi4[:, :, :, t], TS_[1][:], ALU.add), reads=[rTS, rBS], writes=[rBS])
            P.op("act", lambda e: e.copy(XSr[:], BSr[:]), reads=[rBS], writes=[rXS])
            P.op("act", lambda e: e.copy(XSi[:], BSi[:]), reads=[rBS], writes=[rXS])
            P.op("act", lambda e: e.copy(TS_[0][:], BSr4[:, :, :, 7]), reads=[rBS], writes=[rTS])
            P.dma("sp", lambda e: e.dma_start(out=esr_d[l], in_=TS_[0][:].rearrange("p q b -> p (q b)")), reads=[rTS])
            P.op("act", lambda e: e.copy(TS_[1][:], BSi4[:, :, :, 7]), reads=[rBS], writes=[rTS])
            P.dma("sp", lambda e: e.dma_start(out=esi_d[l], in_=TS_[1][:].rearrange("p q b -> p (q b)")), reads=[rTS])
            if KS5 < 3:
                return
            for q in range(16):
                j, qq = q // 4, q % 4
                rows = slice(32 * qq, 32 * qq + 32)
                for ci in range(4):
                    c0 = ci * 512
                    for c_, dst, rdst in ((0, XR[0], rXR[0]), (1, XI[0], rXI[0])):
                        b = bank()
                        P.op("pe", lambda e, b=b, c_=c_, j=j, rows=rows, qq=qq, c0=c0: e.matmul(
                            PS[:, b, :], Wt0[rows, j, c_, :], UB[rows, j, c0:c0 + 512], start=True, stop=True, tile_position=(32 * qq, 0)),
                            reads=[rWt0, rUB], writes=[rPS[b]])
                        P.op("act", lambda e, b=b, dst=dst, c0=c0: e.copy(dst[:, c0:c0 + 512], PS[:, b, :]), reads=[rPS[b]], writes=[rdst])
                HH = 1024
                Ar, Ai, rAr, rAi = XR[0], XI[0], rXR[0], rXI[0]
                Rr, Ri = XR[1][:, 0:HH], XR[1][:, HH:T]
                T1_, T2_ = XI[1][:, 0:HH], XI[1][:, HH:T]
                rR, rTT = rXR[1], rXI[1]
                P.op("dve", lambda e: e.memset(Rr[:, 0:1], 1.0), writes=[rR])
                P.op("dve", lambda e: e.memset(Ri[:, 0:1], 0.0), writes=[rR])
                for k in range(10):
                    m_ = 1 << k
                    P.op("dve", lambda e, m_=m_, q=q, k=k: e.tensor_scalar(Rr[:, m_:2 * m_], Rr[:, 0:m_], LPr[:, q, k:k + 1], None, ALU.mult), reads=[rR, rLP], writes=[rR])
                    P.op("dve", lambda e, m_=m_, q=q, k=k: e.scalar_tensor_tensor(Rr[:, m_:2 * m_], Ri[:, 0:m_], nLPi[:, q, k:k + 1], Rr[:, m_:2 * m_], ALU.mult, ALU.add), reads=[rR, rLP], writes=[rR])
                    P.op("dve", lambda e, m_=m_, q=q, k=k: e.tensor_scalar(Ri[:, m_:2 * m_], Ri[:, 0:m_], LPr[:, q, k:k + 1], None, ALU.mult), reads=[rR, rLP], writes=[rR])
                    P.op("dve", lambda e, m_=m_, q=q, k=k: e.scalar_tensor_tensor(Ri[:, m_:2 * m_], Rr[:, 0:m_], LPi[:, q, k:k + 1], Ri[:, m_:2 * m_], ALU.mult, ALU.add), reads=[rR, rLP], writes=[rR])
                P.op("dve", lambda e, q=q: e.tensor_copy(RHOT[:], LAM[:, 2, q:q + 1].broadcast_to([128, HH])), reads=[rLP], writes=[rRHOT])
                rho = RHOT[:]
                for hf in range(2):
                    cs_ = slice(hf * HH, (hf + 1) * HH)
                    br, bi = Ar[:, cs_], Ai[:, cs_]
                    TT = ALU
                    def tt(out, a, b_, op, rd, wr):
                        P.op("dve", lambda e, out=out, a=a, b_=b_, op=op: e.tensor_tensor(out, a, b_, op), reads=rd, writes=wr)
                    tt(T1_, Rr, br, ALU.mult, [rR, rAr], [rTT])
                    tt(T2_, Ri, bi, ALU.mult, [rR, rAi], [rTT])
                    tt(T1_, T1_, T2_, ALU.add, [rTT], [rTT])
                    tt(T2_, Ri, br, ALU.mult, [rR, rAr], [rTT])
                    tt(br, Rr, bi, ALU.mult, [rR, rAi, rAr], [rAr])
                    tt(br, br, T2_, ALU.subtract, [rAr, rTT], [rAr])
                    if hf == 0:
                        ini_r, ini_i, rdi = 0.0, 0.0, []
                    else:
                        xe_r, xe_i = Ar[:, HH - 1:HH], Ai[:, HH - 1:HH]
                        P.op("dve", lambda e, q=q: e.tensor_scalar(CW[:, 0:1], xe_r, LPr[:, q, 0:1], None, ALU.mult), reads=[rAr, rLP], writes=[rCW])
                        P.op("dve", lambda e, q=q: e.scalar_tensor_tensor(CW[:, 0:1], xe_i, nLPi[:, q, 0:1], CW[:, 0:1], ALU.mult, ALU.add), reads=[rAi, rLP, rCW], writes=[rCW])
                        P.op("dve", lambda e, q=q: e.tensor_scalar(CW[:, 1:2], xe_i, LPr[:, q, 0:1], None, ALU.mult), reads=[rAi, rLP], writes=[rCW])
                        P.op("dve", lambda e, q=q: e.scalar_tensor_tensor(CW[:, 1:2], xe_r, LPi[:, q, 0:1], CW[:, 1:2], ALU.mult, ALU.add), reads=[rAr, rLP, rCW], writes=[rCW])
                        ini_r, ini_i, rdi = CW[:, 0:1], CW[:, 1:2], [rCW]
                    P.op("dve", lambda e, bi=bi, ini_r=ini_r: e.tensor_tensor_scan(bi, rho, T1_, ini_r, ALU.mult, ALU.add), reads=[rTT, rRHOT] + rdi, writes=[rAi])
                    P.op("dve", lambda e, br=br, ini_i=ini_i: e.tensor_tensor_scan(T2_, rho, br, ini_i, ALU.mult, ALU.add), reads=[rAr, rRHOT] + rdi, writes=[rTT])
                    tt(T1_, Rr, bi, ALU.mult, [rR, rAi], [rTT])
                    tt(br, Ri, T2_, ALU.mult, [rR, rTT], [rAr])
                    tt(br, T1_, br, ALU.subtract, [rTT, rAr], [rAr])
                    tt(T1_, Rr, T2_, ALU.mult, [rR, rTT], [rTT])
                    tt(bi, bi, Ri, ALU.mult, [rAi, rR], [rAi])
                    tt(bi, bi, T1_, ALU.add, [rAi, rTT], [rAi])
                cur = 0
                fr, fi, rfr, rfi = XR[cur], XI[cur], rXR[cur], rXI[cur]
                P.op("act", lambda e, fr=fr, q=q: e.copy(EFr[:, q:q + 1], fr[:, T - 1:T]), reads=[rfr], writes=[rEF])
                P.op("act", lambda e, fi=fi, q=q: e.copy(EFi[:, q:q + 1], fi[:, T - 1:T]), reads=[rfi], writes=[rEF])
                if KS5 < 4:
                    continue
                for ci, (c0, n) in enumerate(CH):
                    if c0 < T:
                        P.op("act", lambda e, fr=fr, c0=c0: e.copy(XCb[0][:], fr[:, c0:c0 + 512]), reads=[rfr], writes=[rXCb[0]])
                        P.op("act", lambda e, fi=fi, c0=c0: e.copy(XCb[1][:], fi[:, c0:c0 + 512]), reads=[rfi], writes=[rXCb[1]])
                        r0, r1 = XCb[0][:, 0:n], XCb[1][:, 0:n]
                        rr = [rXCb[0], rXCb[1]]
                    else:
                        r0, r1 = XSr[:, q, :], XSi[:, q, :]
                        rr = [rXS]
                    b = bank()
                    P.op("pe", lambda e, b=b, q=q, qq=qq, n=n, r0=r0, rows=rows: e.matmul(PS[rows, b, 0:n], Ct0[:, q, 0, :], r0, start=True, stop=False, tile_position=(0, 32 * qq)),
                         reads=[rCt0] + rr, writes=[rPS[b]])
                    P.op("pe", lambda e, b=b, q=q, qq=qq, n=n, r1=r1, rows=rows: e.matmul(PS[rows, b, 0:n], Ct0[:, q, 1, :], r1, start=False, stop=True, tile_position=(0, 32 * qq)),
                         reads=[rCt0] + rr, writes=[rPS[b]])
                    P.op("act", lambda e, b=b, rows=rows, c0=c0, n=n: e.copy(YF[rows, c0:c0 + n], PS[rows, b, 0:n]), reads=[rPS[b]], writes=[rYF])
                if qq == 3 and int(os.environ.get("KS5G", "1")):
                    for ci, (c0, n) in enumerate(CH):
                        P.op("act", lambda e, j=j, c0=c0, n=n: e.activation(G1[:, 0:n], UB[:, j, c0:c0 + n], AF.Copy, scale=dA[:, l, j:j + 1]),
                             reads=[rUB, rdA], writes=[rG1])
                        P.op("dve", lambda e, c0=c0, n=n: e.tensor_tensor(G1[:, 0:n], G1[:, 0:n], YF[:, c0:c0 + n], ALU.add),
                             reads=[rG1, rYF], writes=[rG1])
                        P.op("act", lambda e, n=n: e.activation(G2[:, 0:n], G1[:, 0:n], AF.Square), reads=[rG1], writes=[rG2])
                        P.op("dve", lambda e, n=n: e.tensor_scalar(G2[:, 0:n], G2[:, 0:n], 0.044715, 1.0, ALU.mult, ALU.add), reads=[rG2], writes=[rG2])
                        P.op("dve", lambda e, n=n: e.tensor_tensor(G2[:, 0:n], G2[:, 0:n], G1[:, 0:n], ALU.mult), reads=[rG2, rG1], writes=[rG2])
                        P.op("act", lambda e, n=n: e.activation(G2[:, 0:n], G2[:, 0:n], AF.Sigmoid, scale=1.5957691216057308), reads=[rG2], writes=[rG2])
                        P.op("dve", lambda e, j=j, c0=c0, n=n: e.tensor_tensor(YB[:, j, c0:c0 + n], G2[:, 0:n], G1[:, 0:n], ALU.mult), reads=[rG2, rG1], writes=[rYB])
            P.dma("sp", lambda e: e.dma_start(out=efr_d[l], in_=EFr[:]), reads=[rEF])
            P.dma("sp", lambda e: e.dma_start(out=efi_d[l], in_=EFi[:]), reads=[rEF])
            P.fence()
            CV.off = mark
            if KS5 < 5:
                return
            branch_out(l, YB, rYB, w_glu, 5648, glu=True)
            P.fence()

        def cache_copy(l, part):
            for bq in range(4):
                for (src_d, dst_d) in ((ck, kws), (cv, vws)):
                    r0 = 8 + part * 510
                    P.dma("sp", lambda e, src_d=src_d, dst_d=dst_d, bq=bq, r0=r0, l=l: e.dma_start(
                        out=dst_d[l, bq, r0 - 8:r0 - 8 + 510, :], in_=src_d[l, bq, r0:r0 + 510, :]), nofence=True)

        for l in range(n_layers):
            CV.reset()
            SQ = CV.take([128, 8, 512], BF16); rSQ = Reg("sq")
            RS = CV.take([128, 512], F32); rRS = Reg("rs")
            rmsnorm_to_H(2 * l, SQ, rSQ, RS, rRS)
            P.fence()
            import os
            cache_copy(l, 0)
            if os.environ.get("KSKIP_ATT") != "1":
                attention(l)
            cache_copy(l, 1)
            if os.environ.get("KSKIP_S5") != "1":
                s5(l)
            cache_copy(l, 2)
            if os.environ.get("KSKIP_GLA") != "1":
                gla(l)
            cache_copy(l, 3)
            if os.environ.get("KSKIP_MLP") != "1":
                mlp(l)

        import os
        if os.environ.get("KSTAGE") == "1":
            P.dma("sp", lambda e: e.dma_start(out=yp.rearrange("(p a) d -> p (a d)", p=128).rearrange("p (k t) -> p k t", t=2048), in_=X[:, :, 0:2048]),
                  reads=[rX[k][c] for k in range(8) for c in range(5)])
            P.emit()
            return nc
        CV.reset()
        SQ = CV.take([128, 8, 512], BF16); rSQ = Reg("sq")
        RS = CV.take([128, 512], F32); rRS = Reg("rs")
        HF = CV.take([128, 8, 512], F32); rHF = Reg("HF")
        YT = [CV.take([128, 1024], F32) for _ in range(2)]; rYT = [Reg("yt0"), Reg("yt1")]
        for ci, (c0, n) in enumerate(CH):
            P.op("act", lambda e, c0=c0, n=n: e.activation(SQ[:, :, 0:n], X[:, :, c0:c0 + n], AF.Square),
                 reads=[rX[k][ci] for k in range(8)], writes=[rSQ])
            b = bank()
            for k in range(8):
                P.op("pe", lambda e, b=b, k=k, n=n: e.matmul(PS[:, b, 0:n], ones[:], SQ[:, k, 0:n], start=(k == 0), stop=(k == 7)),
                     reads=[rSQ, rones], writes=[rPS[b]])
            P.op("dve", lambda e, b=b, n=n: e.tensor_scalar(RS[:, 0:n], PS[:, b, 0:n], 1.0 / D, EPS, ALU.mult, ALU.add), reads=[rPS[b]], writes=[rRS])
            P.op("act", lambda e, n=n: e.activation(RS[:, 0:n], RS[:, 0:n], AF.Sqrt), reads=[rRS], writes=[rRS])
            P.op("dve", lambda e, n=n: e.reciprocal(RS[:, 0:n], RS[:, 0:n]), reads=[rRS], writes=[rRS])
            for k in range(8):
                P.op("dve", lambda e, k=k, c0=c0, n=n: e.scalar_tensor_tensor(
                    HF[:, k, 0:n], X[:, k, c0:c0 + n], gains[:, 8, k:k + 1], RS[:, 0:n], ALU.mult, ALU.mult),
                    reads=[rX[k][ci], rRS, rgains], writes=[rHF])
            for tq in range((n + 127) // 128):
                ntok = min(128, n - tq * 128)
                i = tq % 2
                for half in range(2):
                    b = bank()
                    for kk in range(4):
                        k = half * 4 + kk
                        P.op("pe", lambda e, b=b, kk=kk, k=k, tq=tq, ntok=ntok: e.transpose(
                            PS[0:ntok, b, kk * 128:(kk + 1) * 128], HF[:, k, tq * 128:tq * 128 + ntok], ident[:, :]),
                            reads=[rHF, rident], writes=[rPS[b]])
                    P.op("act", lambda e, b=b, half=half, i=i, ntok=ntok: e.copy(YT[i][0:ntok, half * 512:(half + 1) * 512], PS[0:ntok, b, :]),
                         reads=[rPS[b]], writes=[rYT[i]])
                t0 = c0 + tq * 128
                dst = yp[t0:t0 + ntok, :] if c0 < T else ys
                P.dma("sp", lambda e, i=i, ntok=ntok, dst=dst: e.dma_start(out=dst, in_=YT[i][0:ntok, :]), reads=[rYT[i]])
        P.emit()
    return nc


_NC = {}


def kernel(**inp):
    f = lambda a: np.ascontiguousarray(np.asarray(a, dtype=np.float32))
    if "nc" not in _NC:
        _NC["nc"] = build()
    nc = _NC["nc"]
    gains = np.zeros((128, 9, 8), np.float32)
    for l in range(L_):
        gains[:, 2 * l, :] = f(inp["norm_mix"])[l].reshape(8, 128).T
        gains[:, 2 * l + 1, :] = f(inp["norm_mlp"])[l].reshape(8, 128).T
    gains[:, 8, :] = f(inp["norm_final"]).reshape(8, 128).T
    rel_bias = f(inp["rel_bias"])
    biasT = np.full((24, 128, 256), -1e30, np.float32)
    kk = np.arange(128)[:, None]
    qq = np.arange(256)[None, :]
    step = qq - kk
    valid = (step >= 0) & (step <= 128)
    for g in range(3):
        bk = _t5_bucket(np.arange(129) * PAT[g][1])
        for h in range(8):
            vals = rel_bias[bk, g * 8 + h]
            biasT[g * 8 + h] = np.where(valid, vals[np.clip(step, 0, 128)], np.float32(-1e30))
    a_re, a_im, ldt = f(inp["s5_a_re"]), f(inp["s5_a_im"]), f(inp["s5_log_dt"])
    b_re, b_im, c_re, c_im = f(inp["s5_b_re"]), f(inp["s5_b_im"]), f(inp["s5_c_re"]), f(inp["s5_c_im"])
    aB = np.zeros((128, L_, 3, 16), np.float32)
    cB = np.zeros((128, L_, 2, 16, 16), np.float32)
    aA = np.zeros((128, L_, 5, 4, 64), np.float32)
    for l in range(L_):
        for g2 in range(2):
            grp = np.arange(16) * 2 + g2
            aB[g2 * 64:(g2 + 1) * 64, l, 0, :] = a_re[l][grp].T
            aB[g2 * 64:(g2 + 1) * 64, l, 1, :] = a_im[l][grp].T
            aB[g2 * 64:(g2 + 1) * 64, l, 2, :] = ldt[l][grp][None, :]
            cB[g2 * 64:(g2 + 1) * 64, l, 0] = c_re[l][grp].transpose(2, 0, 1)
            cB[g2 * 64:(g2 + 1) * 64, l, 1] = c_im[l][grp].transpose(2, 0, 1)
        ar4 = a_re[l].reshape(4, 8, 64); ai4 = a_im[l].reshape(4, 8, 64); ld4 = ldt[l].reshape(4, 8)
        br4 = b_re[l].reshape(4, 8, 64, 16); bi4 = b_im[l].reshape(4, 8, 64, 16)
        aA[:, l, 0] = np.repeat(ar4.transpose(1, 0, 2), 16, axis=0)
        aA[:, l, 1] = np.repeat(ai4.transpose(1, 0, 2), 16, axis=0)
        aA[:, l, 2] = np.repeat(np.broadcast_to(ld4.T[:, :, None], (8, 4, 64)), 16, axis=0)
        aA[:, l, 3] = br4.transpose(1, 3, 0, 2).reshape(128, 4, 64)
        aA[:, l, 4] = bi4.transpose(1, 3, 0, 2).reshape(128, 4, 64)
    dA = np.ascontiguousarray(f(inp["s5_d"]).reshape(L_, 4, 128).transpose(2, 0, 1))
    pidx = np.arange(128)
    mk = np.stack([((pidx // 16) % 2 == 0), ((pidx // 16) % 2 == 1), pidx < 64, pidx >= 64], axis=1).astype(np.float32)
    sre = f(inp["state_ssm_re"]); sim = f(inp["state_ssm_im"])
    nbfg = np.ascontiguousarray(-f(inp["b_fg"]).reshape(L_, 2, 128).transpose(2, 0, 1))
    gnorm = np.ascontiguousarray(f(inp["gla_norm"]).T)
    pp = np.arange(128)[:, None]
    cq = np.arange(128)[None, :]
    cmask = ((cq >= pp) & (cq // 64 == pp // 64)).astype(np.float32)
    si = np.arange(32)
    smask = ((si[:, None] // 8 == si[None, :] // 8) & (si[None, :] >= si[:, None])).astype(np.float32)
    selm = (si[:, None] // 8 == np.arange(4)[None, :]).astype(np.float32)
    sg = f(inp["state_gla"])
    sbias = np.full((8, 128, 10, 3, 8), -1e30, np.float32)
    sbiasn = np.full((8, 32, 4, 3, 8), -1e30, np.float32)
    pp_ = np.arange(128)
    rows = np.zeros((10, 128), np.int64)
    for blk in range(4):
        rows[blk] = 1536 + 128 * blk + pp_
    for u in range(6):
        rows[4 + u] = 256 * u + 16 * (pp_ // 8) + (pp_ % 8)
    tt = np.arange(8)
    for g in range(3):
        dd = PAT[g][1]
        dist = 2048 + tt[None, None, :] - rows[:, :, None]
        ok = (dist % dd == 0) & (dist // dd <= 128) & (dist >= 0)
        bkt = _t5_bucket(np.where(ok, dist, 0))
        for h in range(8):
            vals = rel_bias[bkt, g * 8 + h]
            sbias[h, :, :, g, :] = np.where(ok, vals, np.float32(-1e30)).transpose(1, 0, 2)
        kb_, kt_ = np.arange(32) // 8, np.arange(32) % 8
        for bq in range(4):
            dn = tt[None, :] - kt_[:, None]
            okn = (kb_[:, None] == bq) & (dn >= 0) & (dn % dd == 0)
            bktn = _t5_bucket(np.where(okn, dn, 0))
            for h in range(8):
                sbiasn[h, :, bq, g, :] = np.where(okn, rel_bias[bktn, g * 8 + h], np.float32(-1e30))
    sbias = np.ascontiguousarray(sbias.reshape(8, 128, 240))
    sbiasn = np.ascontiguousarray(sbiasn.reshape(8, 32, 96))
    shared = dict(sbias=sbias, sbiasn=sbiasn, aB=aB, cB=cB, aA=aA, dA=dA, mk=mk, wfg2=f(inp["w_fg2"]), nbfg=nbfg, gnorm=gnorm, cmask=cmask, smask=smask, selm=selm, w_in=f(inp["w_in"]), w_o_a=f(inp["w_o_a"]), w_glu=f(inp["w_glu"]), w_o_c=f(inp["w_o_c"]),
                  w_out=f(inp["w_out"]), w_up=f(inp["w_up"]), w_down=f(inp["w_down"]),
                  ident=np.eye(128, dtype=np.float32), gains=gains, biasT=biasT)
    xp = f(inp["x_prompt"]); xs = f(inp["x_sample"])
    ck = f(inp["cache_k_win"]); cv = f(inp["cache_v_win"])
    in_maps = []
    for c in range(8):
        m = dict(shared)
        m["xp"] = xp[c]
        m["xs"] = xs[4 * c:4 * c + 4].reshape(TS, D)
        m["ck"] = ck[:, 4 * c:4 * c + 4].reshape(L_, 4, 2048, 512)
        m["cv"] = cv[:, 4 * c:4 * c + 4].reshape(L_, 4, 2048, 512)
        m["sg"] = np.ascontiguousarray(sg[:, 4 * c:4 * c + 4])
        m["x0r"] = np.ascontiguousarray(sre[:, 4 * c:4 * c + 4].reshape(L_, 4, 16, 2, 64).transpose(3, 4, 0, 2, 1).reshape(128, L_, 16, 4))
        m["x0i"] = np.ascontiguousarray(sim[:, 4 * c:4 * c + 4].reshape(L_, 4, 16, 2, 64).transpose(3, 4, 0, 2, 1).reshape(128, L_, 16, 4))
        in_maps.append(m)
    res = run_bass_kernel_spmd(nc, in_maps, core_ids=list(range(8)))
    R = res.results
    y_prompt = np.stack([R[c]["yp"] for c in range(8)])
    y_sample = np.concatenate([R[c]["ys"].reshape(4, 8, D) for c in range(8)])
    kwp = np.stack([R[c]["kwp"].reshape(L_, T, 8, 64) for c in range(8)], axis=1)
    vwp = np.stack([R[c]["vwp"].reshape(L_, T, 8, 64) for c in range(8)], axis=1)
    kws = np.concatenate([R[c]["kws"].reshape(L_, 4, 2048, 8, 64) for c in range(8)], axis=1)
    vws = np.concatenate([R[c]["vws"].reshape(L_, 4, 2048, 8, 64) for c in range(8)], axis=1)
    z = lambda *s: np.zeros(s, np.float32)
    gla_p = np.stack([R[c]["gp"] for c in range(8)], axis=1)
    gla_s = np.concatenate([R[c]["gs"] for c in range(8)], axis=1)
    def unp(a):
        return np.ascontiguousarray(a.reshape(L_, 2, 64, 16).transpose(0, 3, 1, 2).reshape(L_, 32, 64))

    def uns(a):
        return np.ascontiguousarray(a.reshape(L_, 2, 64, 16, 4).transpose(0, 4, 3, 1, 2).reshape(L_, 4, 32, 64))
    srp_ = np.stack([unp(R[c]["efr"]) for c in range(8)], axis=1)
    sip_ = np.stack([unp(R[c]["efi"]) for c in range(8)], axis=1)
    srs_ = np.concatenate([uns(R[c]["esr"]) for c in range(8)], axis=1)
    sis_ = np.concatenate([uns(R[c]["esi"]) for c in range(8)], axis=1)
    return (y_prompt, y_sample, kwp, vwp, kws, vws, srp_, sip_, srs_, sis_, gla_p, gla_s)
```
